# Optimizing a Trainium2 kernel written in Bass

```python
import math
import jax, jax.numpy as jnp
from jax import lax
import numpy as np

D_MODEL = 1024
BATCH = 8
SEQ = 2048
DEPTH = 1
DEC_BATCH = 128
DEC_SEQ = 4
PAST_LEN = 16384
PAGE_SIZE = 128

S5_GROUP = 16
S5_WIDTH = D_MODEL // 2
S5_GROUPS = S5_WIDTH // S5_GROUP
S5_STATE = 64
S5_DT_MIN = 0.001
S5_DT_MAX = 0.1
HG_HEADS = 8
HG_DK = D_MODEL // HG_HEADS
HG_DV = D_MODEL // HG_HEADS
HG_KEY_WIDTH = HG_HEADS * HG_DK
HG_VAL_WIDTH = HG_HEADS * HG_DV
HG_CHUNK = 32
D_FF = 4 * D_MODEL
IN_COLS = S5_WIDTH + 2 * HG_KEY_WIDTH + 2 * HG_VAL_WIDTH + 2 * D_MODEL
SPLIT_POINTS = [S5_WIDTH,
                S5_WIDTH + HG_KEY_WIDTH,
                S5_WIDTH + 2 * HG_KEY_WIDTH,
                S5_WIDTH + 2 * HG_KEY_WIDTH + HG_VAL_WIDTH,
                S5_WIDTH + 2 * HG_KEY_WIDTH + 2 * HG_VAL_WIDTH,
                S5_WIDTH + 2 * HG_KEY_WIDTH + 2 * HG_VAL_WIDTH + D_MODEL]
NORM_EPS = 1e-6

kernel_name = "s5_hgrn2_gated_parallel_decoder_step"

f32 = jnp.float32


def rms_norm(x, g):
    xf = x.astype(f32)
    y = xf * lax.rsqrt(jnp.mean(xf * xf, axis=-1, keepdims=True) + NORM_EPS)
    return (y * g.astype(f32)).astype(x.dtype)


def s5_discretise(a_re, a_im, log_dt, b_re, b_im):
    dt = jnp.exp(log_dt)[:, None]
    mag = jnp.exp(dt * a_re)
    ab_re = mag * jnp.cos(dt * a_im)
    ab_im = mag * jnp.sin(dt * a_im)
    den = a_re * a_re + a_im * a_im
    nr = ab_re - 1.0
    ni = ab_im
    coef_re = (nr * a_re + ni * a_im) / den
    coef_im = (ni * a_re - nr * a_im) / den
    bb_re = coef_re[..., None] * b_re - coef_im[..., None] * b_im
    bb_im = coef_re[..., None] * b_im + coef_im[..., None] * b_re
    return ab_re, ab_im, bb_re, bb_im


def s5_combine(e1, e2):
    a1r, a1i, b1r, b1i = e1
    a2r, a2i, b2r, b2i = e2
    ar = a1r * a2r - a1i * a2i
    ai = a1r * a2i + a1i * a2r
    br = a2r * b1r - a2i * b1i + b2r
    bi = a2r * b1i + a2i * b1r + b2i
    return (ar, ai, br, bi)


def s5_mixer(u, h0_re, h0_im, a_re, a_im, log_dt, b_re, b_im, c_re, c_im, d, w_glu, b_glu):
    bsz, L, _ = u.shape
    uf = u.astype(f32)
    ug = uf.reshape(bsz, L, S5_GROUPS, S5_GROUP)
    ab_re, ab_im, bb_re, bb_im = s5_discretise(a_re.astype(f32), a_im.astype(f32), log_dt.astype(f32),
                                               b_re.astype(f32), b_im.astype(f32))
    bu_re = jnp.einsum('gnc,blgc->blgn', bb_re, ug)
    bu_im = jnp.einsum('gnc,blgc->blgn', bb_im, ug)
    shp = (bsz, L, S5_GROUPS, S5_STATE)
    elems = (jnp.broadcast_to(ab_re, shp), jnp.broadcast_to(ab_im, shp), bu_re, bu_im)
    pr, pi, sr, si = lax.associative_scan(s5_combine, elems, axis=1)
    h0r = h0_re.astype(f32)[:, None]
    h0i = h0_im.astype(f32)[:, None]
    h_re = pr * h0r - pi * h0i + sr
    h_im = pr * h0i + pi * h0r + si
    y = (jnp.einsum('gcn,blgn->blgc', c_re.astype(f32), h_re)
         - jnp.einsum('gcn,blgn->blgc', c_im.astype(f32), h_im))
    y = y.reshape(bsz, L, S5_WIDTH) + d.astype(f32) * uf
    y = jax.nn.gelu(y)
    y = y * jax.nn.sigmoid(y @ w_glu.astype(f32) + b_glu.astype(f32))
    return y.astype(u.dtype), h_re[:, -1], h_im[:, -1]


def hgrn2_mixer(q_raw, f_raw, i_raw, g_raw, s0, lb, norm_g):
    bsz, L, _ = q_raw.shape
    C = math.gcd(L, HG_CHUNK)
    nc = L // C
    q = jax.nn.silu(q_raw.astype(f32))
    f = lb.astype(f32) + (1.0 - lb.astype(f32)) * jax.nn.sigmoid(f_raw.astype(f32))
    log_f = jnp.log(f)
    k = 1.0 - f
    v = i_raw.astype(f32)

    def to_chunks(t, width):
        return jnp.moveaxis(t.reshape(bsz, nc, C, HG_HEADS, width), 1, 0)

    xs = (to_chunks(q, HG_DK), to_chunks(k, HG_DK), to_chunks(v, HG_DV), to_chunks(log_f, HG_DK))
    causal = jnp.tril(jnp.ones((C, C), dtype=bool))

    def chunk_step(S, chunk):
        qc, kc, vc, lfc = chunk
        b = jnp.cumsum(lfc, axis=1)
        o_inter = jnp.einsum('bthk,bhkv->bthv', qc * jnp.exp(b), S)
        diff = b[:, :, None] - b[:, None, :]
        decay = jnp.exp(jnp.where(causal[None, :, :, None, None], diff, -jnp.inf))
        scores = jnp.einsum('bthk,bshk,btshk->btsh', qc, kc, decay)
        o_intra = jnp.einsum('btsh,bshv->bthv', scores, vc)
        b_last = b[:, -1]
        S_new = (jnp.exp(b_last)[..., None] * S
                 + jnp.einsum('bshk,bshv->bhkv', kc * jnp.exp(b_last[:, None] - b), vc))
        return S_new, o_inter + o_intra

    S_fin, o = lax.scan(chunk_step, s0.astype(f32), xs)
    o = jnp.moveaxis(o, 0, 1).reshape(bsz, L, HG_HEADS, HG_DV)
    o = o * lax.rsqrt(jnp.mean(o * o, axis=-1, keepdims=True) + NORM_EPS) * norm_g.astype(f32)
    o = o * jax.nn.silu(g_raw.astype(f32).reshape(bsz, L, HG_HEADS, HG_DV))
    return o.reshape(bsz, L, HG_VAL_WIDTH).astype(q_raw.dtype), S_fin


def trunk_layer(x, s5_h0_re, s5_h0_im, hg_s0, lb,
                norm_mix_pre, norm_mix_post, norm_mlp_pre, norm_mlp_post, w_in, b_in,
                s5_a_re, s5_a_im, s5_log_dt, s5_b_re, s5_b_im, s5_c_re, s5_c_im, s5_d, s5_w_glu, s5_b_glu,
                hg_norm, w_br_s5, w_br_hg, w_out, w_up, w_down):
    h = rms_norm(x, norm_mix_pre)
    proj = h @ w_in + b_in
    u, q, f, i, g, gate_s5, gate_hg = jnp.split(proj, SPLIT_POINTS, axis=-1)
    y_s5, s5_re, s5_im = s5_mixer(u, s5_h0_re, s5_h0_im, s5_a_re, s5_a_im, s5_log_dt,
                                  s5_b_re, s5_b_im, s5_c_re, s5_c_im, s5_d, s5_w_glu, s5_b_glu)
    y_hg, hg_s = hgrn2_mixer(q, f, i, g, hg_s0, lb, hg_norm)
    merged = jax.nn.sigmoid(gate_s5) * (y_s5 @ w_br_s5) + jax.nn.sigmoid(gate_hg) * (y_hg @ w_br_hg)
    x = x + rms_norm(merged @ w_out, norm_mix_post)
    h2 = rms_norm(x, norm_mlp_pre)
    m = jnp.square(jax.nn.relu(h2 @ w_up)) @ w_down
    x = x + rms_norm(m, norm_mlp_post)
    return x, s5_re, s5_im, hg_s


def setup_inputs(seed: int = 0) -> dict:
    key = jax.random.key(seed)
    ks = jax.random.split(key, 32)
    nrm = lambda k, shp, s: jax.random.normal(k, shp, f32) * s
    n_idx = jnp.arange(S5_STATE, dtype=f32)
    a_re = -0.5 + nrm(ks[10], (DEPTH, S5_GROUPS, S5_STATE), 0.01)
    a_im = math.pi * n_idx[None, None, :] + nrm(ks[11], (DEPTH, S5_GROUPS, S5_STATE), 0.01)
    log_dt = jax.random.uniform(ks[12], (DEPTH, S5_GROUPS), f32,
                                math.log(S5_DT_MIN), math.log(S5_DT_MAX))
    return {
        "x_prompt": nrm(ks[0], (BATCH, SEQ, D_MODEL), 1.0),
        "x_sample": nrm(ks[1], (DEC_BATCH, DEC_SEQ, D_MODEL), 1.0),
        "state_s5_re": nrm(ks[2], (DEPTH, DEC_BATCH, S5_GROUPS, S5_STATE), 0.3),
        "state_s5_im": nrm(ks[3], (DEPTH, DEC_BATCH, S5_GROUPS, S5_STATE), 0.3),
        "state_hg": nrm(ks[4], (DEPTH, DEC_BATCH, HG_HEADS, HG_DK, HG_DV), 0.3),
        "norm_mix_pre": 1.0 + nrm(ks[5], (DEPTH, D_MODEL), 0.02),
        "norm_mix_post": 1.0 + nrm(ks[6], (DEPTH, D_MODEL), 0.02),
        "norm_mlp_pre": 1.0 + nrm(ks[7], (DEPTH, D_MODEL), 0.02),
        "norm_mlp_post": 1.0 + nrm(ks[8], (DEPTH, D_MODEL), 0.02),
        "w_in": nrm(ks[9], (DEPTH, D_MODEL, IN_COLS), D_MODEL ** -0.5),
        "b_in": nrm(ks[13], (DEPTH, IN_COLS), 0.01),
        "s5_a_re": a_re,
        "s5_a_im": a_im,
        "s5_log_dt": log_dt,
        "s5_b_re": nrm(ks[14], (DEPTH, S5_GROUPS, S5_STATE, S5_GROUP), S5_GROUP ** -0.5),
        "s5_b_im": nrm(ks[15], (DEPTH, S5_GROUPS, S5_STATE, S5_GROUP), S5_GROUP ** -0.5),
        "s5_c_re": nrm(ks[16], (DEPTH, S5_GROUPS, S5_GROUP, S5_STATE), S5_STATE ** -0.5),
        "s5_c_im": nrm(ks[17], (DEPTH, S5_GROUPS, S5_GROUP, S5_STATE), S5_STATE ** -0.5),
        "s5_d": nrm(ks[18], (DEPTH, S5_WIDTH), 0.5),
        "s5_w_glu": nrm(ks[19], (DEPTH, S5_WIDTH, S5_WIDTH), S5_WIDTH ** -0.5),
        "s5_b_glu": nrm(ks[20], (DEPTH, S5_WIDTH), 0.01),
        "hg_lb_logits": nrm(ks[21], (DEPTH + 1, HG_KEY_WIDTH), 0.1),
        "hg_norm": 1.0 + nrm(ks[22], (DEPTH, HG_DV), 0.02),
        "w_br_s5": nrm(ks[23], (DEPTH, S5_WIDTH, D_MODEL), S5_WIDTH ** -0.5),
        "w_br_hg": nrm(ks[24], (DEPTH, HG_VAL_WIDTH, D_MODEL), HG_VAL_WIDTH ** -0.5),
        "w_out": nrm(ks[25], (DEPTH, D_MODEL, D_MODEL), D_MODEL ** -0.5),
        "w_up": nrm(ks[26], (DEPTH, D_MODEL, D_FF), D_MODEL ** -0.5),
        "w_down": nrm(ks[27], (DEPTH, D_FF, D_MODEL), D_FF ** -0.5),
    }


def reference(x_prompt, x_sample, state_s5_re, state_s5_im, state_hg,
              norm_mix_pre, norm_mix_post, norm_mlp_pre, norm_mlp_post, w_in, b_in,
              s5_a_re, s5_a_im, s5_log_dt, s5_b_re, s5_b_im, s5_c_re, s5_c_im, s5_d, s5_w_glu, s5_b_glu,
              hg_lb_logits, hg_norm, w_br_s5, w_br_hg, w_out, w_up, w_down):
    lb_all = jnp.cumsum(jax.nn.softmax(hg_lb_logits.astype(f32), axis=0), axis=0)
    xp, xs = x_prompt, x_sample
    p_re, p_im, p_hg, s_re, s_im, s_hg = [], [], [], [], [], []
    for l in range(DEPTH):
        w = (norm_mix_pre[l], norm_mix_post[l], norm_mlp_pre[l], norm_mlp_post[l], w_in[l], b_in[l],
             s5_a_re[l], s5_a_im[l], s5_log_dt[l], s5_b_re[l], s5_b_im[l], s5_c_re[l], s5_c_im[l],
             s5_d[l], s5_w_glu[l], s5_b_glu[l], hg_norm[l], w_br_s5[l], w_br_hg[l], w_out[l],
             w_up[l], w_down[l])
        bsz = xp.shape[0]
        xp, r, im, hg = trunk_layer(xp,
                                    jnp.zeros((bsz, S5_GROUPS, S5_STATE), f32),
                                    jnp.zeros((bsz, S5_GROUPS, S5_STATE), f32),
                                    jnp.zeros((bsz, HG_HEADS, HG_DK, HG_DV), f32),
                                    lb_all[l], *w)
        p_re.append(r); p_im.append(im); p_hg.append(hg)
        xs, r, im, hg = trunk_layer(xs, state_s5_re[l], state_s5_im[l], state_hg[l], lb_all[l], *w)
        s_re.append(r); s_im.append(im); s_hg.append(hg)
    return (xp, xs, jnp.stack(p_re), jnp.stack(p_im), jnp.stack(p_hg),
            jnp.stack(s_re), jnp.stack(s_im), jnp.stack(s_hg))
```

```python
import contextlib
import math
import numpy as np
import concourse.bass as bass
import concourse.mybir as mybir
from concourse.bass_utils import run_bass_kernel_spmd

F32 = mybir.dt.float32
BF16 = mybir.dt.bfloat16
I32 = mybir.dt.int32
AF = mybir.ActivationFunctionType
ALU = mybir.AluOpType

ENGS = ("pe", "act", "dve", "pool", "sp")
TP = 256
NCORES = 8
SEQ = 2048
NSS = 16
TS = 64
TWO_PI = 2.0 * math.pi


class Res:
    __slots__ = ("name", "writers", "readers")

    def __init__(self, name=""):
        self.name = name
        self.writers = {}
        self.readers = {}


class Op:
    __slots__ = ("id", "eng", "emit", "deps", "is_dma", "sig", "signals", "waits")

    def __init__(self, id, eng, emit, deps, is_dma):
        self.id = id
        self.eng = eng
        self.emit = emit
        self.deps = deps
        self.is_dma = is_dma
        self.sig = None
        self.signals = is_dma
        self.waits = []


class Sched:
    NDMA = 56

    def __init__(self):
        self.ops = []

    def op(self, eng, emit, reads=(), writes=(), is_dma=False):
        oid = len(self.ops)
        deps = set()
        for r in reads:
            deps.update(r.writers.values())
        for w in writes:
            deps.update(w.writers.values())
            deps.update(w.readers.values())
        o = Op(oid, eng, emit, deps, is_dma)
        self.ops.append(o)
        k = ("dma", oid) if is_dma else eng
        for r in reads:
            r.readers[k] = oid
        for w in writes:
            w.writers = {k: oid}
            w.readers = {}
        return oid

    def dma(self, q, out, in_, reads=(), writes=(), **kw):
        return self.op(q, lambda e: e.dma_start(out=out, in_=in_, **kw), reads, writes, is_dma=True)

    def finalize(self, final_eng="sp"):
        ops = self.ops
        for o in ops:
            nd = set()
            for d in o.deps:
                do = ops[d]
                if (not do.is_dma) and (not o.is_dma) and do.eng == o.eng and o.eng == "pe":
                    continue
                nd.add(d)
            o.deps = nd
        slot_last = [None] * self.NDMA
        slot_cnt = [0] * self.NDMA
        pools = {"pool": list(range(0, 40)), "sp": list(range(40, self.NDMA)), "act": list(range(40, self.NDMA))}
        rrq = {"pool": 0, "sp": 0, "act": 0}
        for o in ops:
            if o.is_dma:
                pl = pools[o.eng]
                s = pl[rrq[o.eng] % len(pl)]
                rrq[o.eng] += 1
                if slot_last[s] is not None:
                    o.deps.add(slot_last[s])
                slot_last[s] = o.id
                slot_cnt[s] += 16
                o.sig = ("dma", s, slot_cnt[s])
        for o in ops:
            for d in o.deps:
                ops[d].signals = True
        cnt = {e: 0 for e in ENGS}
        for o in ops:
            if (not o.is_dma) and o.signals:
                cnt[o.eng] += 1
                o.sig = ("eng", o.eng, cnt[o.eng])
        seen = {e: {} for e in ENGS}
        for o in ops:
            need = {}
            for d in o.deps:
                kind, key, val = ops[d].sig
                k = (kind, key)
                if val > need.get(k, 0):
                    need[k] = val
            sn = seen[o.eng]
            for k, val in need.items():
                if sn.get(k, 0) >= val:
                    continue
                sn[k] = val
                o.waits.append((k, val))
        self.final_waits = []
        sn = seen[final_eng]
        last = {}
        for o in ops:
            if o.sig is not None:
                kind, key, val = o.sig
                last[(kind, key)] = max(last.get((kind, key), 0), val)
        for k, val in last.items():
            if sn.get(k, 0) < val:
                self.final_waits.append((k, val))
        self.final_eng = final_eng

    def emit_engine(self, e, eng, sems_eng, sems_dma):
        def semof(k):
            kind, key = k
            return sems_dma[key] if kind == "dma" else sems_eng[key]
        for o in self.ops:
            if o.eng != e:
                continue
            for (k, val) in o.waits:
                eng.wait_ge(semof(k), val)
            ins = o.emit(eng)
            if o.signals:
                kind, key, val = o.sig
                if kind == "dma":
                    ins.then_inc(sems_dma[key], 16)
                else:
                    ins.then_inc(sems_eng[key], 1)
        if e == self.final_eng:
            for (k, val) in self.final_waits:
                eng.wait_ge(semof(k), val)


DBG = None


def build_nc():
    nc = bass.Bass("TRN2", target_bir_lowering=False)
    S = Sched()
    es = contextlib.ExitStack()

    def din(name, shape):
        return nc.dram_tensor(name, list(shape), F32, kind="ExternalInput").ap()

    def dout(name, shape):
        return nc.dram_tensor(name, list(shape), F32, kind="ExternalOutput").ap()

    xp = din("xp", [SEQ, 1024])
    xs = din("xs", [TS, 1024])
    st_re = din("st_re", [NSS, 2048])
    st_im = din("st_im", [NSS, 2048])
    st_hg = din("st_hg", [NSS, 8, 128, 128])
    g_pre_d = din("g_pre", [1024])
    g_post_d = din("g_post", [1024])
    g2_pre_d = din("g2_pre", [1024])
    g2_post_d = din("g2_post", [1024])
    w_in = din("w_in", [1024, 6656])
    b_in = din("b_in", [6656])
    a_re_d = din("a_re", [2048])
    a_im_d = din("a_im", [2048])
    ldt_d = din("ldt", [32])
    b_re_d = din("b_re", [2048, 16])
    b_im_d = din("b_im", [2048, 16])
    c_re_d = din("c_re", [32, 16, 64])
    c_im_d = din("c_im", [32, 16, 64])
    s5d_d = din("s5d", [512])
    w_glu = din("w_glu", [512, 512])
    b_glu_d = din("b_glu", [512])
    lbl_d = din("lbl", [2, 1024])
    hgn_d = din("hgn", [128])
    w_bs = din("w_bs", [512, 1024])
    w_bh = din("w_bh", [1024, 1024])
    w_out = din("w_out", [1024, 1024])
    w_up = din("w_up", [1024, 4096])
    w_down = din("w_down", [4096, 1024])

    yp = dout("yp", [SEQ, 1024])
    ys = dout("ys", [TS, 1024])
    o_pre = dout("o_pre", [2048])
    o_pim = dout("o_pim", [2048])
    o_phg = dout("o_phg", [8, 128, 128])
    o_sre = dout("o_sre", [NSS, 2048])
    o_sim = dout("o_sim", [NSS, 2048])
    o_shg = dout("o_shg", [NSS, 8, 128, 128])

    def dbg(name, ap, res):
        if DBG is None or name not in DBG:
            return
        d = nc.dram_tensor("dbg_" + name, list(ap.shape), ap.dtype, kind="ExternalOutput").ap()
        S.dma("sp", d, ap, reads=res)

    def sb(name, shape, dt=F32):
        return es.enter_context(nc.sbuf_tensor("sb_" + name, list(shape), dt))

    class TL:
        def __init__(self, name, shape, dt=F32):
            self.t = sb(name, shape, dt)
            self.r = Res(name)

    def pe(f, r=(), w=()):
        S.op("pe", f, r, w)

    def mm(out, lhsT, rhs, start, stop, r=(), w=()):
        S.op("pe", lambda e: e.matmul(out, lhsT=lhsT, rhs=rhs, start=start, stop=stop), r, w)

    def tr(out, in_, identity, r=(), w=()):
        S.op("pe", lambda e: e.transpose(out=out, in_=in_, identity=identity), r, w)

    def scan(out, d0, d1, r=(), w=()):
        S.op("dve", lambda e: e.tensor_tensor_scan(out=out, data0=d0, data1=d1, initial=0.0, op0=ALU.mult, op1=ALU.add), r, w)

    def scan2(out, d0, d1, init, r=(), w=()):
        S.op("dve", lambda e: e.tensor_tensor_scan(out=out, data0=d0, data1=d1, initial=init, op0=ALU.mult, op1=ALU.add), r, w)

    def act(out, in_, func, r=(), w=(), bias=None, scale=1.0, accum=None):
        def f(e):
            kw = dict(out=out, in_=in_, func=func, scale=scale)
            if bias is not None:
                kw["bias"] = bias
            if accum is not None:
                kw["accum_out"] = accum
            return e.activation(**kw)
        S.op("act", f, r, w)

    def tt(eng, out, in0, in1, op, r=(), w=()):
        S.op(eng, lambda e: e.tensor_tensor(out=out, in0=in0, in1=in1, op=op), r, w)

    def ts(eng, out, in0, s1, s2, op0, op1=None, r=(), w=()):
        if op1 is None:
            S.op(eng, lambda e: e.tensor_scalar(out=out, in0=in0, scalar1=s1, scalar2=None, op0=op0), r, w)
        else:
            S.op(eng, lambda e: e.tensor_scalar(out=out, in0=in0, scalar1=s1, scalar2=s2, op0=op0, op1=op1), r, w)

    def stt(eng, out, in0, scalar, in1, op0, op1, r=(), w=()):
        S.op(eng, lambda e: e.scalar_tensor_tensor(out=out, in0=in0, scalar=scalar, in1=in1, op0=op0, op1=op1), r, w)

    def cp(eng, out, in_, r=(), w=()):
        S.op(eng, lambda e: e.tensor_copy(out=out, in_=in_), r, w)

    def mset(eng, ap, val, w=()):
        S.op(eng, lambda e: e.memset(ap, val), (), w)

    psf = [es.enter_context(nc.psum_tensor("psf%d" % i, [128, 512], F32)) for i in range(6)]
    psf_r = [Res("psf%d" % i) for i in range(6)]
    psb = [es.enter_context(nc.psum_tensor("psb%d" % i, [128, 1024], BF16)) for i in range(2)]
    psb_r = [Res("psb%d" % i) for i in range(2)]
    ring = {"f": 0, "b": 0, "slab": 0, "xin": 0}

    held = set()

    def bank(hold=False):
        while True:
            i = ring["f"] % 5
            ring["f"] += 1
            if i not in held:
                break
        if hold:
            held.add(i)
        return psf[i], psf_r[i]

    def release(bk):
        for i in range(5):
            if psf[i] is bk[0]:
                held.discard(i)

    def bankb():
        i = ring["b"] % 2
        ring["b"] += 1
        return psb[i], psb_r[i]

    NSLAB = 4
    UC = 256
    slabs = [sb("slab%d" % i, [128, 8, UC], BF16) for i in range(NSLAB)]
    slabs_r = [Res("slab%d" % i) for i in range(NSLAB)]

    WSPEC = {"w_in": (w_in, 1024, 6656, 8), "w_glu": (w_glu, 512, 512, 4), "w_bs": (w_bs, 512, 1024, 4),
             "w_bh": (w_bh, 1024, 1024, 8), "w_out": (w_out, 1024, 1024, 8), "w_up": (w_up, 1024, 4096, 8),
             "w_down": (w_down, 4096, 1024, 8)}
    wscr = {}
    wscr_r = {}

    def convert_weight(wn, units=None):
        w, K, N, nk = WSPEC[wn]
        KG = K // (nk * 128)
        NU2 = N // (2 * UC)
        if wn not in wscr:
            wscr[wn] = nc.dram_tensor("scr_" + wn, [KG, NU2, 128, nk, 2 * UC], BF16, kind="Internal").ap()
        ul = list(range(NU2)) if units is None else list(units)
        for kg in range(KG):
            for u2 in ul:
                r = Res("scr_%s_%d_%d" % (wn, kg, u2))
                wscr_r[(wn, kg, u2)] = r
                src = w[kg * nk * 128:(kg + 1) * nk * 128, u2 * 2 * UC:(u2 + 1) * 2 * UC].rearrange("(k p) n -> p k n", p=128)
                S.dma("pool", wscr[wn][kg, u2], src, writes=[r])

    def load_unit(wn, kg, u):
        nk = WSPEC[wn][3]
        i = ring["slab"] % NSLAB
        ring["slab"] += 1
        S.dma("sp", slabs[i][:, 0:nk, :], wscr[wn][kg, u // 2][:, :, (u % 2) * UC:(u % 2 + 1) * UC],
              reads=[wscr_r[(wn, kg, u // 2)]], writes=[slabs_r[i]])
        return slabs[i], slabs_r[i]

    class VW:
        def __init__(self, ap2d, r):
            self.t = ap2d
            self.r = r

    class _T:
        def __init__(self, ap):
            self.ap = ap

        def __getitem__(self, key):
            return self.ap[key]

    TM = TP
    NB = TM // 128
    ge = sb("ge", [128, 2, 4 * TM]); ge_r = [Res("ge0"), Res("ge1")]
    NXIN = 3
    xin = [TL("xin%d" % i, [128, 1024]) for i in range(NXIN)]
    xsbA = [TL("xsbA%d" % i, [128, 1024], BF16) for i in range(2)]
    hT = sb("hT", [128, 8, TM], BF16); hT_r = [Res("hT%d" % k) for k in range(8)]
    hT2 = sb("hT2", [128, 8, TM], BF16); hT2_r = [Res("hT2_%d" % k) for k in range(8)]

    c_eps = TL("c_eps", [128, 1]); mset("pool", c_eps.t[:], 1e-6, [c_eps.r])
    c_negpi = TL("c_negpi", [128, 1]); mset("pool", c_negpi.t[:], -math.pi, [c_negpi.r])
    onesf = TL("onesf", [128, 128]); mset("pool", onesf.t[:], 1.0, [onesf.r])
    ones_bf = TL("ones_bf", [128, 128], BF16); mset("pool", ones_bf.t[:], 1.0, [ones_bf.r])
    identf = TL("identf", [128, 128])
    S.op("pool", lambda e: e.affine_select(out=identf.t[:], in_=onesf.t[:], pattern=[[1, 128]], compare_op=ALU.is_equal,
                                           fill=0.0, base=0, channel_multiplier=-1), [onesf.r], [identf.r])
    ident = TL("ident", [128, 128], BF16)
    cp("dve", ident.t[:], identf.t[:], [identf.r], [ident.r])
    maskP = TL("maskP", [128, 128])
    S.op("pool", lambda e: e.affine_select(out=maskP.t[:], in_=onesf.t[:], pattern=[[1, 128]], compare_op=ALU.is_ge,
                                           fill=0.0, base=0, channel_multiplier=-1), [onesf.r], [maskP.r])
    maskS = TL("maskS", [64, 64])
    cp("pool", maskS.t[:], maskP.t[0:64, 0:64], [maskP.r], [maskS.r])
    for sq in range(NSS):
        S.op("pool", lambda e, sq=sq: e.affine_select(out=maskS.t[:, 4 * sq:4 * sq + 4], in_=maskS.t[:, 4 * sq:4 * sq + 4],
                                                      pattern=[[0, 4]], compare_op=ALU.is_ge, fill=0.0, base=-4 * sq,
                                                      channel_multiplier=1), [maskS.r], [maskS.r])
    seqm = TL("seqm", [64, NSS])
    S.op("pool", lambda e: e.affine_select(out=seqm.t[:], in_=onesf.t[0:64, 0:NSS], pattern=[[-4, NSS]], compare_op=ALU.is_ge,
                                           fill=0.0, base=0, channel_multiplier=1), [onesf.r], [seqm.r])
    S.op("pool", lambda e: e.affine_select(out=seqm.t[:], in_=seqm.t[:], pattern=[[4, NSS]], compare_op=ALU.is_ge,
                                           fill=0.0, base=3, channel_multiplier=-1), [seqm.r], [seqm.r])

    cmask = TL("cmask", [128, NSS, TS], BF16)
    mset("pool", cmask.t[:], 0.0, [cmask.r])
    for sq in range(NSS):
        mset("pool", cmask.t[:, sq, 4 * sq:4 * sq + 4], 1.0, [cmask.r])

    mask2 = TL("mask2", [128, 8])
    S.op("pool", lambda e: e.affine_select(out=mask2.t[:], in_=onesf.t[:, 0:8], pattern=[[-16, 8]], compare_op=ALU.is_ge,
                                           fill=0.0, base=0, channel_multiplier=1), [onesf.r], [mask2.r])
    S.op("pool", lambda e: e.affine_select(out=mask2.t[:], in_=mask2.t[:], pattern=[[16, 8]], compare_op=ALU.is_ge,
                                           fill=0.0, base=15, channel_multiplier=-1), [mask2.r], [mask2.r])
    iot_i = TL("iot_i", [128, 128], I32)
    S.op("pool", lambda e: e.iota(iot_i.t[:], pattern=[[1, 128]], base=1, channel_multiplier=0), [], [iot_i.r])
    convert_weight("w_in", [0, 5, 6, 1, 2, 3, 4, 7, 8, 9, 10, 11, 12])
    for wn in ("w_glu", "w_bs", "w_bh", "w_out", "w_up", "w_down"):
        convert_weight(wn)

    stg = [TL("stg%d" % i, [64, 128]) for i in range(2)]
    stg_i = [0]

    def load_cols(name, vec, n):
        t = TL(name, [128, n])
        st = stg[stg_i[0] % 2]; stg_i[0] += 1
        S.dma("sp", st.t[0:n, :], vec.rearrange("(c p) -> c p", p=128), writes=[st.r])
        pm, pmr = bank()
        tr(out=pm[:, 0:n], in_=st.t[0:n, :], identity=identf.t[0:n, 0:n], r=[st.r, identf.r], w=[pmr])
        cp("dve", t.t[:], pm[:, 0:n], [pmr], [t.r])
        return t

    def load_bc(name, vec, n):
        t = TL(name, [128, n])
        src = bass.AP(vec.tensor, vec.offset, [[0, 128], [1, n]])
        S.dma("sp", t.t[:], src, writes=[t.r])
        return t

    bcol = load_cols("bcol", b_in, 52)
    gpre = load_cols("gpre", g_pre_d, 8)
    g2pre = load_cols("g2pre", g2_pre_d, 8)
    gpost_bc = load_bc("gpost_bc", g_post_d, 1024)
    g2post_bc = load_bc("g2post_bc", g2_post_d, 1024)
    bi_bc = load_bc("bi_bc", b_in[2560:3584], 1024)
    s5d = load_cols("s5d", s5d_d, 4)
    bglu = load_cols("bglu", b_glu_d, 4)
    hgn = load_cols("hgn", hgn_d, 1)
    l0 = load_cols("l0", lbl_d[0, :], 8)
    l1 = load_cols("l1", lbl_d[1, :], 8)
    lbd = TL("lbd", [128, 8])
    tt("dve", lbd.t[:], l0.t[:], l1.t[:], ALU.subtract, [l0.r, l1.r], [lbd.r])
    lb = TL("lb", [128, 8]); act(lb.t[:], lbd.t[:], AF.Sigmoid, [lbd.r], [lb.r])
    oml = TL("oml", [128, 8]); act(oml.t[:], lbd.t[:], AF.Sigmoid, [lbd.r], [oml.r], scale=-1.0)
    noml = TL("noml", [128, 8]); ts("dve", noml.t[:], oml.t[:], -1.0, None, ALU.mult, None, [oml.r], [noml.r])

    are = load_cols("are", a_re_d, 16)
    aim = load_cols("aim", a_im_d, 16)
    ldt = TL("ldt", [128, 16])
    ldt_bc = load_bc("ldt_bc", ldt_d, 32)
    for two in range(2):
        cp("dve", ldt.t[64 * two:64 * two + 64, :], ldt_bc.t[64 * two:64 * two + 64, two::2], [ldt_bc.r], [ldt.r])
    dtt = TL("dtt", [128, 16]); act(dtt.t[:], ldt.t[:], AF.Exp, [ldt.r], [dtt.r])
    dre = TL("dre", [128, 16]); tt("dve", dre.t[:], dtt.t[:], are.t[:], ALU.mult, [dtt.r, are.r], [dre.r])
    th = TL("th", [128, 16]); tt("dve", th.t[:], dtt.t[:], aim.t[:], ALU.mult, [dtt.r, aim.r], [th.r])
    rmag = TL("rmag", [128, 16]); act(rmag.t[:], dre.t[:], AF.Exp, [dre.r], [rmag.r])

    FQ = 512
    scrA = TL("scrA", [128, 2048]); scrB = TL("scrB", [128, 2048])
    rs_a = VW(_T(ge[:, 0, 0:512]), ge_r[0]); rs_b = VW(_T(ge[:, 0, 512:1024]), ge_r[0])
    rs_i = VW(_T(ge[:, 1, 0:512].bitcast(I32)), ge_r[1])

    def range_sin(out_ap, ang_ap, shift, n, rr, ww):
        ta, tb, ti = rs_a, rs_b, rs_i
        ts("dve", ta.t[:, 0:n], ang_ap, shift + math.pi, None, ALU.add, None, rr, [ta.r])
        ts("dve", tb.t[:, 0:n], ta.t[:, 0:n], 1.0 / TWO_PI, None, ALU.mult, None, [ta.r], [tb.r])
        cp("dve", ti.t[:, 0:n], tb.t[:, 0:n], [tb.r], [ti.r])
        cp("dve", tb.t[:, 0:n], ti.t[:, 0:n], [ti.r], [tb.r])
        stt("dve", ta.t[:, 0:n], tb.t[:, 0:n], -TWO_PI, ta.t[:, 0:n], ALU.mult, ALU.add, [tb.r, ta.r], [ta.r])
        ts("dve", tb.t[:, 0:n], ta.t[:, 0:n], 0.0, TWO_PI, ALU.is_lt, ALU.mult, [ta.r], [tb.r])
        tt("dve", ta.t[:, 0:n], ta.t[:, 0:n], tb.t[:, 0:n], ALU.add, [ta.r, tb.r], [ta.r])
        ts("dve", tb.t[:, 0:n], ta.t[:, 0:n], TWO_PI, -TWO_PI, ALU.is_ge, ALU.mult, [ta.r], [tb.r])
        tt("dve", ta.t[:, 0:n], ta.t[:, 0:n], tb.t[:, 0:n], ALU.add, [ta.r, tb.r], [ta.r])
        act(out_ap, ta.t[:, 0:n], AF.Sin, [ta.r, c_negpi.r], ww, bias=c_negpi.t[:], scale=1.0)

    cth = TL("cth", [128, 16]); sth = TL("sth", [128, 16])
    range_sin(cth.t[:], th.t[:], math.pi / 2, 16, [th.r], [cth.r])
    range_sin(sth.t[:], th.t[:], 0.0, 16, [th.r], [sth.r])
    abre = TL("abre", [128, 16]); tt("dve", abre.t[:], rmag.t[:], cth.t[:], ALU.mult, [rmag.r, cth.r], [abre.r])
    abim = TL("abim", [128, 16]); tt("dve", abim.t[:], rmag.t[:], sth.t[:], ALU.mult, [rmag.r, sth.r], [abim.r])
    nr = TL("nr", [128, 16]); ts("dve", nr.t[:], abre.t[:], -1.0, None, ALU.add, None, [abre.r], [nr.r])
    den = TL("den", [128, 16]); t16 = TL("t16", [128, 16]); t16b = TL("t16b", [128, 16])
    tt("dve", den.t[:], are.t[:], are.t[:], ALU.mult, [are.r], [den.r])
    tt("dve", t16.t[:], aim.t[:], aim.t[:], ALU.mult, [aim.r], [t16.r])
    tt("dve", den.t[:], den.t[:], t16.t[:], ALU.add, [den.r, t16.r], [den.r])
    rden = TL("rden", [128, 16])
    S.op("dve", lambda e: e.reciprocal(out=rden.t[:], in_=den.t[:]), [den.r], [rden.r])
    cre = TL("cre", [128, 16]); cim = TL("cim", [128, 16])
    tt("dve", t16.t[:], nr.t[:], are.t[:], ALU.mult, [nr.r, are.r], [t16.r])
    tt("dve", t16b.t[:], abim.t[:], aim.t[:], ALU.mult, [abim.r, aim.r], [t16b.r])
    tt("dve", t16.t[:], t16.t[:], t16b.t[:], ALU.add, [t16.r, t16b.r], [t16.r])
    tt("dve", cre.t[:], t16.t[:], rden.t[:], ALU.mult, [t16.r, rden.r], [cre.r])
    tt("dve", t16.t[:], abim.t[:], are.t[:], ALU.mult, [abim.r, are.r], [t16.r])
    tt("dve", t16b.t[:], nr.t[:], aim.t[:], ALU.mult, [nr.r, aim.r], [t16b.r])
    tt("dve", t16.t[:], t16.t[:], t16b.t[:], ALU.subtract, [t16.r, t16b.r], [t16.r])
    tt("dve", cim.t[:], t16.t[:], rden.t[:], ALU.mult, [t16.r, rden.r], [cim.r])

    def v3(ap):
        return _T(ap.rearrange("p (a b) -> p a b", a=16))
    Bre = VW(v3(xin[0].t[:, 0:256]), xin[0].r); Bim = VW(v3(xin[0].t[:, 256:512]), xin[0].r)
    S.dma("sp", Bre.t[:], b_re_d.rearrange("(ct p) c -> p ct c", p=128), writes=[Bre.r])
    S.dma("sp", Bim.t[:], b_im_d.rearrange("(ct p) c -> p ct c", p=128), writes=[Bim.r])
    Bbre = VW(v3(xin[0].t[:, 512:768]), xin[0].r); Bbim = VW(v3(xin[0].t[:, 768:1024]), xin[0].r)
    tB = VW(v3(xin[1].t[:, 0:256]), xin[1].r)
    creb = cre.t[:].unsqueeze(2).to_broadcast([128, 16, 16])
    cimb = cim.t[:].unsqueeze(2).to_broadcast([128, 16, 16])
    tt("dve", Bbre.t[:], Bre.t[:], creb, ALU.mult, [Bre.r, cre.r], [Bbre.r])
    tt("dve", tB.t[:], Bim.t[:], cimb, ALU.mult, [Bim.r, cim.r], [tB.r])
    tt("dve", Bbre.t[:], Bbre.t[:], tB.t[:], ALU.subtract, [Bbre.r, tB.r], [Bbre.r])
    tt("dve", Bbim.t[:], Bim.t[:], creb, ALU.mult, [Bim.r, cre.r], [Bbim.r])
    tt("dve", tB.t[:], Bre.t[:], cimb, ALU.mult, [Bre.r, cim.r], [tB.r])
    tt("dve", Bbim.t[:], Bbim.t[:], tB.t[:], ALU.add, [Bbim.r, tB.r], [Bbim.r])

    lhsT_B = TL("lhsT_B", [128, 16, 2, 128], BF16)
    lhsT_C = TL("lhsT_C", [128, 16, 2, 128], BF16)
    padf = scrA
    padb = VW(_T(hT[:].rearrange("p a b -> p (a b)").rearrange("p (a b) -> p a b", a=16)), Res("padb"))
    padb_extra = hT_r
    padf3 = padf.t[:].rearrange("p (a b) -> p a b", a=16)
    for ri, Bb in enumerate((Bbre, Bbim)):
        mset("dve", padf.t[:], 0.0, [padf.r])
        for two in range(2):
            for m in range(4):
                col = (2 * m + two) * 16
                cp("dve", padf3[64 * two:64 * two + 64, m::4, col:col + 16], Bb.t[64 * two:64 * two + 64, m::4, :],
                   [Bb.r, padf.r], [padf.r])
        cp("dve", padb.t[:], padf3, [padf.r], [padb.r])
        for half in range(2):
            pb, pbr = bankb()
            for j in range(8):
                ct = half * 8 + j
                tr(out=pb[:, j * 128:(j + 1) * 128], in_=padb.t[:, ct, :], identity=ident.t[:], r=[padb.r, ident.r], w=[pbr])
            cp("dve", lhsT_B.t[:, half * 8:half * 8 + 8, ri, :], pb[:].rearrange("p (a b) -> p a b", a=8), [pbr], [lhsT_B.r])
    for ri, cd in enumerate((c_re_d, c_im_d)):
        Cn = xin[1].t[:, 256 + 256 * ri:512 + 256 * ri].rearrange("p (u n) -> p u n", u=4)
        S.dma("sp", Cn, cd.rearrange("g c n -> (g c) n").rearrange("(u q) n -> q u n", q=128), writes=[xin[1].r])
        for uc in range(4):
            tt("dve", padf3[:, 4 * uc:4 * uc + 4, :].rearrange("p m (t n) -> p m t n", t=2),
               Cn[:, uc, :].unsqueeze(1).unsqueeze(1).to_broadcast([128, 4, 2, 64]),
               mask2.t[:].rearrange("p (m t) -> p m t", t=2).unsqueeze(3).to_broadcast([128, 4, 2, 64]), ALU.mult,
               [xin[1].r, mask2.r], [padf.r])
        if ri == 0:
            cp("dve", padb.t[:], padf3, [padf.r], [padb.r])
        else:
            ts("dve", padb.t[:], padf3, -1.0, None, ALU.mult, None, [padf.r], [padb.r])
        for half in range(2):
            pb, pbr = bankb()
            for j in range(8):
                ct = half * 8 + j
                tr(out=pb[:, j * 128:(j + 1) * 128], in_=padb.t[:, ct, :], identity=ident.t[:], r=[padb.r, ident.r], w=[pbr])
            cp("dve", lhsT_C.t[:, half * 8:half * 8 + 8, ri, :], pb[:].rearrange("p (a b) -> p a b", a=8), [pbr], [lhsT_C.r])

    iot_f = TL("iot_f", [128, 128])
    cp("dve", iot_f.t[:], iot_i.t[:], [iot_i.r], [iot_f.r])
    cosT = TL("cosT", [128, 16, 128]); sinT = TL("sinT", [128, 16, 128])
    ang = scrB
    tt("dve", ang.t[:].rearrange("p (a b) -> p a b", a=16), th.t[:].unsqueeze(2).to_broadcast([128, 16, 128]),
       iot_f.t[:].unsqueeze(1).to_broadcast([128, 16, 128]), ALU.mult, [th.r, iot_f.r], [ang.r])
    cosTf = cosT.t[:].rearrange("p a b -> p (a b)"); sinTf = sinT.t[:].rearrange("p a b -> p (a b)")
    for pc in range(4):
        range_sin(cosTf[:, pc * 512:(pc + 1) * 512], ang.t[:, pc * 512:(pc + 1) * 512], math.pi / 2, 512, [ang.r], [cosT.r])
        range_sin(sinTf[:, pc * 512:(pc + 1) * 512], ang.t[:, pc * 512:(pc + 1) * 512], 0.0, 512, [ang.r], [sinT.r])
    d0s = TL("d0s", [128, 16, 64]); d0p = d0s
    cp("dve", d0s.t[:], rmag.t[:].unsqueeze(2).to_broadcast([128, 16, 64]), [rmag.r], [d0s.r])
    mset("dve", d0s.t[:].rearrange("p a (s t) -> p a s t", t=4)[:, :, :, 0:1], 0.0, [d0s.r])
    d0hp = TL("d0hp", [128, TP]); mset("dve", d0hp.t[:], 1.0, [d0hp.r])
    mset("dve", d0hp.t[:].rearrange("p (c t) -> p c t", t=128)[:, :, 0:1], 0.0, [d0hp.r])
    d0hs = TL("d0hs", [128, TS]); mset("dve", d0hs.t[:], 1.0, [d0hs.r])
    mset("dve", d0hs.t[:].rearrange("p (c t) -> p c t", t=4)[:, :, 0:1], 0.0, [d0hs.r])

    dbg("rmag", rmag.t[:], [rmag.r]); dbg("cth", cth.t[:], [cth.r]); dbg("sth", sth.t[:], [sth.r])
    dbg("cre", cre.t[:], [cre.r]); dbg("cim", cim.t[:], [cim.r]); dbg("Bbre", Bbre.t[:], [Bbre.r]); dbg("Bbim", Bbim.t[:], [Bbim.r])
    dbg("lhsT_B", lhsT_B.t[:], [lhsT_B.r]); dbg("lhsT_C", lhsT_C.t[:], [lhsT_C.r])
    dbg("cosT", cosT.t[:], [cosT.r]); dbg("sinT", sinT.t[:], [sinT.r])
    dbg("maskS", maskS.t[:], [maskS.r]); dbg("seqm", seqm.t[:], [seqm.r]); dbg("lb", lb.t[:], [lb.r])
    hc_re = TL("hc_re", [128, 16, NSS]); hc_im = TL("hc_im", [128, 16, NSS])
    mset("dve", hc_re.t[:], 0.0, [hc_re.r]); mset("dve", hc_im.t[:], 0.0, [hc_im.r])
    Sst = TL("Sst", [128, 8, 128])
    Sst_r = [Res("Sst%d" % h) for h in range(8)]
    mset("dve", Sst.t[:], 0.0, Sst_r)

    xsb = TL("xsb", [128, 1024], BF16)
    junk = xsb
    ssq = TL("ssq", [128, 1]); lnv1 = TL("lnv1", [128, 1]); rstd1 = TL("rstd1", [128, 1])
    u_f = sb("u_f", [128, 4, TM]); u_f_r = [Res("u_f%d" % k) for k in range(4)]
    u_bf = sb("u_bf", [128, 4, TM], BF16); u_bf_r = [Res("u_bf%d" % k) for k in range(4)]
    yg_bf = sb("yg_bf", [128, 4, TM], BF16); yg_r = [Res("yg%d" % k) for k in range(4)]
    y2_bf = sb("y2_bf", [128, 4, TM], BF16); y2_r = [Res("y2%d" % k) for k in range(4)]
    q_f = sb("q_f", [128, 8, TM]); q_r = [Res("q%d" % k) for k in range(8)]
    sig_f = sb("sig_f", [128, 8, TM]); sig_r = [Res("sig%d" % k) for k in range(8)]
    v_tok = sb("v_tok", [128, NB, 1024], BF16); v_r = [Res("v%d" % b) for b in range(NB)]
    gs5 = sb("gs5", [128, 8, TM], BF16); gs5_r = [Res("gs5%d" % k) for k in range(8)]
    ghg = sb("ghg", [128, 8, TM], BF16); ghg_r = [Res("ghg%d" % k) for k in range(8)]
    big = sb("big", [128, 32, TM], BF16); big_r = [Res("big%d" % k) for k in range(32)]
    qe = big[:, 0:8, :]; qe_r = big_r[0:8]
    ke = big[:, 8:16, :]; ke_r = big_r[8:16]
    sg_bf = big[:, 16:24, :]; sg_r = big_r[16:24]
    yhg = big[:, 24:32, :]; yhg_r = big_r[24:32]
    hid = big; hid_r = big_r
    ms = sig_f; ms_r = sig_r
    mg_bf = qe; mg_r = qe_r
    x1 = q_f[:].rearrange("p a b -> p (a b)").rearrange("p (n d) -> p n d", n=NB)
    x1_rl = lambda b: q_r[(8 // NB) * b:(8 // NB) * (b + 1)]
    def mk_hgset(idx):
        if idx == 0:
            f = [TL("hg%d_%d" % (idx, i), [128, TM]) for i in range(7)]
            osq_ = TL("hg%d_osq" % idx, [128, TM], BF16)
        elif idx == 1:
            gflat = ge[:].rearrange("p a b -> p (a b)")
            f = [VW(_T(gflat[:, i * TM:(i + 1) * TM]), Res("hg1_%d" % i)) for i in range(7)]
            osq_ = VW(_T(gflat[:, 7 * TM:7 * TM + TM // 2].bitcast(BF16)), Res("hg1_osq"))
        else:
            sc = scrA if idx == 2 else scrB
            f = [VW(_T(sc.t[:, i * TM:(i + 1) * TM]), Res("hg%d_%d" % (idx, i))) for i in range(7)]
            osq_ = VW(_T(sc.t[:, 7 * TM:7 * TM + TM // 2].bitcast(BF16)), Res("hg%d_osq" % idx))
        hG_ = TL("hG%d" % idx, [128, NSS]); hel_ = TL("hel%d" % idx, [128, NSS])
        Ssc_ = TL("Ssc%d" % idx, [128, 128], BF16)
        if idx >= 2 and 7 * TM + TM // 2 + 128 <= 2048:
            dst_ = VW(_T(sc.t[:, 7 * TM + TM // 2:7 * TM + TM // 2 + 128]), Res("hg%d_dst" % idx))
        else:
            dst_ = TL("dst%d" % idx, [128, 128])
        kt_ = TL("ketok%d" % idx, [128, 128], BF16); sm_ = TL("scm%d" % idx, [128, 128], BF16)
        bufs = f + [osq_, hG_, hel_, Ssc_, dst_, kt_, sm_]
        return {"bufs": bufs, "res": [x.r for x in f] + [osq_.r, dst_.r], "po": None}
    HGSET = [mk_hgset(0), mk_hgset(1), mk_hgset(2), mk_hgset(3)]
    HGSET[0]["po"] = (psf[5], psf_r[5])
    ketokM = TL("ketokM", [64, NSS, 128], BF16)
    qeM = TL("qeM", [128, NSS, TS], BF16)
    gtmp = [TL("gtmp%d" % i, [128, TM]) for i in range(2)]
    scrA_q = [Res("scrAq%d" % i) for i in range(8)]
    scrB_q = [Res("scrBq%d" % i) for i in range(8)]
    S5SET = []
    for si, (sc, scq) in enumerate(((scrA, scrA_q), (scrB, scrB_q))):
        S5SET.append([VW(_T(sc.t[:, i * 256:(i + 1) * 256]), scq[i]) for i in range(8)])
    S.op("dve", lambda e: e.memset(scrA.t[0:1, 0:1], 0.0), [], [scrA.r, scrB.r, padb.r] + scrA_q + scrB_q + hT_r)
    hre_bf = sb("hre_bf", [128, FQ], BF16); him_bf = sb("him_bf", [128, FQ], BF16)
    hbf_r = [[Res("hre0"), Res("him0")], [Res("hre1"), Res("him1")]]
    ctmp = sb("ctmp", [128, 4, NSS]); ctmp_r = [Res("ctmp0"), Res("ctmp1")]
    ge1 = VW(_T(ge[:, 0, :]), ge_r[0]); ge2 = VW(_T(ge[:, 1, :]), ge_r[1])
    assert 8 * TM == NB * 1024
    mo = ge[:].rearrange("p a b -> p (a b)").rearrange("p (n d) -> p n d", n=NB)
    mo_rl = lambda b: ge_r if NB == 1 else [ge_r[b]]

    def lin_fm_gen(wn, col0, ncols, rhs_fn, rhs_res, T, evac):
        nk = WSPEC[wn][3]
        for u0 in range(0, ncols, UC):
            sl, slr = load_unit(wn, 0, (col0 + u0) // UC)
            for mi in range(UC // 128):
                pm, pmr = bank()
                for k in range(nk):
                    mm(pm[:, 0:T], lhsT=sl[:, k, mi * 128:(mi + 1) * 128], rhs=rhs_fn(k), start=(k == 0), stop=(k == nk - 1),
                       r=[slr] + rhs_res, w=[pmr])
                evac((u0 // 128) + mi, pm, pmr)
                yield

    def lin_fm(*a, **k):
        for _ in lin_fm_gen(*a, **k):
            pass

    def interleave(gens):
        act_l = list(gens)
        while act_l:
            for item in list(act_l):
                g, k = item
                for _ in range(k):
                    try:
                        next(g)
                    except StopIteration:
                        act_l.remove(item)
                        break

    def rms_rows(src_ap, src_res, nrows):
        mset("dve", ssq.t[:], 0.0, [ssq.r])
        act(junk.t[0:nrows, :], src_ap, AF.Square, src_res, [junk.r, ssq.r], accum=ssq.t[0:nrows, :])
        act(lnv1.t[0:nrows, :], ssq.t[0:nrows, :], AF.Ln, [ssq.r, c_eps.r], [lnv1.r], bias=c_eps.t[0:nrows, :], scale=1.0 / 1024)
        act(rstd1.t[0:nrows, :], lnv1.t[0:nrows, :], AF.Exp, [lnv1.r], [rstd1.r], scale=-0.5)

    def norm_transpose(src_ap, src_res, nrows, gcol, col0, dstT=None, dstT_r=None):
        if dstT is None:
            dstT, dstT_r = hT, hT_r
        rms_rows(src_ap, src_res, nrows)
        ts("dve", xsb.t[0:nrows, :], src_ap, rstd1.t[0:nrows, :], None, ALU.mult, None, src_res + [rstd1.r], [xsb.r])
        pb, pbr = bankb()
        for k in range(8):
            tr(out=pb[:, k * 128:k * 128 + nrows], in_=xsb.t[0:nrows, k * 128:(k + 1) * 128],
                                                 identity=ident.t[0:nrows, 0:nrows], r=[xsb.r, ident.r], w=[pbr])
        tt("dve", dstT[:, :, col0:col0 + nrows], pb[:].rearrange("p (a b) -> p a b", a=8)[:, :, 0:nrows],
           gcol.t[:].unsqueeze(2).to_broadcast([128, 8, nrows]), ALU.mult, [pbr, gcol.r], dstT_r)

    a_state = {"done": None}

    def a_pre(kind, tok0, T):
        nrows = 128 if kind == "p" else 64
        nblk = T // 128 if kind == "p" else 1
        xd = xp if kind == "p" else xs
        for b in range(nblk):
            xi = xin[ring["xin"] % NXIN]; ring["xin"] += 1
            S.dma("sp", xi.t[0:nrows, :], xd[tok0 + b * 128: tok0 + b * 128 + nrows, :], writes=[xi.r])
            rms_rows(xi.t[0:nrows, :], [xi.r], nrows)
            ts("dve", xsbA[b].t[0:nrows, :], xi.t[0:nrows, :], rstd1.t[0:nrows, :], None, ALU.mult, None, [xi.r, rstd1.r], [xsbA[b].r])

    def a_tr(kind, tok0, T):
        nrows = 128 if kind == "p" else 64
        nblk = T // 128 if kind == "p" else 1
        for b in range(nblk):
            pb, pbr = bankb()
            for k in range(8):
                tr(out=pb[:, k * 128:k * 128 + nrows], in_=xsbA[b].t[0:nrows, k * 128:(k + 1) * 128], identity=ident.t[0:nrows, 0:nrows],
                   r=[xsbA[b].r, ident.r], w=[pbr])
            tt("dve", hT[:, :, b * 128:b * 128 + nrows], pb[:].rearrange("p (a b) -> p a b", a=8)[:, :, 0:nrows],
               gpre.t[:].unsqueeze(2).to_broadcast([128, 8, nrows]), ALU.mult, [pbr, gpre.r], hT_r)
        a_state["done"] = (kind, tok0)

    def make_tile(kind, tok0, T):
        nrows = 128 if kind == "p" else 64
        nblk = T // 128 if kind == "p" else 1
        xd = xp if kind == "p" else xs
        yd = yp if kind == "p" else ys
        hrhs = lambda k: hT[:, k, 0:T]
        hrhs2 = lambda k: hT2[:, k, 0:T]
        tagn = "%s%d_" % (kind, tok0)
        tile = {}

        def front_pre():
            a_pre(kind, tok0, T)

        def ev_u(m, pm, pmr):
            act(u_f[:, m, 0:T], pm[:, 0:T], AF.Identity, [pmr, bcol.r], [u_f_r[m]], bias=bcol.t[:, m:m + 1])
            act(u_bf[:, m, 0:T], u_f[:, m, 0:T], AF.Copy, [u_f_r[m]], [u_bf_r[m]])

        def front_rest():
            a_tr(kind, tok0, T)
            lin_fm("w_in", 0, 512, hrhs, hT_r, T, ev_u)
            for s_ in range(4):
                sl, slr = load_unit("w_in", 0, 10 + s_)
                for b in range(nblk):
                    pm, pmr = bank()
                    for k in range(8):
                        mm(pm[0:nrows, 0:UC], lhsT=hT[:, k, b * 128:b * 128 + nrows], rhs=sl[:, k, :], start=(k == 0), stop=(k == 7),
                           r=[slr] + hT_r, w=[pmr])
                    tt("dve", v_tok[0:nrows, b, s_ * UC:(s_ + 1) * UC], pm[0:nrows, 0:UC], bi_bc.t[0:nrows, s_ * UC:(s_ + 1) * UC], ALU.add,
                       [pmr, bi_bc.r], [v_r[b]])
        tile["front_pre"] = front_pre
        tile["front_rest"] = front_rest

        def ev_q(m, pm, pmr):
            act(q_f[:, m, 0:T], pm[:, 0:T], AF.Silu, [pmr, bcol.r], [q_r[m]], bias=bcol.t[:, 4 + m:5 + m])

        def ev_g(m, pm, pmr):
            act(sg_bf[:, m, 0:T], pm[:, 0:T], AF.Silu, [pmr, bcol.r], [sg_r[m]], bias=bcol.t[:, 28 + m:29 + m])

        def ev_f(m, pm, pmr):
            act(sig_f[:, m, 0:T], pm[:, 0:T], AF.Sigmoid, [pmr, bcol.r], [sig_r[m]], bias=bcol.t[:, 12 + m:13 + m])

        def ev_gs(m, pm, pmr):
            act(gs5[:, m, 0:T], pm[:, 0:T], AF.Sigmoid, [pmr, bcol.r], [gs5_r[m]], bias=bcol.t[:, 36 + m:37 + m])

        def ev_gh(m, pm, pmr):
            act(ghg[:, m, 0:T], pm[:, 0:T], AF.Sigmoid, [pmr, bcol.r], [ghg_r[m]], bias=bcol.t[:, 44 + m:45 + m])

        def proj_a_gen():
            segs = ((512, ev_q), (1536, ev_f)) if kind == "p" else ((512, ev_q), (3584, ev_g), (1536, ev_f))
            for (c0_, ev) in segs:
                yield from lin_fm_gen("w_in", c0_, 1024, hrhs, hT_r, T, ev)

        def proj_b_gen():
            segs = ((3584, ev_g), (4608, ev_gs), (5632, ev_gh)) if kind == "p" else ((4608, ev_gs), (5632, ev_gh))
            for (c0_, ev) in segs:
                yield from lin_fm_gen("w_in", c0_, 1024, hrhs, hT_r, T, ev)

        if kind == "p":
            groups = [(c * 128, 128, 1) for c in range(T // 128)]
        else:
            groups = [(0, 4, NSS)]

        def s5_gen():
            pending_c = []
            for (c0, L, nch) in groups:
                F = L * nch
                NF = 2 * F
                d0 = d0p if kind == "p" else d0s
                assert F == (128 if kind == "p" else 64)
                py = None
                for pr in range(8):
                    si = pr % 2
                    a0, a1, a2, a3, wre, wim, zre, zim = S5SET[si]
                    hre = hre_bf[:, si * 256:si * 256 + NF]; him = him_bf[:, si * 256:si * 256 + NF]
                    hre_r, him_r = hbf_r[si]
                    ctm = ctmp[:, 2 * si:2 * si + 2, 0:nch]; ctm_r = ctmp_r[si]
                    uc = pr // 2
                    pp, ppr = bank()
                    for j in range(2):
                        ct = 2 * pr + j
                        mm(pp[:, j * F:(j + 1) * F], lhsT=lhsT_B.t[:, ct, 0, :], rhs=u_bf[:, uc, c0:c0 + F], start=True, stop=True,
                           r=[lhsT_B.r, u_bf_r[uc]], w=[ppr])
                        mm(pp[:, 256 + j * F:256 + (j + 1) * F], lhsT=lhsT_B.t[:, ct, 1, :], rhs=u_bf[:, uc, c0:c0 + F], start=True, stop=True,
                           r=[lhsT_B.r, u_bf_r[uc]], w=[ppr])
                    pre = pp[:, 0:NF]; pim = pp[:, 256:256 + NF]

                    def v4(ap, nch=nch):
                        return ap.rearrange("p (j c t) -> p j c t", j=2, c=nch)
                    cosb = cosT.t[:, 2 * pr:2 * pr + 2, 0:L].unsqueeze(2).to_broadcast([128, 2, nch, L])
                    sinb = sinT.t[:, 2 * pr:2 * pr + 2, 0:L].unsqueeze(2).to_broadcast([128, 2, nch, L])
                    tt("dve", v4(a0.t[:, 0:NF]), v4(pre), cosb, ALU.mult, [ppr, cosT.r], [a0.r])
                    tt("dve", v4(a1.t[:, 0:NF]), v4(pim), sinb, ALU.mult, [ppr, sinT.r], [a1.r])
                    tt("dve", v4(a2.t[:, 0:NF]), v4(pim), cosb, ALU.mult, [ppr, cosT.r], [a2.r])
                    tt("dve", v4(a3.t[:, 0:NF]), v4(pre), sinb, ALU.mult, [ppr, sinT.r], [a3.r])
                    yield
                    tt("pool", wre.t[:, 0:NF], a0.t[:, 0:NF], a1.t[:, 0:NF], ALU.add, [a0.r, a1.r], [wre.r])
                    tt("pool", wim.t[:, 0:NF], a2.t[:, 0:NF], a3.t[:, 0:NF], ALU.subtract, [a2.r, a3.r], [wim.r])
                    yield
                    if kind == "p":
                        for j in range(2):
                            ct = 2 * pr + j
                            rbc = rmag.t[:, ct:ct + 1].to_broadcast([128, F])
                            scan2(zre.t[:, j * F:(j + 1) * F], rbc, wre.t[:, j * F:(j + 1) * F], hc_re.t[:, ct, 0:1],
                                  [rmag.r, wre.r, hc_re.r], [zre.r])
                            scan2(zim.t[:, j * F:(j + 1) * F], rbc, wim.t[:, j * F:(j + 1) * F], hc_im.t[:, ct, 0:1],
                                  [rmag.r, wim.r, hc_im.r], [zim.r])
                            yield
                    else:
                        rb = rmag.t[:, 2 * pr:2 * pr + 2].unsqueeze(2).to_broadcast([128, 2, nch])
                        for (wt, hc) in ((wre, hc_re), (wim, hc_im)):
                            tt("pool", ctm, hc.t[:, 2 * pr:2 * pr + 2, 0:nch], rb, ALU.mult, [hc.r, rmag.r], [ctm_r])
                            w0 = v4(wt.t[:, 0:NF])[:, :, :, 0]
                            tt("pool", w0, w0, ctm, ALU.add, [wt.r, ctm_r], [wt.r])
                        yield
                        d0q = d0.t[:, 2 * pr:2 * pr + 2, :].rearrange("p a b -> p (a b)")
                        scan(zre.t[:, 0:NF], d0q, wre.t[:, 0:NF], [d0.r, wre.r], [zre.r])
                        scan(zim.t[:, 0:NF], d0q, wim.t[:, 0:NF], [d0.r, wim.r], [zim.r])
                        yield
                    tt("pool", v4(a0.t[:, 0:NF]), v4(zre.t[:, 0:NF]), cosb, ALU.mult, [zre.r, cosT.r], [a0.r])
                    tt("dve", v4(a1.t[:, 0:NF]), v4(zim.t[:, 0:NF]), sinb, ALU.mult, [zim.r, sinT.r], [a1.r])
                    yield
                    tt("pool", v4(a2.t[:, 0:NF]), v4(zim.t[:, 0:NF]), cosb, ALU.mult, [zim.r, cosT.r], [a2.r])
                    tt("dve", v4(a3.t[:, 0:NF]), v4(zre.t[:, 0:NF]), sinb, ALU.mult, [zre.r, sinT.r], [a3.r])
                    yield
                    if pending_c and pr % 2 == 0:
                        pending_c.pop(0)()
                    tt("dve", hre, a0.t[:, 0:NF], a1.t[:, 0:NF], ALU.subtract, [a0.r, a1.r], [hre_r])
                    tt("pool", him, a2.t[:, 0:NF], a3.t[:, 0:NF], ALU.add, [a2.r, a3.r], [him_r])
                    tt("pool", hc_re.t[:, 2 * pr:2 * pr + 2, 0:nch], v4(a0.t[:, 0:NF])[:, :, :, L - 1], v4(a1.t[:, 0:NF])[:, :, :, L - 1],
                       ALU.subtract, [a0.r, a1.r], [hc_re.r])
                    tt("pool", hc_im.t[:, 2 * pr:2 * pr + 2, 0:nch], v4(a2.t[:, 0:NF])[:, :, :, L - 1], v4(a3.t[:, 0:NF])[:, :, :, L - 1],
                       ALU.add, [a2.r, a3.r], [hc_im.r])
                    yield
                    if pr % 2 == 1:
                        def do_c(pr=pr, uc=uc, c0=c0, F=F, NF=NF):
                            pyt, pyr = bank()
                            idx = 0
                            for pq in (pr - 1, pr):
                                sj = pq % 2
                                hre_q = hre_bf[:, sj * 256:sj * 256 + NF]; him_q = him_bf[:, sj * 256:sj * 256 + NF]
                                for j in range(2):
                                    ct = 2 * pq + j
                                    mm(pyt[:, 0:F], lhsT=lhsT_C.t[:, ct, 0, :], rhs=hre_q[:, j * F:(j + 1) * F],
                                       start=(idx == 0), stop=False, r=[lhsT_C.r, hbf_r[sj][0]], w=[pyr])
                                    mm(pyt[:, 0:F], lhsT=lhsT_C.t[:, ct, 1, :], rhs=him_q[:, j * F:(j + 1) * F],
                                       start=False, stop=(idx == 3), r=[lhsT_C.r, hbf_r[sj][1]], w=[pyr])
                                    idx += 1
                            stt("dve", u_f[:, uc, c0:c0 + F], u_f[:, uc, c0:c0 + F], s5d.t[:, uc:uc + 1], pyt[:, 0:F], ALU.mult, ALU.add,
                                [u_f_r[uc], s5d.r, pyr], [u_f_r[uc]])
                        pending_c.append(do_c)
                    yield
            while pending_c:
                pending_c.pop(0)()
            yield

        def gelu_glu_gen():
            for m in range(4):
                yv = u_f[:, m, 0:T]
                g1 = scrA.t[:, m * 256:m * 256 + T]; g1r = scrA_q[m]
                g2 = scrB.t[:, m * 256:m * 256 + T]; g2r = scrB_q[m]
                tt("dve", g1, yv, yv, ALU.mult, [u_f_r[m]], [g1r])
                ts("dve", g1, g1, 0.044715, 1.0, ALU.mult, ALU.add, [g1r], [g1r])
                tt("dve", g1, g1, yv, ALU.mult, [g1r, u_f_r[m]], [g1r])
                yield
                act(g2, g1, AF.Sigmoid, [g1r], [g2r], scale=1.5957691216057308)
                tt("dve", yv, yv, g2, ALU.mult, [u_f_r[m], g2r], [u_f_r[m]])
                act(yg_bf[:, m, 0:T], yv, AF.Copy, [u_f_r[m]], [yg_r[m]])
                yield

        def glu_now():
            def ev_glu(m, pm, pmr):
                gt = scrA.t[:, 1024 + (m % 2) * 256:1024 + (m % 2) * 256 + T]; gtr = scrA_q[4 + m % 2]
                act(gt, pm[:, 0:T], AF.Sigmoid, [pmr, bglu.r], [gtr], bias=bglu.t[:, m:m + 1])
                tt("dve", y2_bf[:, m, 0:T], u_f[:, m, 0:T], gt, ALU.mult, [u_f_r[m], gtr], [y2_r[m]])
            lin_fm("w_glu", 0, 512, lambda k: yg_bf[:, k, 0:T], yg_r, T, ev_glu)

        def s5_plus_gen():
            yield from s5_gen()
            yield from gelu_glu_gen()

        s5_holder = {}

        def s5():
            if "g" not in s5_holder:
                s5_holder["g"] = s5_plus_gen()
            return s5_holder["g"]
        tile["s5"] = s5

        def rest(nt=None, pre_s5_hook=None):
            g_s5 = tile["s5"]()
            g_pa = proj_a_gen()
            done = {"s5": False}

            def step(g, n):
                for _ in range(n):
                    try:
                        next(g)
                    except StopIteration:
                        return False
                return True
            while True:
                if not done["s5"] and not step(g_s5, 2):
                    done["s5"] = True
                if not step(g_pa, 1):
                    break
            if kind == "s" and not done["s5"]:
                for _ in g_s5:
                    pass
                done["s5"] = True
            if done["s5"]:
                glu_now()
                tile["glu_done"] = True

            dbg(tagn + "u_f", u_f[:, :, 0:T], u_f_r)
            dbg(tagn + "q_f", q_f[:, :, 0:T], q_r)
            dbg(tagn + "sig_f", sig_f[:, :, 0:T], sig_r)
            dbg(tagn + "v_tok", v_tok[:], v_r)
            dbg(tagn + "sg", sg_bf[:, :, 0:T], sg_r)
            dbg(tagn + "gs5", gs5[:, :, 0:T], gs5_r)

            if kind == "p":
                chunks = [(c * 128, 128) for c in range(T // 128)]
            else:
                chunks = [(0, 64)]
            ncs = len(chunks)

            def hg_head_gen(h, st):
                hf, hb, hbm, heb, henb, o_sb, orstd, osq, hG, hel, Ssc, dst, kt, sm = st["bufs"]
                hlf = hf; hk = hf; olv = orstd
                d0h = d0hp if kind == "p" else d0hs
                act(hlf.t[:, 0:T], sig_f[:, h, 0:T], AF.Ln, [sig_r[h], oml.r, lb.r], [hlf.r], bias=lb.t[:, h:h + 1], scale=oml.t[:, h:h + 1])
                yield
                scan(hb.t[:, 0:T], d0h.t[:, 0:T], hlf.t[:, 0:T], [d0h.r, hlf.r], [hb.r])
                yield
                if kind == "p":
                    hb3 = hb.t[:, 0:T].rearrange("p (c t) -> p c t", t=128)
                    act(hG.t[:, 0:ncs], hb3[:, :, 63], AF.Exp, [hb.r], [hG.r])
                    act(hel.t[:, 0:ncs], hb3[:, :, 127], AF.Exp, [hb.r], [hel.r])
                    tt("dve", hbm.t[:, 0:T].rearrange("p (c t) -> p c t", t=128), hb3, hb3[:, :, 63:64].to_broadcast([128, ncs, 128]),
                       ALU.subtract, [hb.r], [hbm.r])
                    yield
                    act(heb.t[:, 0:T], hbm.t[:, 0:T], AF.Exp, [hbm.r], [heb.r])
                    act(henb.t[:, 0:T], hbm.t[:, 0:T], AF.Exp, [hbm.r], [henb.r], scale=-1.0)
                else:
                    act(hel.t[:, 0:NSS], hb.t[:, 0:T].rearrange("p (c t) -> p c t", t=4)[:, :, 3], AF.Exp, [hb.r], [hel.r])
                    yield
                    act(heb.t[:, 0:T], hb.t[:, 0:T], AF.Exp, [hb.r], [heb.r])
                    act(henb.t[:, 0:T], hb.t[:, 0:T], AF.Exp, [hb.r], [henb.r], scale=-1.0)
                ts("dve", hk.t[:, 0:T], sig_f[:, h, 0:T], noml.t[:, h:h + 1], oml.t[:, h:h + 1], ALU.mult, ALU.add,
                   [sig_r[h], noml.r, oml.r], [hk.r])
                yield
                tt("pool", qe[:, h, 0:T], q_f[:, h, 0:T], heb.t[:, 0:T], ALU.mult, [q_r[h], heb.r], [qe_r[h]])
                tt("pool", ke[:, h, 0:T], hk.t[:, 0:T], henb.t[:, 0:T], ALU.mult, [hk.r, henb.r], [ke_r[h]])
                yield

                if st["po"] is None:
                    pob = bank(hold=True)
                else:
                    pob = st["po"]
                po, por = pob
                for ci, (c0, Sz) in enumerate(chunks):
                    pb, pbr = bankb()
                    tr(out=pb[0:Sz, 0:128], in_=ke[:, h, c0:c0 + Sz], identity=ident.t[:], r=[ke_r[h], ident.r], w=[pbr])
                    psc, pscr = bank()
                    mm(psc[0:Sz, 0:Sz], lhsT=ke[:, h, c0:c0 + Sz], rhs=qe[:, h, c0:c0 + Sz],
                       start=True, stop=True, r=[ke_r[h], qe_r[h]], w=[pscr])
                    act(kt.t[0:Sz, :], pb[0:Sz, 0:128], AF.Copy, [pbr], [kt.r])
                    mk = maskP if kind == "p" else maskS
                    tt("dve", sm.t[0:Sz, 0:Sz], psc[0:Sz, 0:Sz], mk.t[0:Sz, 0:Sz], ALU.mult, [pscr, mk.r], [sm.r])
                    vb = ci if kind == "p" else 0
                    if kind == "p":
                        act(Ssc.t[:], Sst.t[:, h, :], AF.Copy, [Sst_r[h], hG.r], [Ssc.r], scale=hG.t[:, ci:ci + 1])
                        yield
                        mm(po[:, c0:c0 + Sz], lhsT=v_tok[0:Sz, vb, h * 128:(h + 1) * 128], rhs=sm.t[0:Sz, 0:Sz], start=True, stop=False,
                           r=[v_r[vb], sm.r], w=[por])
                        mm(po[:, c0:c0 + Sz], lhsT=Ssc.t[:], rhs=qe[:, h, c0:c0 + Sz], start=False, stop=True, r=[Ssc.r, qe_r[h]], w=[por])
                        pds, pdsr = bank()
                        mm(pds[:, 0:128], lhsT=kt.t[0:Sz, :], rhs=v_tok[0:Sz, vb, h * 128:(h + 1) * 128], start=True, stop=True,
                           r=[kt.r, v_r[vb]], w=[pdsr])
                        ts("dve", dst.t[:], pds[:, 0:128], heb.t[:, c0 + 127:c0 + 128], None, ALU.mult, None, [pdsr, heb.r], [dst.r])
                        stt("dve", Sst.t[:, h, :], Sst.t[:, h, :], hel.t[:, ci:ci + 1], dst.t[:], ALU.mult, ALU.add,
                            [Sst_r[h], hel.r, dst.r], [Sst_r[h]])
                        yield
                    else:
                        mm(po[:, 0:64], lhsT=v_tok[0:64, 0, h * 128:(h + 1) * 128], rhs=sm.t[0:64, 0:64],
                           start=True, stop=False, r=[v_r[0], sm.r], w=[por])
                        tt("dve", ketokM.t[:], kt.t[0:64, :].unsqueeze(1).to_broadcast([64, NSS, 128]),
                           seqm.t[:].unsqueeze(2).to_broadcast([64, NSS, 128]), ALU.mult, [kt.r, seqm.r], [ketokM.r])
                        tt("dve", qeM.t[:], qe[:, h, 0:TS].unsqueeze(1).to_broadcast([128, NSS, TS]), cmask.t[:], ALU.mult,
                           [qe_r[h], cmask.r], [qeM.r])
                        sX, sXr = (scrA, scrA_q) if h % 2 == 0 else (scrB, scrB_q)
                        s03 = sX.t[:].rearrange("p (s v) -> p s v", s=NSS)
                        xsb3 = xsb.t[:].rearrange("p (s v) -> p s v", s=8)
                        for half in range(2):
                            if half == 0:
                                act(xsb3, s03[:, 0:8, :], AF.Copy, sXr, [xsb.r])
                            else:
                                act(xsb3, s03[:, 8:16, :], AF.Copy, sXr, [xsb.r])
                            for j in range(8):
                                sq = half * 8 + j
                                mm(po[:, 0:64], lhsT=xsb3[:, j, :], rhs=qeM.t[:, sq, :], start=False, stop=(sq == NSS - 1),
                                   r=[xsb.r, qeM.r], w=[por])
                        sn3 = ge[:].rearrange("p a b -> p (a b)").rearrange("p (s v) -> p s v", s=NSS)
                        for g4 in range(4):
                            pds, pdsr = bank()
                            for j in range(4):
                                sq = 4 * g4 + j
                                mm(pds[:, j * 128:(j + 1) * 128], lhsT=ketokM.t[:, sq, :], rhs=v_tok[0:64, 0, h * 128:(h + 1) * 128],
                                   start=True, stop=True, r=[ketokM.r, v_r[0]], w=[pdsr])
                            tt("dve", sn3[:, 4 * g4:4 * g4 + 4, :], pds[:, 0:512].rearrange("p (s v) -> p s v", s=4), s03[:, 4 * g4:4 * g4 + 4, :],
                               ALU.add, [pdsr] + sXr, ge_r)
                            tt("dve", sn3[:, 4 * g4:4 * g4 + 4, :], sn3[:, 4 * g4:4 * g4 + 4, :],
                               hel.t[:, 4 * g4:4 * g4 + 4].unsqueeze(2).to_broadcast([128, 4, 128]), ALU.mult, ge_r + [hel.r], ge_r)
                        S.dma("sp", o_shg[:, h].rearrange("s k v -> k s v"), sn3, reads=ge_r)
                        yield
                act(o_sb.t[:, 0:T], po[:, 0:T], AF.Copy, [por], [o_sb.r])
                act(osq.t[:, 0:T], po[:, 0:T], AF.Square, [por], [osq.r])
                if st["po"] is None:
                    release(pob)
                yield
                pss, pssr = bank()
                mm(pss[:, 0:T], lhsT=ones_bf.t[:], rhs=osq.t[:, 0:T], start=True, stop=True, r=[ones_bf.r, osq.r], w=[pssr])
                act(olv.t[:, 0:T], pss[:, 0:T], AF.Ln, [pssr, c_eps.r], [olv.r], bias=c_eps.t[:], scale=1.0 / 128)
                yield
                act(orstd.t[:, 0:T], olv.t[:, 0:T], AF.Exp, [olv.r], [orstd.r], scale=-0.5)
                yield
                tt("dve", o_sb.t[:, 0:T], o_sb.t[:, 0:T], orstd.t[:, 0:T], ALU.mult, [o_sb.r, orstd.r], [o_sb.r])
                yield
                stt("dve", yhg[:, h, 0:T], o_sb.t[:, 0:T], hgn.t[:, 0:1], sg_bf[:, h, 0:T], ALU.mult, ALU.mult,
                    [o_sb.r, hgn.r, sg_r[h]], [yhg_r[h]])
                yield

            def hg_all_gen():
                if kind == "p":
                    nway = 4 if done["s5"] else 2
                    for h in range(0, 8, nway):
                        alive = [hg_head_gen(h + j, HGSET[j]) for j in range(nway)]
                        while alive:
                            for g in list(alive):
                                try:
                                    next(g)
                                    yield
                                except StopIteration:
                                    alive.remove(g)
                else:
                    def s0_load(hh):
                        sX, sXr = (scrA, scrA_q) if hh % 2 == 0 else (scrB, scrB_q)
                        S.dma("sp", sX.t[:].rearrange("p (s v) -> p s v", s=NSS), st_hg[:, hh].rearrange("s k v -> k s v"), writes=sXr)
                    s0_load(0)
                    for h in range(8):
                        if h + 1 < 8:
                            s0_load(h + 1)
                        yield from hg_head_gen(h, HGSET[0])

            if kind == "p":
                S.op("pool", lambda e: e.memset(HGSET[1]["bufs"][9].t[0:1, 0:1], 0.0), [], ge_r + HGSET[1]["res"])
                if done["s5"]:
                    S.op("pool", lambda e: e.memset(HGSET[2]["bufs"][9].t[0:1, 0:1], 0.0), [],
                         scrA_q + scrB_q + HGSET[2]["res"] + HGSET[3]["res"])
            gl = [(g_s5, 8), (hg_all_gen(), 16), (proj_b_gen(), 8)] if not done["s5"] else [(hg_all_gen(), 16), (proj_b_gen(), 8)]
            interleave(gl)
            if kind == "p":
                S.op("pool", lambda e: e.memset(HGSET[1]["bufs"][9].t[0:1, 0:1], 0.0), [], ge_r + HGSET[1]["res"])
                if done["s5"]:
                    S.op("pool", lambda e: e.memset(HGSET[2]["bufs"][9].t[0:1, 0:1], 0.0), [],
                         scrA_q + scrB_q + HGSET[2]["res"] + HGSET[3]["res"])

            dbg(tagn + "ys5", u_f[:, :, 0:T], u_f_r)
            dbg(tagn + "hc_re", hc_re.t[:], [hc_re.r])
            dbg(tagn + "y2", y2_bf[:, :, 0:T], y2_r)
            dbg(tagn + "qe", qe[:, :, 0:T], qe_r)
            dbg(tagn + "ke", ke[:, :, 0:T], ke_r)
            dbg(tagn + "yhg", yhg[:, :, 0:T], yhg_r)
            dbg(tagn + "Sst", Sst.t[:], Sst_r)
            if nt is not None:
                nt["front_pre"]()
            if not tile.get("glu_done"):
                glu_now()

            def ev_bs(m, pm, pmr):
                tt("dve", ms[:, m, 0:T], pm[:, 0:T], gs5[:, m, 0:T], ALU.mult, [pmr, gs5_r[m]], [ms_r[m]])
            lin_fm("w_bs", 0, 1024, lambda k: y2_bf[:, k, 0:T], y2_r, T, ev_bs)

            def ev_bh(m, pm, pmr):
                gt = gtmp[m % 2]
                tt("dve", gt.t[:, 0:T], pm[:, 0:T], ghg[:, m, 0:T], ALU.mult, [pmr, ghg_r[m]], [gt.r])
                tt("dve", mg_bf[:, m, 0:T], ms[:, m, 0:T], gt.t[:, 0:T], ALU.add, [ms_r[m], gt.r], [mg_r[m]])
            lin_fm("w_bh", 0, 1024, lambda k: yhg[:, k, 0:T], yhg_r, T, ev_bh)

            def tm_mm_gen(wn, nkg, lhs_fn, lhs_res):
                for u in range(1024 // UC):
                    bks = [bank(hold=True) for _ in range(nblk)]
                    for kg in range(nkg):
                        sl, slr = load_unit(wn, kg, u)
                        for b in range(nblk):
                            pm, pmr = bks[b]
                            for k in range(8):
                                kk = kg * 8 + k
                                mm(pm[0:nrows, 0:UC], lhsT=lhs_fn(kk, b), rhs=sl[:, k, :], start=(kk == 0), stop=(kk == nkg * 8 - 1),
                                   r=[slr] + lhs_res, w=[pmr])
                            yield
                    for b in range(nblk):
                        pm, pmr = bks[b]
                        act(mo[0:nrows, b, u * UC:(u + 1) * UC], pm[0:nrows, 0:UC], AF.Copy, [pmr], mo_rl(b))
                        release(bks[b])
                    yield

            def tm_epilogue(gbc, res_fn, out_fn):
                for b in range(nblk):
                    rms_rows(mo[0:nrows, b, :], mo_rl(b), nrows)
                    stt("dve", mo[0:nrows, b, :], mo[0:nrows, b, :], rstd1.t[0:nrows, :], gbc.t[0:nrows, :], ALU.mult, ALU.mult,
                        mo_rl(b) + [rstd1.r, gbc.r], mo_rl(b))
                    res_ap, res_res = res_fn(b)
                    out_ap, out_res = out_fn(b)
                    tt("dve", out_ap, mo[0:nrows, b, :], res_ap, ALU.add, mo_rl(b) + res_res, out_res)

            def lin_tm_norm_res(wn, nkg, lhs_fn, lhs_res, gbc, res_fn, out_fn):
                for _ in tm_mm_gen(wn, nkg, lhs_fn, lhs_res):
                    pass
                tm_epilogue(gbc, res_fn, out_fn)

            xres = {}

            def res_x(b):
                xi = xin[ring["xin"] % NXIN]; ring["xin"] += 1
                S.dma("sp", xi.t[0:nrows, :], xd[tok0 + b * 128: tok0 + b * 128 + nrows, :], writes=[xi.r])
                return xi.t[0:nrows, :], [xi.r]

            lin_tm_norm_res("w_out", 1, lambda kk, b: mg_bf[:, kk, b * 128:b * 128 + nrows], mg_r, gpost_bc, res_x,
                            lambda b: (x1[0:nrows, b, :], x1_rl(b)))

            dbg(tagn + "mg", mg_bf[:, :, 0:T], mg_r)
            dbg(tagn + "x1", x1[:], q_r)
            for b in range(nblk):
                norm_transpose(x1[0:nrows, b, :], x1_rl(b), nrows, g2pre, b * 128, hT2, hT2_r)
            if nt is not None:
                if pre_s5_hook is not None:
                    pre_s5_hook()
                nt["front_rest"]()

            def ev_up(m, pm, pmr):
                gt = gtmp[m % 2]
                act(gt.t[:, 0:T], pm[:, 0:T], AF.Relu, [pmr], [gt.r])
                act(hid[:, m, 0:T], gt.t[:, 0:T], AF.Square, [gt.r], [hid_r[m]])

            obuf = {}

            def out_y(b):
                xi = xin[ring["xin"] % NXIN]; ring["xin"] += 1
                obuf[b] = xi
                return xi.t[0:nrows, :], [xi.r]

            def mlp_gen():
                yield from lin_fm_gen("w_up", 0, 4096, hrhs2, hT2_r, T, ev_up)
                yield from tm_mm_gen("w_down", 4, lambda kk, b: hid[:, kk, b * 128:b * 128 + nrows], hid_r)
            if nt is not None:
                interleave([(mlp_gen(), 2), (nt["s5"](), 3)])
            else:
                for _ in mlp_gen():
                    pass
            tm_epilogue(g2post_bc, lambda b: (x1[0:nrows, b, :], x1_rl(b)), out_y)
            for b in range(nblk):
                xi = obuf[b]
                S.dma("pool", yd[tok0 + b * 128: tok0 + b * 128 + nrows, :], xi.t[0:nrows, :], reads=[xi.r])

        tile["rest"] = rest
        return tile

    def s5_prompt_out(hc, od):
        pm, pmr = bank()
        tr(out=pm[0:16, 0:128], in_=hc.t[:, :, 0], identity=identf.t[:], r=[hc.r, identf.r], w=[pmr])
        cp("dve", scrA.t[0:16, 0:128], pm[0:16, 0:128], [pmr], scrA_q)
        S.dma("sp", od.rearrange("(ct p) -> ct p", p=128), scrA.t[0:16, 0:128], reads=scrA_q)

    def s5_sample_in(hc, sd, sX, sXr):
        S.dma("sp", sX.t[0:NSS, :], sd, writes=sXr)
        pm, pmr = bank()
        for ct in range(16):
            tr(out=pm[:, ct * NSS:(ct + 1) * NSS], in_=sX.t[0:NSS, ct * 128:(ct + 1) * 128], identity=identf.t[0:NSS, 0:NSS],
               r=sXr + [identf.r], w=[pmr])
        cp("dve", hc.t[:].rearrange("p a b -> p (a b)"), pm[:, 0:16 * NSS], [pmr], [hc.r])

    def s5_sample_out(hc, od, sX, sXr):
        for g4 in range(4):
            pm, pmr = bank()
            for j in range(4):
                ct = 4 * g4 + j
                tr(out=pm[0:NSS, j * 128:(j + 1) * 128], in_=hc.t[:, ct, :], identity=identf.t[:], r=[hc.r, identf.r], w=[pmr])
            cp("dve", sX.t[0:NSS, g4 * 512:(g4 + 1) * 512], pm[0:NSS, 0:512], [pmr], sXr)
        S.dma("sp", od, sX.t[0:NSS, :], reads=sXr)

    def swap_to_prompt():
        s5_sample_out(hc_re, o_sre, scrA, scrA_q)
        s5_sample_out(hc_im, o_sim, scrB, scrB_q)
        mset("dve", hc_re.t[:], 0.0, [hc_re.r])
        mset("dve", hc_im.t[:], 0.0, [hc_im.r])

    NT = SEQ // TP
    tiles = [make_tile("s", 0, TS)] + [make_tile("p", t * TP, TP) for t in range(NT)]
    s5_sample_in(hc_re, st_re, scrA, scrA_q)
    s5_sample_in(hc_im, st_im, scrB, scrB_q)
    tiles[0]["front_pre"]()
    tiles[0]["front_rest"]()
    tiles[0]["rest"](tiles[1], swap_to_prompt)
    for t in range(1, NT + 1):
        tiles[t]["rest"](tiles[t + 1] if t < NT else None, None)
    s5_prompt_out(hc_re, o_pre)
    s5_prompt_out(hc_im, o_pim)
    S.dma("sp", o_phg.rearrange("h k v -> k h v"), Sst.t[:], reads=Sst_r)

    S.finalize("sp")
    with contextlib.ExitStack() as es2:
        sems_eng = {e: es2.enter_context(nc.semaphore("s_" + e)) for e in ENGS}
        sems_dma = [es2.enter_context(nc.semaphore("d%d" % i)) for i in range(Sched.NDMA)]
        block = es2.enter_context(nc.Block())

        def mk(ename):
            def f(eng):
                S.emit_engine(ename, eng, sems_eng, sems_dma)
            return f
        block.tensor(mk("pe"))
        block.scalar(mk("act"))
        block.vector(mk("dve"))
        block.gpsimd(mk("pool"))
        block.sync(mk("sp"))
    es.close()
    return nc


_NC_CACHE = {}


def kernel(x_prompt, x_sample, state_s5_re, state_s5_im, state_hg,
           norm_mix_pre, norm_mix_post, norm_mlp_pre, norm_mlp_post, w_in, b_in,
           s5_a_re, s5_a_im, s5_log_dt, s5_b_re, s5_b_im, s5_c_re, s5_c_im, s5_d, s5_w_glu, s5_b_glu,
           hg_lb_logits, hg_norm, w_br_s5, w_br_hg, w_out, w_up, w_down):
    f = lambda a: np.ascontiguousarray(np.asarray(a, dtype=np.float32))
    if "nc" not in _NC_CACHE:
        _NC_CACHE["nc"] = build_nc()
    nc = _NC_CACHE["nc"]
    shared = {
        "g_pre": f(norm_mix_pre).reshape(1024), "g_post": f(norm_mix_post).reshape(1024),
        "g2_pre": f(norm_mlp_pre).reshape(1024), "g2_post": f(norm_mlp_post).reshape(1024),
        "w_in": f(w_in).reshape(1024, 6656), "b_in": f(b_in).reshape(6656),
        "a_re": f(s5_a_re).reshape(2048), "a_im": f(s5_a_im).reshape(2048), "ldt": f(s5_log_dt).reshape(32),
        "b_re": f(s5_b_re).reshape(2048, 16), "b_im": f(s5_b_im).reshape(2048, 16),
        "c_re": f(s5_c_re).reshape(32, 16, 64), "c_im": f(s5_c_im).reshape(32, 16, 64),
        "s5d": f(s5_d).reshape(512), "w_glu": f(s5_w_glu).reshape(512, 512), "b_glu": f(s5_b_glu).reshape(512),
        "lbl": f(hg_lb_logits).reshape(2, 1024), "hgn": f(hg_norm).reshape(128),
        "w_bs": f(w_br_s5).reshape(512, 1024), "w_bh": f(w_br_hg).reshape(1024, 1024),
        "w_out": f(w_out).reshape(1024, 1024), "w_up": f(w_up).reshape(1024, 4096), "w_down": f(w_down).reshape(4096, 1024),
    }
    xpn = f(x_prompt); xsn = f(x_sample)
    sre = f(state_s5_re).reshape(128, 2048); sim = f(state_s5_im).reshape(128, 2048)
    shg = f(state_hg).reshape(128, 8, 128, 128)
    in_maps = []
    for c in range(NCORES):
        d = dict(shared)
        d["xp"] = xpn[c]
        d["xs"] = np.ascontiguousarray(xsn[c * NSS:(c + 1) * NSS].reshape(TS, 1024))
        d["st_re"] = np.ascontiguousarray(sre[c * NSS:(c + 1) * NSS])
        d["st_im"] = np.ascontiguousarray(sim[c * NSS:(c + 1) * NSS])
        d["st_hg"] = np.ascontiguousarray(shg[c * NSS:(c + 1) * NSS])
        in_maps.append(d)
    res = run_bass_kernel_spmd(nc, in_maps, core_ids=list(range(NCORES)))
    R = res.results
    y_prompt = np.stack([R[c]["yp"] for c in range(NCORES)], axis=0).astype(np.float32)
    y_sample = np.concatenate([R[c]["ys"].reshape(NSS, 4, 1024) for c in range(NCORES)], axis=0).astype(np.float32)
    p_re = np.stack([R[c]["o_pre"].reshape(32, 64) for c in range(NCORES)], axis=0)[None].astype(np.float32)
    p_im = np.stack([R[c]["o_pim"].reshape(32, 64) for c in range(NCORES)], axis=0)[None].astype(np.float32)
    p_hg = np.stack([R[c]["o_phg"] for c in range(NCORES)], axis=0)[None].astype(np.float32)
    s_re = np.concatenate([R[c]["o_sre"].reshape(NSS, 32, 64) for c in range(NCORES)], axis=0)[None].astype(np.float32)
    s_im = np.concatenate([R[c]["o_sim"].reshape(NSS, 32, 64) for c in range(NCORES)], axis=0)[None].astype(np.float32)
    s_hg = np.concatenate([R[c]["o_shg"] for c in range(NCORES)], axis=0)[None].astype(np.float32)
    return (y_prompt, y_sample, p_re, p_im, p_hg, s_re, s_im, s_hg)
```

```python
import contextlib
import math
import numpy as np
import concourse.bass as bass
import concourse.mybir as mybir
from concourse.bass_utils import run_bass_kernel_spmd

F32 = mybir.dt.float32
BF16 = mybir.dt.bfloat16
I32 = mybir.dt.int32
AF = mybir.ActivationFunctionType
ALU = mybir.AluOpType

ENGS = ("pe", "act", "dve", "pool", "sp")
TP = 256
NCORES = 8
SEQ = 2048
NSS = 16
TS = 64
TWO_PI = 2.0 * math.pi


class Res:
    __slots__ = ("name", "writers", "readers")

    def __init__(self, name=""):
        self.name = name
        self.writers = {}
        self.readers = {}


class Op:
    __slots__ = ("id", "eng", "emit", "deps", "is_dma", "sig", "signals", "waits")

    def __init__(self, id, eng, emit, deps, is_dma):
        self.id = id
        self.eng = eng
        self.emit = emit
        self.deps = deps
        self.is_dma = is_dma
        self.sig = None
        self.signals = is_dma
        self.waits = []


class Sched:
    NDMA = 56

    def __init__(self):
        self.ops = []

    def op(self, eng, emit, reads=(), writes=(), is_dma=False):
        oid = len(self.ops)
        deps = set()
        for r in reads:
            deps.update(r.writers.values())
        for w in writes:
            deps.update(w.writers.values())
            deps.update(w.readers.values())
        o = Op(oid, eng, emit, deps, is_dma)
        self.ops.append(o)
        k = ("dma", oid) if is_dma else eng
        for r in reads:
            r.readers[k] = oid
        for w in writes:
            w.writers = {k: oid}
            w.readers = {}
        return oid

    def dma(self, q, out, in_, reads=(), writes=(), **kw):
        return self.op(q, lambda e: e.dma_start(out=out, in_=in_, **kw), reads, writes, is_dma=True)

    def finalize(self, final_eng="sp"):
        ops = self.ops
        for o in ops:
            nd = set()
            for d in o.deps:
                do = ops[d]
                if (not do.is_dma) and (not o.is_dma) and do.eng == o.eng and o.eng == "pe":
                    continue
                nd.add(d)
            o.deps = nd
        slot_last = [None] * self.NDMA
        slot_cnt = [0] * self.NDMA
        pools = {"pool": list(range(0, 40)), "sp": list(range(40, self.NDMA)), "act": list(range(40, self.NDMA))}
        rrq = {"pool": 0, "sp": 0, "act": 0}
        for o in ops:
            if o.is_dma:
                pl = pools[o.eng]
                s = pl[rrq[o.eng] % len(pl)]
                rrq[o.eng] += 1
                if slot_last[s] is not None:
                    o.deps.add(slot_last[s])
                slot_last[s] = o.id
                slot_cnt[s] += 16
                o.sig = ("dma", s, slot_cnt[s])
        for o in ops:
            for d in o.deps:
                ops[d].signals = True
        cnt = {e: 0 for e in ENGS}
        for o in ops:
            if (not o.is_dma) and o.signals:
                cnt[o.eng] += 1
                o.sig = ("eng", o.eng, cnt[o.eng])
        seen = {e: {} for e in ENGS}
        for o in ops:
            need = {}
            for d in o.deps:
                kind, key, val = ops[d].sig
                k = (kind, key)
                if val > need.get(k, 0):
                    need[k] = val
            sn = seen[o.eng]
            for k, val in need.items():
                if sn.get(k, 0) >= val:
                    continue
                sn[k] = val
                o.waits.append((k, val))
        self.final_waits = []
        sn = seen[final_eng]
        last = {}
        for o in ops:
            if o.sig is not None:
                kind, key, val = o.sig
                last[(kind, key)] = max(last.get((kind, key), 0), val)
        for k, val in last.items():
            if sn.get(k, 0) < val:
                self.final_waits.append((k, val))
        self.final_eng = final_eng

    def emit_engine(self, e, eng, sems_eng, sems_dma):
        def semof(k):
            kind, key = k
            return sems_dma[key] if kind == "dma" else sems_eng[key]
        for o in self.ops:
            if o.eng != e:
                continue
            for (k, val) in o.waits:
                eng.wait_ge(semof(k), val)
            ins = o.emit(eng)
            if o.signals:
                kind, key, val = o.sig
                if kind == "dma":
                    ins.then_inc(sems_dma[key], 16)
                else:
                    ins.then_inc(sems_eng[key], 1)
        if e == self.final_eng:
            for (k, val) in self.final_waits:
                eng.wait_ge(semof(k), val)


DBG = None


def build_nc():
    nc = bass.Bass("TRN2", target_bir_lowering=False)
    S = Sched()
    es = contextlib.ExitStack()

    def din(name, shape):
        return nc.dram_tensor(name, list(shape), F32, kind="ExternalInput").ap()

    def dout(name, shape):
        return nc.dram_tensor(name, list(shape), F32, kind="ExternalOutput").ap()

    xp = din("xp", [SEQ, 1024])
    xs = din("xs", [TS, 1024])
    st_re = din("st_re", [NSS, 2048])
    st_im = din("st_im", [NSS, 2048])
    st_hg = din("st_hg", [NSS, 8, 128, 128])
    g_pre_d = din("g_pre", [1024])
    g_post_d = din("g_post", [1024])
    g2_pre_d = din("g2_pre", [1024])
    g2_post_d = din("g2_post", [1024])
    w_in = din("w_in", [1024, 6656])
    b_in = din("b_in", [6656])
    a_re_d = din("a_re", [2048])
    a_im_d = din("a_im", [2048])
    ldt_d = din("ldt", [32])
    b_re_d = din("b_re", [2048, 16])
    b_im_d = din("b_im", [2048, 16])
    c_re_d = din("c_re", [32, 16, 64])
    c_im_d = din("c_im", [32, 16, 64])
    s5d_d = din("s5d", [512])
    w_glu = din("w_glu", [512, 512])
    b_glu_d = din("b_glu", [512])
    lbl_d = din("lbl", [2, 1024])
    hgn_d = din("hgn", [128])
    w_bs = din("w_bs", [512, 1024])
    w_bh = din("w_bh", [1024, 1024])
    w_out = din("w_out", [1024, 1024])
    w_up = din("w_up", [1024, 4096])
    w_down = din("w_down", [4096, 1024])

    yp = dout("yp", [SEQ, 1024])
    ys = dout("ys", [TS, 1024])
    o_pre = dout("o_pre", [2048])
    o_pim = dout("o_pim", [2048])
    o_phg = dout("o_phg", [8, 128, 128])
    o_sre = dout("o_sre", [NSS, 2048])
    o_sim = dout("o_sim", [NSS, 2048])
    o_shg = dout("o_shg", [NSS, 8, 128, 128])

    def dbg(name, ap, res):
        if DBG is None or name not in DBG:
            return
        d = nc.dram_tensor("dbg_" + name, list(ap.shape), ap.dtype, kind="ExternalOutput").ap()
        S.dma("sp", d, ap, reads=res)

    def sb(name, shape, dt=F32):
        return es.enter_context(nc.sbuf_tensor("sb_" + name, list(shape), dt))

    class TL:
        def __init__(self, name, shape, dt=F32):
            self.t = sb(name, shape, dt)
            self.r = Res(name)

    def pe(f, r=(), w=()):
        S.op("pe", f, r, w)

    def mm(out, lhsT, rhs, start, stop, r=(), w=()):
        S.op("pe", lambda e: e.matmul(out, lhsT=lhsT, rhs=rhs, start=start, stop=stop), r, w)

    def tr(out, in_, identity, r=(), w=()):
        S.op("pe", lambda e: e.transpose(out=out, in_=in_, identity=identity), r, w)

    def scan(out, d0, d1, r=(), w=()):
        S.op("dve", lambda e: e.tensor_tensor_scan(out=out, data0=d0, data1=d1, initial=0.0, op0=ALU.mult, op1=ALU.add), r, w)

    def scan2(out, d0, d1, init, r=(), w=()):
        S.op("dve", lambda e: e.tensor_tensor_scan(out=out, data0=d0, data1=d1, initial=init, op0=ALU.mult, op1=ALU.add), r, w)

    def act(out, in_, func, r=(), w=(), bias=None, scale=1.0, accum=None):
        def f(e):
            kw = dict(out=out, in_=in_, func=func, scale=scale)
            if bias is not None:
                kw["bias"] = bias
            if accum is not None:
                kw["accum_out"] = accum
            return e.activation(**kw)
        S.op("act", f, r, w)

    def tt(eng, out, in0, in1, op, r=(), w=()):
        S.op(eng, lambda e: e.tensor_tensor(out=out, in0=in0, in1=in1, op=op), r, w)

    def ts(eng, out, in0, s1, s2, op0, op1=None, r=(), w=()):
        if op1 is None:
            S.op(eng, lambda e: e.tensor_scalar(out=out, in0=in0, scalar1=s1, scalar2=None, op0=op0), r, w)
        else:
            S.op(eng, lambda e: e.tensor_scalar(out=out, in0=in0, scalar1=s1, scalar2=s2, op0=op0, op1=op1), r, w)

    def stt(eng, out, in0, scalar, in1, op0, op1, r=(), w=()):
        S.op(eng, lambda e: e.scalar_tensor_tensor(out=out, in0=in0, scalar=scalar, in1=in1, op0=op0, op1=op1), r, w)

    def cp(eng, out, in_, r=(), w=()):
        S.op(eng, lambda e: e.tensor_copy(out=out, in_=in_), r, w)

    def mset(eng, ap, val, w=()):
        S.op(eng, lambda e: e.memset(ap, val), (), w)

    psf = [es.enter_context(nc.psum_tensor("psf%d" % i, [128, 512], F32)) for i in range(6)]
    psf_r = [Res("psf%d" % i) for i in range(6)]
    psb = [es.enter_context(nc.psum_tensor("psb%d" % i, [128, 1024], BF16)) for i in range(2)]
    psb_r = [Res("psb%d" % i) for i in range(2)]
    ring = {"f": 0, "b": 0, "slab": 0, "xin": 0}

    held = set()

    def bank(hold=False):
        while True:
            i = ring["f"] % 5
            ring["f"] += 1
            if i not in held:
                break
        if hold:
            held.add(i)
        return psf[i], psf_r[i]

    def release(bk):
        for i in range(5):
            if psf[i] is bk[0]:
                held.discard(i)

    def bankb():
        i = ring["b"] % 2
        ring["b"] += 1
        return psb[i], psb_r[i]

    NSLAB = 4
    UC = 256
    slabs = [sb("slab%d" % i, [128, 8, UC], BF16) for i in range(NSLAB)]
    slabs_r = [Res("slab%d" % i) for i in range(NSLAB)]

    WSPEC = {"w_in": (w_in, 1024, 6656, 8), "w_glu": (w_glu, 512, 512, 4), "w_bs": (w_bs, 512, 1024, 4),
             "w_bh": (w_bh, 1024, 1024, 8), "w_out": (w_out, 1024, 1024, 8), "w_up": (w_up, 1024, 4096, 8),
             "w_down": (w_down, 4096, 1024, 8)}
    wscr = {}
    wscr_r = {}

    def convert_weight(wn, units=None):
        w, K, N, nk = WSPEC[wn]
        KG = K // (nk * 128)
        NU2 = N // (2 * UC)
        if wn not in wscr:
            wscr[wn] = nc.dram_tensor("scr_" + wn, [KG, NU2, 128, nk, 2 * UC], BF16, kind="Internal").ap()
        ul = list(range(NU2)) if units is None else list(units)
        for kg in range(KG):
            for u2 in ul:
                r = Res("scr_%s_%d_%d" % (wn, kg, u2))
                wscr_r[(wn, kg, u2)] = r
                src = w[kg * nk * 128:(kg + 1) * nk * 128, u2 * 2 * UC:(u2 + 1) * 2 * UC].rearrange("(k p) n -> p k n", p=128)
                S.dma("pool", wscr[wn][kg, u2], src, writes=[r])

    def load_unit(wn, kg, u):
        nk = WSPEC[wn][3]
        i = ring["slab"] % NSLAB
        ring["slab"] += 1
        S.dma("sp", slabs[i][:, 0:nk, :], wscr[wn][kg, u // 2][:, :, (u % 2) * UC:(u % 2 + 1) * UC],
              reads=[wscr_r[(wn, kg, u // 2)]], writes=[slabs_r[i]])
        return slabs[i], slabs_r[i]

    class VW:
        def __init__(self, ap2d, r):
            self.t = ap2d
            self.r = r

    class _T:
        def __init__(self, ap):
            self.ap = ap

        def __getitem__(self, key):
            return self.ap[key]

    TM = TP
    NB = TM // 128
    ge = sb("ge", [128, 2, 4 * TM]); ge_r = [Res("ge0"), Res("ge1")]
    NXIN = 3
    xin = [TL("xin%d" % i, [128, 1024]) for i in range(NXIN)]
    xsbA = [TL("xsbA%d" % i, [128, 1024], BF16) for i in range(2)]
    hT = sb("hT", [128, 8, TM], BF16); hT_r = [Res("hT%d" % k) for k in range(8)]
    hT2 = sb("hT2", [128, 8, TM], BF16); hT2_r = [Res("hT2_%d" % k) for k in range(8)]

    c_eps = TL("c_eps", [128, 1]); mset("pool", c_eps.t[:], 1e-6, [c_eps.r])
    c_negpi = TL("c_negpi", [128, 1]); mset("pool", c_negpi.t[:], -math.pi, [c_negpi.r])
    onesf = TL("onesf", [128, 128]); mset("pool", onesf.t[:], 1.0, [onesf.r])
    ones_bf = TL("ones_bf", [128, 128], BF16); mset("pool", ones_bf.t[:], 1.0, [ones_bf.r])
    identf = TL("identf", [128, 128])
    S.op("pool", lambda e: e.affine_select(out=identf.t[:], in_=onesf.t[:], pattern=[[1, 128]], compare_op=ALU.is_equal,
                                           fill=0.0, base=0, channel_multiplier=-1), [onesf.r], [identf.r])
    ident = TL("ident", [128, 128], BF16)
    cp("dve", ident.t[:], identf.t[:], [identf.r], [ident.r])
    maskP = TL("maskP", [128, 128])
    S.op("pool", lambda e: e.affine_select(out=maskP.t[:], in_=onesf.t[:], pattern=[[1, 128]], compare_op=ALU.is_ge,
                                           fill=0.0, base=0, channel_multiplier=-1), [onesf.r], [maskP.r])
    maskS = TL("maskS", [64, 64])
    cp("pool", maskS.t[:], maskP.t[0:64, 0:64], [maskP.r], [maskS.r])
    for sq in range(NSS):
        S.op("pool", lambda e, sq=sq: e.affine_select(out=maskS.t[:, 4 * sq:4 * sq + 4], in_=maskS.t[:, 4 * sq:4 * sq + 4],
                                                      pattern=[[0, 4]], compare_op=ALU.is_ge, fill=0.0, base=-4 * sq,
                                                      channel_multiplier=1), [maskS.r], [maskS.r])
    seqm = TL("seqm", [64, NSS])
    S.op("pool", lambda e: e.affine_select(out=seqm.t[:], in_=onesf.t[0:64, 0:NSS], pattern=[[-4, NSS]], compare_op=ALU.is_ge,
                                           fill=0.0, base=0, channel_multiplier=1), [onesf.r], [seqm.r])
    S.op("pool", lambda e: e.affine_select(out=seqm.t[:], in_=seqm.t[:], pattern=[[4, NSS]], compare_op=ALU.is_ge,
                                           fill=0.0, base=3, channel_multiplier=-1), [seqm.r], [seqm.r])

    cmask = TL("cmask", [128, NSS, TS], BF16)
    mset("pool", cmask.t[:], 0.0, [cmask.r])
    for sq in range(NSS):
        mset("pool", cmask.t[:, sq, 4 * sq:4 * sq + 4], 1.0, [cmask.r])

    mask2 = TL("mask2", [128, 8])
    S.op("pool", lambda e: e.affine_select(out=mask2.t[:], in_=onesf.t[:, 0:8], pattern=[[-16, 8]], compare_op=ALU.is_ge,
                                           fill=0.0, base=0, channel_multiplier=1), [onesf.r], [mask2.r])
    S.op("pool", lambda e: e.affine_select(out=mask2.t[:], in_=mask2.t[:], pattern=[[16, 8]], compare_op=ALU.is_ge,
                                           fill=0.0, base=15, channel_multiplier=-1), [mask2.r], [mask2.r])
    iot_i = TL("iot_i", [128, 128], I32)
    S.op("pool", lambda e: e.iota(iot_i.t[:], pattern=[[1, 128]], base=1, channel_multiplier=0), [], [iot_i.r])
    convert_weight("w_in", [0, 5, 6, 1, 2, 3, 4, 7, 8, 9, 10, 11, 12])
    for wn in ("w_glu", "w_bs", "w_bh", "w_out", "w_up", "w_down"):
        convert_weight(wn)

    stg = [TL("stg%d" % i, [64, 128]) for i in range(2)]
    stg_i = [0]

    def load_cols(name, vec, n):
        t = TL(name, [128, n])
        st = stg[stg_i[0] % 2]; stg_i[0] += 1
        S.dma("sp", st.t[0:n, :], vec.rearrange("(c p) -> c p", p=128), writes=[st.r])
        pm, pmr = bank()
        tr(out=pm[:, 0:n], in_=st.t[0:n, :], identity=identf.t[0:n, 0:n], r=[st.r, identf.r], w=[pmr])
        cp("dve", t.t[:], pm[:, 0:n], [pmr], [t.r])
        return t

    def load_bc(name, vec, n):
        t = TL(name, [128, n])
        src = bass.AP(vec.tensor, vec.offset, [[0, 128], [1, n]])
        S.dma("sp", t.t[:], src, writes=[t.r])
        return t

    bcol = load_cols("bcol", b_in, 52)
    gpre = load_cols("gpre", g_pre_d, 8)
    g2pre = load_cols("g2pre", g2_pre_d, 8)
    gpost_bc = load_bc("gpost_bc", g_post_d, 1024)
    g2post_bc = load_bc("g2post_bc", g2_post_d, 1024)
    bi_bc = load_bc("bi_bc", b_in[2560:3584], 1024)
    s5d = load_cols("s5d", s5d_d, 4)
    bglu = load_cols("bglu", b_glu_d, 4)
    hgn = load_cols("hgn", hgn_d, 1)
    l0 = load_cols("l0", lbl_d[0, :], 8)
    l1 = load_cols("l1", lbl_d[1, :], 8)
    lbd = TL("lbd", [128, 8])
    tt("dve", lbd.t[:], l0.t[:], l1.t[:], ALU.subtract, [l0.r, l1.r], [lbd.r])
    lb = TL("lb", [128, 8]); act(lb.t[:], lbd.t[:], AF.Sigmoid, [lbd.r], [lb.r])
    oml = TL("oml", [128, 8]); act(oml.t[:], lbd.t[:], AF.Sigmoid, [lbd.r], [oml.r], scale=-1.0)
    noml = TL("noml", [128, 8]); ts("dve", noml.t[:], oml.t[:], -1.0, None, ALU.mult, None, [oml.r], [noml.r])

    are = load_cols("are", a_re_d, 16)
    aim = load_cols("aim", a_im_d, 16)
    ldt = TL("ldt", [128, 16])
    ldt_bc = load_bc("ldt_bc", ldt_d, 32)
    for two in range(2):
        cp("dve", ldt.t[64 * two:64 * two + 64, :], ldt_bc.t[64 * two:64 * two + 64, two::2], [ldt_bc.r], [ldt.r])
    dtt = TL("dtt", [128, 16]); act(dtt.t[:], ldt.t[:], AF.Exp, [ldt.r], [dtt.r])
    dre = TL("dre", [128, 16]); tt("dve", dre.t[:], dtt.t[:], are.t[:], ALU.mult, [dtt.r, are.r], [dre.r])
    th = TL("th", [128, 16]); tt("dve", th.t[:], dtt.t[:], aim.t[:], ALU.mult, [dtt.r, aim.r], [th.r])
    rmag = TL("rmag", [128, 16]); act(rmag.t[:], dre.t[:], AF.Exp, [dre.r], [rmag.r])

    FQ = 512
    scrA = TL("scrA", [128, 2048]); scrB = TL("scrB", [128, 2048])
    rs_a = VW(_T(ge[:, 0, 0:512]), ge_r[0]); rs_b = VW(_T(ge[:, 0, 512:1024]), ge_r[0])
    rs_i = VW(_T(ge[:, 1, 0:512].bitcast(I32)), ge_r[1])

    def range_sin(out_ap, ang_ap, shift, n, rr, ww):
        ta, tb, ti = rs_a, rs_b, rs_i
        ts("dve", ta.t[:, 0:n], ang_ap, shift + math.pi, None, ALU.add, None, rr, [ta.r])
        ts("dve", tb.t[:, 0:n], ta.t[:, 0:n], 1.0 / TWO_PI, None, ALU.mult, None, [ta.r], [tb.r])
        cp("dve", ti.t[:, 0:n], tb.t[:, 0:n], [tb.r], [ti.r])
        cp("dve", tb.t[:, 0:n], ti.t[:, 0:n], [ti.r], [tb.r])
        stt("dve", ta.t[:, 0:n], tb.t[:, 0:n], -TWO_PI, ta.t[:, 0:n], ALU.mult, ALU.add, [tb.r, ta.r], [ta.r])
        ts("dve", tb.t[:, 0:n], ta.t[:, 0:n], 0.0, TWO_PI, ALU.is_lt, ALU.mult, [ta.r], [tb.r])
        tt("dve", ta.t[:, 0:n], ta.t[:, 0:n], tb.t[:, 0:n], ALU.add, [ta.r, tb.r], [ta.r])
        ts("dve", tb.t[:, 0:n], ta.t[:, 0:n], TWO_PI, -TWO_PI, ALU.is_ge, ALU.mult, [ta.r], [tb.r])
        tt("dve", ta.t[:, 0:n], ta.t[:, 0:n], tb.t[:, 0:n], ALU.add, [ta.r, tb.r], [ta.r])
        act(out_ap, ta.t[:, 0:n], AF.Sin, [ta.r, c_negpi.r], ww, bias=c_negpi.t[:], scale=1.0)

    cth = TL("cth", [128, 16]); sth = TL("sth", [128, 16])
    range_sin(cth.t[:], th.t[:], math.pi / 2, 16, [th.r], [cth.r])
    range_sin(sth.t[:], th.t[:], 0.0, 16, [th.r], [sth.r])
    abre = TL("abre", [128, 16]); tt("dve", abre.t[:], rmag.t[:], cth.t[:], ALU.mult, [rmag.r, cth.r], [abre.r])
    abim = TL("abim", [128, 16]); tt("dve", abim.t[:], rmag.t[:], sth.t[:], ALU.mult, [rmag.r, sth.r], [abim.r])
    nr = TL("nr", [128, 16]); ts("dve", nr.t[:], abre.t[:], -1.0, None, ALU.add, None, [abre.r], [nr.r])
    den = TL("den", [128, 16]); t16 = TL("t16", [128, 16]); t16b = TL("t16b", [128, 16])
    tt("dve", den.t[:], are.t[:], are.t[:], ALU.mult, [are.r], [den.r])
    tt("dve", t16.t[:], aim.t[:], aim.t[:], ALU.mult, [aim.r], [t16.r])
    tt("dve", den.t[:], den.t[:], t16.t[:], ALU.add, [den.r, t16.r], [den.r])
    rden = TL("rden", [128, 16])
    S.op("dve", lambda e: e.reciprocal(out=rden.t[:], in_=den.t[:]), [den.r], [rden.r])
    cre = TL("cre", [128, 16]); cim = TL("cim", [128, 16])
    tt("dve", t16.t[:], nr.t[:], are.t[:], ALU.mult, [nr.r, are.r], [t16.r])
    tt("dve", t16b.t[:], abim.t[:], aim.t[:], ALU.mult, [abim.r, aim.r], [t16b.r])
    tt("dve", t16.t[:], t16.t[:], t16b.t[:], ALU.add, [t16.r, t16b.r], [t16.r])
    tt("dve", cre.t[:], t16.t[:], rden.t[:], ALU.mult, [t16.r, rden.r], [cre.r])
    tt("dve", t16.t[:], abim.t[:], are.t[:], ALU.mult, [abim.r, are.r], [t16.r])
    tt("dve", t16b.t[:], nr.t[:], aim.t[:], ALU.mult, [nr.r, aim.r], [t16b.r])
    tt("dve", t16.t[:], t16.t[:], t16b.t[:], ALU.subtract, [t16.r, t16b.r], [t16.r])
    tt("dve", cim.t[:], t16.t[:], rden.t[:], ALU.mult, [t16.r, rden.r], [cim.r])

    def v3(ap):
        return _T(ap.rearrange("p (a b) -> p a b", a=16))
    Bre = VW(v3(xin[0].t[:, 0:256]), xin[0].r); Bim = VW(v3(xin[0].t[:, 256:512]), xin[0].r)
    S.dma("sp", Bre.t[:], b_re_d.rearrange("(ct p) c -> p ct c", p=128), writes=[Bre.r])
    S.dma("sp", Bim.t[:], b_im_d.rearrange("(ct p) c -> p ct c", p=128), writes=[Bim.r])
    Bbre = VW(v3(xin[0].t[:, 512:768]), xin[0].r); Bbim = VW(v3(xin[0].t[:, 768:1024]), xin[0].r)
    tB = VW(v3(xin[1].t[:, 0:256]), xin[1].r)
    creb = cre.t[:].unsqueeze(2).to_broadcast([128, 16, 16])
    cimb = cim.t[:].unsqueeze(2).to_broadcast([128, 16, 16])
    tt("dve", Bbre.t[:], Bre.t[:], creb, ALU.mult, [Bre.r, cre.r], [Bbre.r])
    tt("dve", tB.t[:], Bim.t[:], cimb, ALU.mult, [Bim.r, cim.r], [tB.r])
    tt("dve", Bbre.t[:], Bbre.t[:], tB.t[:], ALU.subtract, [Bbre.r, tB.r], [Bbre.r])
    tt("dve", Bbim.t[:], Bim.t[:], creb, ALU.mult, [Bim.r, cre.r], [Bbim.r])
    tt("dve", tB.t[:], Bre.t[:], cimb, ALU.mult, [Bre.r, cim.r], [tB.r])
    tt("dve", Bbim.t[:], Bbim.t[:], tB.t[:], ALU.add, [Bbim.r, tB.r], [Bbim.r])

    lhsT_B = TL("lhsT_B", [128, 16, 2, 128], BF16)
    lhsT_C = TL("lhsT_C", [128, 16, 2, 128], BF16)
    padf = scrA
    padb = VW(_T(hT[:].rearrange("p a b -> p (a b)").rearrange("p (a b) -> p a b", a=16)), Res("padb"))
    padb_extra = hT_r
    padf3 = padf.t[:].rearrange("p (a b) -> p a b", a=16)
    for ri, Bb in enumerate((Bbre, Bbim)):
        mset("dve", padf.t[:], 0.0, [padf.r])
        for two in range(2):
            for m in range(4):
                col = (2 * m + two) * 16
                cp("dve", padf3[64 * two:64 * two + 64, m::4, col:col + 16], Bb.t[64 * two:64 * two + 64, m::4, :],
                   [Bb.r, padf.r], [padf.r])
        cp("dve", padb.t[:], padf3, [padf.r], [padb.r])
        for half in range(2):
            pb, pbr = bankb()
            for j in range(8):
                ct = half * 8 + j
                tr(out=pb[:, j * 128:(j + 1) * 128], in_=padb.t[:, ct, :], identity=ident.t[:], r=[padb.r, ident.r], w=[pbr])
            cp("dve", lhsT_B.t[:, half * 8:half * 8 + 8, ri, :], pb[:].rearrange("p (a b) -> p a b", a=8), [pbr], [lhsT_B.r])
    for ri, cd in enumerate((c_re_d, c_im_d)):
        Cn = xin[1].t[:, 256 + 256 * ri:512 + 256 * ri].rearrange("p (u n) -> p u n", u=4)
        S.dma("sp", Cn, cd.rearrange("g c n -> (g c) n").rearrange("(u q) n -> q u n", q=128), writes=[xin[1].r])
        for uc in range(4):
            tt("dve", padf3[:, 4 * uc:4 * uc + 4, :].rearrange("p m (t n) -> p m t n", t=2),
               Cn[:, uc, :].unsqueeze(1).unsqueeze(1).to_broadcast([128, 4, 2, 64]),
               mask2.t[:].rearrange("p (m t) -> p m t", t=2).unsqueeze(3).to_broadcast([128, 4, 2, 64]), ALU.mult,
               [xin[1].r, mask2.r], [padf.r])
        if ri == 0:
            cp("dve", padb.t[:], padf3, [padf.r], [padb.r])
        else:
            ts("dve", padb.t[:], padf3, -1.0, None, ALU.mult, None, [padf.r], [padb.r])
        for half in range(2):
            pb, pbr = bankb()
            for j in range(8):
                ct = half * 8 + j
                tr(out=pb[:, j * 128:(j + 1) * 128], in_=padb.t[:, ct, :], identity=ident.t[:], r=[padb.r, ident.r], w=[pbr])
            cp("dve", lhsT_C.t[:, half * 8:half * 8 + 8, ri, :], pb[:].rearrange("p (a b) -> p a b", a=8), [pbr], [lhsT_C.r])

    iot_f = TL("iot_f", [128, 128])
    cp("dve", iot_f.t[:], iot_i.t[:], [iot_i.r], [iot_f.r])
    cosT = TL("cosT", [128, 16, 128]); sinT = TL("sinT", [128, 16, 128])
    ang = scrB
    tt("dve", ang.t[:].rearrange("p (a b) -> p a b", a=16), th.t[:].unsqueeze(2).to_broadcast([128, 16, 128]),
       iot_f.t[:].unsqueeze(1).to_broadcast([128, 16, 128]), ALU.mult, [th.r, iot_f.r], [ang.r])
    cosTf = cosT.t[:].rearrange("p a b -> p (a b)"); sinTf = sinT.t[:].rearrange("p a b -> p (a b)")
    for pc in range(4):
        range_sin(cosTf[:, pc * 512:(pc + 1) * 512], ang.t[:, pc * 512:(pc + 1) * 512], math.pi / 2, 512, [ang.r], [cosT.r])
        range_sin(sinTf[:, pc * 512:(pc + 1) * 512], ang.t[:, pc * 512:(pc + 1) * 512], 0.0, 512, [ang.r], [sinT.r])
    d0s = TL("d0s", [128, 16, 64]); d0p = d0s
    cp("dve", d0s.t[:], rmag.t[:].unsqueeze(2).to_broadcast([128, 16, 64]), [rmag.r], [d0s.r])
    mset("dve", d0s.t[:].rearrange("p a (s t) -> p a s t", t=4)[:, :, :, 0:1], 0.0, [d0s.r])
    d0hp = TL("d0hp", [128, TP]); mset("dve", d0hp.t[:], 1.0, [d0hp.r])
    mset("dve", d0hp.t[:].rearrange("p (c t) -> p c t", t=128)[:, :, 0:1], 0.0, [d0hp.r])
    d0hs = TL("d0hs", [128, TS]); mset("dve", d0hs.t[:], 1.0, [d0hs.r])
    mset("dve", d0hs.t[:].rearrange("p (c t) -> p c t", t=4)[:, :, 0:1], 0.0, [d0hs.r])

    dbg("rmag", rmag.t[:], [rmag.r]); dbg("cth", cth.t[:], [cth.r]); dbg("sth", sth.t[:], [sth.r])
    dbg("cre", cre.t[:], [cre.r]); dbg("cim", cim.t[:], [cim.r]); dbg("Bbre", Bbre.t[:], [Bbre.r]); dbg("Bbim", Bbim.t[:], [Bbim.r])
    dbg("lhsT_B", lhsT_B.t[:], [lhsT_B.r]); dbg("lhsT_C", lhsT_C.t[:], [lhsT_C.r])
    dbg("cosT", cosT.t[:], [cosT.r]); dbg("sinT", sinT.t[:], [sinT.r])
    dbg("maskS", maskS.t[:], [maskS.r]); dbg("seqm", seqm.t[:], [seqm.r]); dbg("lb", lb.t[:], [lb.r])
    hc_re = TL("hc_re", [128, 16, NSS]); hc_im = TL("hc_im", [128, 16, NSS])
    mset("dve", hc_re.t[:], 0.0, [hc_re.r]); mset("dve", hc_im.t[:], 0.0, [hc_im.r])
    Sst = TL("Sst", [128, 8, 128])
    Sst_r = [Res("Sst%d" % h) for h in range(8)]
    mset("dve", Sst.t[:], 0.0, Sst_r)

    xsb = TL("xsb", [128, 1024], BF16)
    junk = xsb
    ssq = TL("ssq", [128, 1]); lnv1 = TL("lnv1", [128, 1]); rstd1 = TL("rstd1", [128, 1])
    u_f = sb("u_f", [128, 4, TM]); u_f_r = [Res("u_f%d" % k) for k in range(4)]
    u_bf = sb("u_bf", [128, 4, TM], BF16); u_bf_r = [Res("u_bf%d" % k) for k in range(4)]
    yg_bf = sb("yg_bf", [128, 4, TM], BF16); yg_r = [Res("yg%d" % k) for k in range(4)]
    y2_bf = sb("y2_bf", [128, 4, TM], BF16); y2_r = [Res("y2%d" % k) for k in range(4)]
    q_f = sb("q_f", [128, 8, TM]); q_r = [Res("q%d" % k) for k in range(8)]
    sig_f = sb("sig_f", [128, 8, TM]); sig_r = [Res("sig%d" % k) for k in range(8)]
    v_tok = sb("v_tok", [128, NB, 1024], BF16); v_r = [Res("v%d" % b) for b in range(NB)]
    gs5 = sb("gs5", [128, 8, TM], BF16); gs5_r = [Res("gs5%d" % k) for k in range(8)]
    ghg = sb("ghg", [128, 8, TM], BF16); ghg_r = [Res("ghg%d" % k) for k in range(8)]
    big = sb("big", [128, 32, TM], BF16); big_r = [Res("big%d" % k) for k in range(32)]
    qe = big[:, 0:8, :]; qe_r = big_r[0:8]
    ke = big[:, 8:16, :]; ke_r = big_r[8:16]
    sg_bf = big[:, 16:24, :]; sg_r = big_r[16:24]
    yhg = big[:, 24:32, :]; yhg_r = big_r[24:32]
    hid = big; hid_r = big_r
    ms = sig_f; ms_r = sig_r
    mg_bf = qe; mg_r = qe_r
    x1 = q_f[:].rearrange("p a b -> p (a b)").rearrange("p (n d) -> p n d", n=NB)
    x1_rl = lambda b: q_r[(8 // NB) * b:(8 // NB) * (b + 1)]
    def mk_hgset(idx):
        if idx == 0:
            f = [TL("hg%d_%d" % (idx, i), [128, TM]) for i in range(7)]
            osq_ = TL("hg%d_osq" % idx, [128, TM], BF16)
        elif idx == 1:
            gflat = ge[:].rearrange("p a b -> p (a b)")
            f = [VW(_T(gflat[:, i * TM:(i + 1) * TM]), Res("hg1_%d" % i)) for i in range(7)]
            osq_ = VW(_T(gflat[:, 7 * TM:7 * TM + TM // 2].bitcast(BF16)), Res("hg1_osq"))
        else:
            sc = scrA if idx == 2 else scrB
            f = [VW(_T(sc.t[:, i * TM:(i + 1) * TM]), Res("hg%d_%d" % (idx, i))) for i in range(7)]
            osq_ = VW(_T(sc.t[:, 7 * TM:7 * TM + TM // 2].bitcast(BF16)), Res("hg%d_osq" % idx))
        hG_ = TL("hG%d" % idx, [128, NSS]); hel_ = TL("hel%d" % idx, [128, NSS])
        Ssc_ = TL("Ssc%d" % idx, [128, 128], BF16)
        if idx >= 2 and 7 * TM + TM // 2 + 128 <= 2048:
            dst_ = VW(_T(sc.t[:, 7 * TM + TM // 2:7 * TM + TM // 2 + 128]), Res("hg%d_dst" % idx))
        else:
            dst_ = TL("dst%d" % idx, [128, 128])
        kt_ = TL("ketok%d" % idx, [128, 128], BF16); sm_ = TL("scm%d" % idx, [128, 128], BF16)
        bufs = f + [osq_, hG_, hel_, Ssc_, dst_, kt_, sm_]
        return {"bufs": bufs, "res": [x.r for x in f] + [osq_.r, dst_.r], "po": None}
    HGSET = [mk_hgset(0), mk_hgset(1), mk_hgset(2), mk_hgset(3)]
    HGSET[0]["po"] = (psf[5], psf_r[5])
    ketokM = TL("ketokM", [64, NSS, 128], BF16)
    qeM = TL("qeM", [128, NSS, TS], BF16)
    gtmp = [TL("gtmp%d" % i, [128, TM]) for i in range(2)]
    scrA_q = [Res("scrAq%d" % i) for i in range(8)]
    scrB_q = [Res("scrBq%d" % i) for i in range(8)]
    S5SET = []
    for si, (sc, scq) in enumerate(((scrA, scrA_q), (scrB, scrB_q))):
        S5SET.append([VW(_T(sc.t[:, i * 256:(i + 1) * 256]), scq[i]) for i in range(8)])
    S.op("dve", lambda e: e.memset(scrA.t[0:1, 0:1], 0.0), [], [scrA.r, scrB.r, padb.r] + scrA_q + scrB_q + hT_r)
    hre_bf = sb("hre_bf", [128, FQ], BF16); him_bf = sb("him_bf", [128, FQ], BF16)
    hbf_r = [[Res("hre0"), Res("him0")], [Res("hre1"), Res("him1")]]
    ctmp = sb("ctmp", [128, 4, NSS]); ctmp_r = [Res("ctmp0"), Res("ctmp1")]
    ge1 = VW(_T(ge[:, 0, :]), ge_r[0]); ge2 = VW(_T(ge[:, 1, :]), ge_r[1])
    assert 8 * TM == NB * 1024
    mo = ge[:].rearrange("p a b -> p (a b)").rearrange("p (n d) -> p n d", n=NB)
    mo_rl = lambda b: ge_r if NB == 1 else [ge_r[b]]

    def lin_fm_gen(wn, col0, ncols, rhs_fn, rhs_res, T, evac):
        nk = WSPEC[wn][3]
        for u0 in range(0, ncols, UC):
            sl, slr = load_unit(wn, 0, (col0 + u0) // UC)
            for mi in range(UC // 128):
                pm, pmr = bank()
                for k in range(nk):
                    mm(pm[:, 0:T], lhsT=sl[:, k, mi * 128:(mi + 1) * 128], rhs=rhs_fn(k), start=(k == 0), stop=(k == nk - 1),
                       r=[slr] + rhs_res, w=[pmr])
                evac((u0 // 128) + mi, pm, pmr)
                yield

    def lin_fm(*a, **k):
        for _ in lin_fm_gen(*a, **k):
            pass

    def interleave(gens):
        act_l = list(gens)
        while act_l:
            for item in list(act_l):
                g, k = item
                for _ in range(k):
                    try:
                        next(g)
                    except StopIteration:
                        act_l.remove(item)
                        break

    def rms_rows(src_ap, src_res, nrows):
        mset("dve", ssq.t[:], 0.0, [ssq.r])
        act(junk.t[0:nrows, :], src_ap, AF.Square, src_res, [junk.r, ssq.r], accum=ssq.t[0:nrows, :])
        act(lnv1.t[0:nrows, :], ssq.t[0:nrows, :], AF.Ln, [ssq.r, c_eps.r], [lnv1.r], bias=c_eps.t[0:nrows, :], scale=1.0 / 1024)
        act(rstd1.t[0:nrows, :], lnv1.t[0:nrows, :], AF.Exp, [lnv1.r], [rstd1.r], scale=-0.5)

    def norm_transpose(src_ap, src_res, nrows, gcol, col0, dstT=None, dstT_r=None):
        if dstT is None:
            dstT, dstT_r = hT, hT_r
        rms_rows(src_ap, src_res, nrows)
        ts("dve", xsb.t[0:nrows, :], src_ap, rstd1.t[0:nrows, :], None, ALU.mult, None, src_res + [rstd1.r], [xsb.r])
        pb, pbr = bankb()
        for k in range(8):
            tr(out=pb[:, k * 128:k * 128 + nrows], in_=xsb.t[0:nrows, k * 128:(k + 1) * 128],
                                                 identity=ident.t[0:nrows, 0:nrows], r=[xsb.r, ident.r], w=[pbr])
        tt("dve", dstT[:, :, col0:col0 + nrows], pb[:].rearrange("p (a b) -> p a b", a=8)[:, :, 0:nrows],
           gcol.t[:].unsqueeze(2).to_broadcast([128, 8, nrows]), ALU.mult, [pbr, gcol.r], dstT_r)

    a_state = {"done": None}

    def a_pre(kind, tok0, T):
        nrows = 128 if kind == "p" else 64
        nblk = T // 128 if kind == "p" else 1
        xd = xp if kind == "p" else xs
        for b in range(nblk):
            xi = xin[ring["xin"] % NXIN]; ring["xin"] += 1
            S.dma("sp", xi.t[0:nrows, :], xd[tok0 + b * 128: tok0 + b * 128 + nrows, :], writes=[xi.r])
            rms_rows(xi.t[0:nrows, :], [xi.r], nrows)
            ts("dve", xsbA[b].t[0:nrows, :], xi.t[0:nrows, :], rstd1.t[0:nrows, :], None, ALU.mult, None, [xi.r, rstd1.r], [xsbA[b].r])

    def a_tr(kind, tok0, T):
        nrows = 128 if kind == "p" else 64
        nblk = T // 128 if kind == "p" else 1
        for b in range(nblk):
            pb, pbr = bankb()
            for k in range(8):
                tr(out=pb[:, k * 128:k * 128 + nrows], in_=xsbA[b].t[0:nrows, k * 128:(k + 1) * 128], identity=ident.t[0:nrows, 0:nrows],
                   r=[xsbA[b].r, ident.r], w=[pbr])
            tt("dve", hT[:, :, b * 128:b * 128 + nrows], pb[:].rearrange("p (a b) -> p a b", a=8)[:, :, 0:nrows],
               gpre.t[:].unsqueeze(2).to_broadcast([128, 8, nrows]), ALU.mult, [pbr, gpre.r], hT_r)
        a_state["done"] = (kind, tok0)

    def make_tile(kind, tok0, T):
        nrows = 128 if kind == "p" else 64
        nblk = T // 128 if kind == "p" else 1
        xd = xp if kind == "p" else xs
        yd = yp if kind == "p" else ys
        hrhs = lambda k: hT[:, k, 0:T]
        hrhs2 = lambda k: hT2[:, k, 0:T]
        tagn = "%s%d_" % (kind, tok0)
        tile = {}

        def front_pre():
            a_pre(kind, tok0, T)

        def ev_u(m, pm, pmr):
            act(u_f[:, m, 0:T], pm[:, 0:T], AF.Identity, [pmr, bcol.r], [u_f_r[m]], bias=bcol.t[:, m:m + 1])
            act(u_bf[:, m, 0:T], u_f[:, m, 0:T], AF.Copy, [u_f_r[m]], [u_bf_r[m]])

        def front_rest():
            a_tr(kind, tok0, T)
            lin_fm("w_in", 0, 512, hrhs, hT_r, T, ev_u)
            for s_ in range(4):
                sl, slr = load_unit("w_in", 0, 10 + s_)
                for b in range(nblk):
                    pm, pmr = bank()
                    for k in range(8):
                        mm(pm[0:nrows, 0:UC], lhsT=hT[:, k, b * 128:b * 128 + nrows], rhs=sl[:, k, :], start=(k == 0), stop=(k == 7),
                           r=[slr] + hT_r, w=[pmr])
                    tt("dve", v_tok[0:nrows, b, s_ * UC:(s_ + 1) * UC], pm[0:nrows, 0:UC], bi_bc.t[0:nrows, s_ * UC:(s_ + 1) * UC], ALU.add,
                       [pmr, bi_bc.r], [v_r[b]])
        tile["front_pre"] = front_pre
        tile["front_rest"] = front_rest

        def ev_q(m, pm, pmr):
            act(q_f[:, m, 0:T], pm[:, 0:T], AF.Silu, [pmr, bcol.r], [q_r[m]], bias=bcol.t[:, 4 + m:5 + m])

        def ev_g(m, pm, pmr):
            act(sg_bf[:, m, 0:T], pm[:, 0:T], AF.Silu, [pmr, bcol.r], [sg_r[m]], bias=bcol.t[:, 28 + m:29 + m])

        def ev_f(m, pm, pmr):
            act(sig_f[:, m, 0:T], pm[:, 0:T], AF.Sigmoid, [pmr, bcol.r], [sig_r[m]], bias=bcol.t[:, 12 + m:13 + m])

        def ev_gs(m, pm, pmr):
            act(gs5[:, m, 0:T], pm[:, 0:T], AF.Sigmoid, [pmr, bcol.r], [gs5_r[m]], bias=bcol.t[:, 36 + m:37 + m])

        def ev_gh(m, pm, pmr):
            act(ghg[:, m, 0:T], pm[:, 0:T], AF.Sigmoid, [pmr, bcol.r], [ghg_r[m]], bias=bcol.t[:, 44 + m:45 + m])

        def off(ev, d):
            return lambda m, pm, pmr: ev(m + d, pm, pmr)

        def proj_a_gen():
            if kind == "p":
                yield from lin_fm_gen("w_in", 512, 512, hrhs, hT_r, T, ev_q)
                yield from lin_fm_gen("w_in", 1536, 512, hrhs, hT_r, T, ev_f)
            else:
                for (c0_, ev) in ((512, ev_q), (3584, ev_g), (1536, ev_f)):
                    yield from lin_fm_gen("w_in", c0_, 1024, hrhs, hT_r, T, ev)

        def proj_b_gen():
            if kind == "p":
                yield from lin_fm_gen("w_in", 512 + 512, 512, hrhs, hT_r, T, off(ev_q, 4))
                yield from lin_fm_gen("w_in", 1536 + 512, 512, hrhs, hT_r, T, off(ev_f, 4))
                segs = ((3584, ev_g), (4608, ev_gs), (5632, ev_gh))
            else:
                segs = ((4608, ev_gs), (5632, ev_gh))
            for (c0_, ev) in segs:
                yield from lin_fm_gen("w_in", c0_, 1024, hrhs, hT_r, T, ev)

        if kind == "p":
            groups = [(c * 128, 128, 1) for c in range(T // 128)]
        else:
            groups = [(0, 4, NSS)]

        def s5_gen():
            pending_c = []
            for (c0, L, nch) in groups:
                F = L * nch
                NF = 2 * F
                d0 = d0p if kind == "p" else d0s
                assert F == (128 if kind == "p" else 64)
                py = None
                for pr in range(8):
                    si = pr % 2
                    a0, a1, a2, a3, wre, wim, zre, zim = S5SET[si]
                    hre = hre_bf[:, si * 256:si * 256 + NF]; him = him_bf[:, si * 256:si * 256 + NF]
                    hre_r, him_r = hbf_r[si]
                    ctm = ctmp[:, 2 * si:2 * si + 2, 0:nch]; ctm_r = ctmp_r[si]
                    uc = pr // 2
                    pp, ppr = bank()
                    for j in range(2):
                        ct = 2 * pr + j
                        mm(pp[:, j * F:(j + 1) * F], lhsT=lhsT_B.t[:, ct, 0, :], rhs=u_bf[:, uc, c0:c0 + F], start=True, stop=True,
                           r=[lhsT_B.r, u_bf_r[uc]], w=[ppr])
                        mm(pp[:, 256 + j * F:256 + (j + 1) * F], lhsT=lhsT_B.t[:, ct, 1, :], rhs=u_bf[:, uc, c0:c0 + F], start=True, stop=True,
                           r=[lhsT_B.r, u_bf_r[uc]], w=[ppr])
                    pre = pp[:, 0:NF]; pim = pp[:, 256:256 + NF]

                    def v4(ap, nch=nch):
                        return ap.rearrange("p (j c t) -> p j c t", j=2, c=nch)
                    cosb = cosT.t[:, 2 * pr:2 * pr + 2, 0:L].unsqueeze(2).to_broadcast([128, 2, nch, L])
                    sinb = sinT.t[:, 2 * pr:2 * pr + 2, 0:L].unsqueeze(2).to_broadcast([128, 2, nch, L])
                    tt("dve", v4(a0.t[:, 0:NF]), v4(pre), cosb, ALU.mult, [ppr, cosT.r], [a0.r])
                    tt("dve", v4(a1.t[:, 0:NF]), v4(pim), sinb, ALU.mult, [ppr, sinT.r], [a1.r])
                    tt("dve", v4(a2.t[:, 0:NF]), v4(pim), cosb, ALU.mult, [ppr, cosT.r], [a2.r])
                    tt("dve", v4(a3.t[:, 0:NF]), v4(pre), sinb, ALU.mult, [ppr, sinT.r], [a3.r])
                    yield
                    tt("pool", wre.t[:, 0:NF], a0.t[:, 0:NF], a1.t[:, 0:NF], ALU.add, [a0.r, a1.r], [wre.r])
                    tt("pool", wim.t[:, 0:NF], a2.t[:, 0:NF], a3.t[:, 0:NF], ALU.subtract, [a2.r, a3.r], [wim.r])
                    yield
                    if kind == "p":
                        for j in range(2):
                            ct = 2 * pr + j
                            rbc = rmag.t[:, ct:ct + 1].to_broadcast([128, F])
                            scan2(zre.t[:, j * F:(j + 1) * F], rbc, wre.t[:, j * F:(j + 1) * F], hc_re.t[:, ct, 0:1],
                                  [rmag.r, wre.r, hc_re.r], [zre.r])
                            scan2(zim.t[:, j * F:(j + 1) * F], rbc, wim.t[:, j * F:(j + 1) * F], hc_im.t[:, ct, 0:1],
                                  [rmag.r, wim.r, hc_im.r], [zim.r])
                            yield
                    else:
                        rb = rmag.t[:, 2 * pr:2 * pr + 2].unsqueeze(2).to_broadcast([128, 2, nch])
                        for (wt, hc) in ((wre, hc_re), (wim, hc_im)):
                            tt("pool", ctm, hc.t[:, 2 * pr:2 * pr + 2, 0:nch], rb, ALU.mult, [hc.r, rmag.r], [ctm_r])
                            w0 = v4(wt.t[:, 0:NF])[:, :, :, 0]
                            tt("pool", w0, w0, ctm, ALU.add, [wt.r, ctm_r], [wt.r])
                        yield
                        d0q = d0.t[:, 2 * pr:2 * pr + 2, :].rearrange("p a b -> p (a b)")
                        scan(zre.t[:, 0:NF], d0q, wre.t[:, 0:NF], [d0.r, wre.r], [zre.r])
                        scan(zim.t[:, 0:NF], d0q, wim.t[:, 0:NF], [d0.r, wim.r], [zim.r])
                        yield
                    tt("pool", v4(a0.t[:, 0:NF]), v4(zre.t[:, 0:NF]), cosb, ALU.mult, [zre.r, cosT.r], [a0.r])
                    tt("dve", v4(a1.t[:, 0:NF]), v4(zim.t[:, 0:NF]), sinb, ALU.mult, [zim.r, sinT.r], [a1.r])
                    yield
                    tt("pool", v4(a2.t[:, 0:NF]), v4(zim.t[:, 0:NF]), cosb, ALU.mult, [zim.r, cosT.r], [a2.r])
                    tt("dve", v4(a3.t[:, 0:NF]), v4(zre.t[:, 0:NF]), sinb, ALU.mult, [zre.r, sinT.r], [a3.r])
                    yield
                    if pending_c and pr % 2 == 0:
                        pending_c.pop(0)()
                    tt("dve", hre, a0.t[:, 0:NF], a1.t[:, 0:NF], ALU.subtract, [a0.r, a1.r], [hre_r])
                    tt("pool", him, a2.t[:, 0:NF], a3.t[:, 0:NF], ALU.add, [a2.r, a3.r], [him_r])
                    tt("pool", hc_re.t[:, 2 * pr:2 * pr + 2, 0:nch], v4(a0.t[:, 0:NF])[:, :, :, L - 1], v4(a1.t[:, 0:NF])[:, :, :, L - 1],
                       ALU.subtract, [a0.r, a1.r], [hc_re.r])
                    tt("pool", hc_im.t[:, 2 * pr:2 * pr + 2, 0:nch], v4(a2.t[:, 0:NF])[:, :, :, L - 1], v4(a3.t[:, 0:NF])[:, :, :, L - 1],
                       ALU.add, [a2.r, a3.r], [hc_im.r])
                    yield
                    if pr % 2 == 1:
                        def do_c(pr=pr, uc=uc, c0=c0, F=F, NF=NF):
                            pyt, pyr = bank()
                            idx = 0
                            for pq in (pr - 1, pr):
                                sj = pq % 2
                                hre_q = hre_bf[:, sj * 256:sj * 256 + NF]; him_q = him_bf[:, sj * 256:sj * 256 + NF]
                                for j in range(2):
                                    ct = 2 * pq + j
                                    mm(pyt[:, 0:F], lhsT=lhsT_C.t[:, ct, 0, :], rhs=hre_q[:, j * F:(j + 1) * F],
                                       start=(idx == 0), stop=False, r=[lhsT_C.r, hbf_r[sj][0]], w=[pyr])
                                    mm(pyt[:, 0:F], lhsT=lhsT_C.t[:, ct, 1, :], rhs=him_q[:, j * F:(j + 1) * F],
                                       start=False, stop=(idx == 3), r=[lhsT_C.r, hbf_r[sj][1]], w=[pyr])
                                    idx += 1
                            stt("dve", u_f[:, uc, c0:c0 + F], u_f[:, uc, c0:c0 + F], s5d.t[:, uc:uc + 1], pyt[:, 0:F], ALU.mult, ALU.add,
                                [u_f_r[uc], s5d.r, pyr], [u_f_r[uc]])
                        pending_c.append(do_c)
                    yield
            while pending_c:
                pending_c.pop(0)()
            yield

        def gelu_glu_gen():
            for m in range(4):
                yv = u_f[:, m, 0:T]
                g1 = scrA.t[:, m * 256:m * 256 + T]; g1r = scrA_q[m]
                g2 = scrB.t[:, m * 256:m * 256 + T]; g2r = scrB_q[m]
                tt("dve", g1, yv, yv, ALU.mult, [u_f_r[m]], [g1r])
                ts("dve", g1, g1, 0.044715, 1.0, ALU.mult, ALU.add, [g1r], [g1r])
                tt("dve", g1, g1, yv, ALU.mult, [g1r, u_f_r[m]], [g1r])
                yield
                act(g2, g1, AF.Sigmoid, [g1r], [g2r], scale=1.5957691216057308)
                tt("dve", yv, yv, g2, ALU.mult, [u_f_r[m], g2r], [u_f_r[m]])
                act(yg_bf[:, m, 0:T], yv, AF.Copy, [u_f_r[m]], [yg_r[m]])
                yield

        def glu_now():
            def ev_glu(m, pm, pmr):
                gt = scrA.t[:, 1024 + (m % 2) * 256:1024 + (m % 2) * 256 + T]; gtr = scrA_q[4 + m % 2]
                act(gt, pm[:, 0:T], AF.Sigmoid, [pmr, bglu.r], [gtr], bias=bglu.t[:, m:m + 1])
                tt("dve", y2_bf[:, m, 0:T], u_f[:, m, 0:T], gt, ALU.mult, [u_f_r[m], gtr], [y2_r[m]])
            lin_fm("w_glu", 0, 512, lambda k: yg_bf[:, k, 0:T], yg_r, T, ev_glu)

        def s5_plus_gen():
            yield from s5_gen()
            yield from gelu_glu_gen()

        s5_holder = {}

        def s5():
            if "g" not in s5_holder:
                s5_holder["g"] = s5_plus_gen()
            return s5_holder["g"]
        tile["s5"] = s5

        def rest(nt=None, pre_s5_hook=None):
            g_s5 = tile["s5"]()
            g_pa = proj_a_gen()
            done = {"s5": False}

            def step(g, n):
                for _ in range(n):
                    try:
                        next(g)
                    except StopIteration:
                        return False
                return True
            while True:
                if not done["s5"] and not step(g_s5, 2):
                    done["s5"] = True
                if not step(g_pa, 1):
                    break
            if kind == "s" and not done["s5"]:
                for _ in g_s5:
                    pass
                done["s5"] = True
            if done["s5"]:
                glu_now()
                tile["glu_done"] = True

            dbg(tagn + "u_f", u_f[:, :, 0:T], u_f_r)
            dbg(tagn + "q_f", q_f[:, :, 0:T], q_r)
            dbg(tagn + "sig_f", sig_f[:, :, 0:T], sig_r)
            dbg(tagn + "v_tok", v_tok[:], v_r)
            dbg(tagn + "sg", sg_bf[:, :, 0:T], sg_r)
            dbg(tagn + "gs5", gs5[:, :, 0:T], gs5_r)

            if kind == "p":
                chunks = [(c * 128, 128) for c in range(T // 128)]
            else:
                chunks = [(0, 64)]
            ncs = len(chunks)

            def hg_head_gen(h, st):
                hf, hb, hbm, heb, henb, o_sb, orstd, osq, hG, hel, Ssc, dst, kt, sm = st["bufs"]
                hlf = hf; hk = hf; olv = orstd
                d0h = d0hp if kind == "p" else d0hs
                act(hlf.t[:, 0:T], sig_f[:, h, 0:T], AF.Ln, [sig_r[h], oml.r, lb.r], [hlf.r], bias=lb.t[:, h:h + 1], scale=oml.t[:, h:h + 1])
                yield
                scan(hb.t[:, 0:T], d0h.t[:, 0:T], hlf.t[:, 0:T], [d0h.r, hlf.r], [hb.r])
                yield
                if kind == "p":
                    hb3 = hb.t[:, 0:T].rearrange("p (c t) -> p c t", t=128)
                    act(hG.t[:, 0:ncs], hb3[:, :, 63], AF.Exp, [hb.r], [hG.r])
                    act(hel.t[:, 0:ncs], hb3[:, :, 127], AF.Exp, [hb.r], [hel.r])
                    tt("dve", hbm.t[:, 0:T].rearrange("p (c t) -> p c t", t=128), hb3, hb3[:, :, 63:64].to_broadcast([128, ncs, 128]),
                       ALU.subtract, [hb.r], [hbm.r])
                    yield
                    act(heb.t[:, 0:T], hbm.t[:, 0:T], AF.Exp, [hbm.r], [heb.r])
                    act(henb.t[:, 0:T], hbm.t[:, 0:T], AF.Exp, [hbm.r], [henb.r], scale=-1.0)
                else:
                    act(hel.t[:, 0:NSS], hb.t[:, 0:T].rearrange("p (c t) -> p c t", t=4)[:, :, 3], AF.Exp, [hb.r], [hel.r])
                    yield
                    act(heb.t[:, 0:T], hb.t[:, 0:T], AF.Exp, [hb.r], [heb.r])
                    act(henb.t[:, 0:T], hb.t[:, 0:T], AF.Exp, [hb.r], [henb.r], scale=-1.0)
                ts("dve", hk.t[:, 0:T], sig_f[:, h, 0:T], noml.t[:, h:h + 1], oml.t[:, h:h + 1], ALU.mult, ALU.add,
                   [sig_r[h], noml.r, oml.r], [hk.r])
                yield
                tt("pool", qe[:, h, 0:T], q_f[:, h, 0:T], heb.t[:, 0:T], ALU.mult, [q_r[h], heb.r], [qe_r[h]])
                tt("pool", ke[:, h, 0:T], hk.t[:, 0:T], henb.t[:, 0:T], ALU.mult, [hk.r, henb.r], [ke_r[h]])
                yield

                if st["po"] is None:
                    pob = bank(hold=True)
                else:
                    pob = st["po"]
                po, por = pob
                for ci, (c0, Sz) in enumerate(chunks):
                    pb, pbr = bankb()
                    tr(out=pb[0:Sz, 0:128], in_=ke[:, h, c0:c0 + Sz], identity=ident.t[:], r=[ke_r[h], ident.r], w=[pbr])
                    psc, pscr = bank()
                    mm(psc[0:Sz, 0:Sz], lhsT=ke[:, h, c0:c0 + Sz], rhs=qe[:, h, c0:c0 + Sz],
                       start=True, stop=True, r=[ke_r[h], qe_r[h]], w=[pscr])
                    act(kt.t[0:Sz, :], pb[0:Sz, 0:128], AF.Copy, [pbr], [kt.r])
                    mk = maskP if kind == "p" else maskS
                    tt("dve", sm.t[0:Sz, 0:Sz], psc[0:Sz, 0:Sz], mk.t[0:Sz, 0:Sz], ALU.mult, [pscr, mk.r], [sm.r])
                    vb = ci if kind == "p" else 0
                    if kind == "p":
                        act(Ssc.t[:], Sst.t[:, h, :], AF.Copy, [Sst_r[h], hG.r], [Ssc.r], scale=hG.t[:, ci:ci + 1])
                        yield
                        mm(po[:, c0:c0 + Sz], lhsT=v_tok[0:Sz, vb, h * 128:(h + 1) * 128], rhs=sm.t[0:Sz, 0:Sz], start=True, stop=False,
                           r=[v_r[vb], sm.r], w=[por])
                        mm(po[:, c0:c0 + Sz], lhsT=Ssc.t[:], rhs=qe[:, h, c0:c0 + Sz], start=False, stop=True, r=[Ssc.r, qe_r[h]], w=[por])
                        pds, pdsr = bank()
                        mm(pds[:, 0:128], lhsT=kt.t[0:Sz, :], rhs=v_tok[0:Sz, vb, h * 128:(h + 1) * 128], start=True, stop=True,
                           r=[kt.r, v_r[vb]], w=[pdsr])
                        ts("dve", dst.t[:], pds[:, 0:128], heb.t[:, c0 + 127:c0 + 128], None, ALU.mult, None, [pdsr, heb.r], [dst.r])
                        stt("dve", Sst.t[:, h, :], Sst.t[:, h, :], hel.t[:, ci:ci + 1], dst.t[:], ALU.mult, ALU.add,
                            [Sst_r[h], hel.r, dst.r], [Sst_r[h]])
                        yield
                    else:
                        mm(po[:, 0:64], lhsT=v_tok[0:64, 0, h * 128:(h + 1) * 128], rhs=sm.t[0:64, 0:64],
                           start=True, stop=False, r=[v_r[0], sm.r], w=[por])
                        tt("dve", ketokM.t[:], kt.t[0:64, :].unsqueeze(1).to_broadcast([64, NSS, 128]),
                           seqm.t[:].unsqueeze(2).to_broadcast([64, NSS, 128]), ALU.mult, [kt.r, seqm.r], [ketokM.r])
                        tt("dve", qeM.t[:], qe[:, h, 0:TS].unsqueeze(1).to_broadcast([128, NSS, TS]), cmask.t[:], ALU.mult,
                           [qe_r[h], cmask.r], [qeM.r])
                        sX, sXr = (scrA, scrA_q) if h % 2 == 0 else (scrB, scrB_q)
                        s03 = sX.t[:].rearrange("p (s v) -> p s v", s=NSS)
                        xsb3 = xsb.t[:].rearrange("p (s v) -> p s v", s=8)
                        for half in range(2):
                            if half == 0:
                                act(xsb3, s03[:, 0:8, :], AF.Copy, sXr, [xsb.r])
                            else:
                                act(xsb3, s03[:, 8:16, :], AF.Copy, sXr, [xsb.r])
                            for j in range(8):
                                sq = half * 8 + j
                                mm(po[:, 0:64], lhsT=xsb3[:, j, :], rhs=qeM.t[:, sq, :], start=False, stop=(sq == NSS - 1),
                                   r=[xsb.r, qeM.r], w=[por])
                        sn3 = ge[:].rearrange("p a b -> p (a b)").rearrange("p (s v) -> p s v", s=NSS)
                        for g4 in range(4):
                            pds, pdsr = bank()
                            for j in range(4):
                                sq = 4 * g4 + j
                                mm(pds[:, j * 128:(j + 1) * 128], lhsT=ketokM.t[:, sq, :], rhs=v_tok[0:64, 0, h * 128:(h + 1) * 128],
                                   start=True, stop=True, r=[ketokM.r, v_r[0]], w=[pdsr])
                            tt("dve", sn3[:, 4 * g4:4 * g4 + 4, :], pds[:, 0:512].rearrange("p (s v) -> p s v", s=4), s03[:, 4 * g4:4 * g4 + 4, :],
                               ALU.add, [pdsr] + sXr, ge_r)
                            tt("dve", sn3[:, 4 * g4:4 * g4 + 4, :], sn3[:, 4 * g4:4 * g4 + 4, :],
                               hel.t[:, 4 * g4:4 * g4 + 4].unsqueeze(2).to_broadcast([128, 4, 128]), ALU.mult, ge_r + [hel.r], ge_r)
                        S.dma("sp", o_shg[:, h].rearrange("s k v -> k s v"), sn3, reads=ge_r)
                        yield
                act(o_sb.t[:, 0:T], po[:, 0:T], AF.Copy, [por], [o_sb.r])
                act(osq.t[:, 0:T], po[:, 0:T], AF.Square, [por], [osq.r])
                if st["po"] is None:
                    release(pob)
                yield
                pss, pssr = bank()
                mm(pss[:, 0:T], lhsT=ones_bf.t[:], rhs=osq.t[:, 0:T], start=True, stop=True, r=[ones_bf.r, osq.r], w=[pssr])
                act(olv.t[:, 0:T], pss[:, 0:T], AF.Ln, [pssr, c_eps.r], [olv.r], bias=c_eps.t[:], scale=1.0 / 128)
                yield
                act(orstd.t[:, 0:T], olv.t[:, 0:T], AF.Exp, [olv.r], [orstd.r], scale=-0.5)
                yield
                tt("dve", o_sb.t[:, 0:T], o_sb.t[:, 0:T], orstd.t[:, 0:T], ALU.mult, [o_sb.r, orstd.r], [o_sb.r])
                yield
                stt("dve", yhg[:, h, 0:T], o_sb.t[:, 0:T], hgn.t[:, 0:1], sg_bf[:, h, 0:T], ALU.mult, ALU.mult,
                    [o_sb.r, hgn.r, sg_r[h]], [yhg_r[h]])
                yield

            def hg_all_gen():
                if kind == "p":
                    nway = 4 if done["s5"] else 2
                    for h in range(0, 8, nway):
                        alive = [hg_head_gen(h + j, HGSET[j]) for j in range(nway)]
                        while alive:
                            for g in list(alive):
                                try:
                                    next(g)
                                    yield
                                except StopIteration:
                                    alive.remove(g)
                else:
                    def s0_load(hh):
                        sX, sXr = (scrA, scrA_q) if hh % 2 == 0 else (scrB, scrB_q)
                        S.dma("sp", sX.t[:].rearrange("p (s v) -> p s v", s=NSS), st_hg[:, hh].rearrange("s k v -> k s v"), writes=sXr)
                    s0_load(0)
                    for h in range(8):
                        if h + 1 < 8:
                            s0_load(h + 1)
                        yield from hg_head_gen(h, HGSET[0])

            if kind == "p":
                S.op("pool", lambda e: e.memset(HGSET[1]["bufs"][9].t[0:1, 0:1], 0.0), [], ge_r + HGSET[1]["res"])
                if done["s5"]:
                    S.op("pool", lambda e: e.memset(HGSET[2]["bufs"][9].t[0:1, 0:1], 0.0), [],
                         scrA_q + scrB_q + HGSET[2]["res"] + HGSET[3]["res"])
            gl = [(g_s5, 8), (hg_all_gen(), 16), (proj_b_gen(), 8)] if not done["s5"] else [(hg_all_gen(), 16), (proj_b_gen(), 8)]
            interleave(gl)
            if kind == "p":
                S.op("pool", lambda e: e.memset(HGSET[1]["bufs"][9].t[0:1, 0:1], 0.0), [], ge_r + HGSET[1]["res"])
                if done["s5"]:
                    S.op("pool", lambda e: e.memset(HGSET[2]["bufs"][9].t[0:1, 0:1], 0.0), [],
                         scrA_q + scrB_q + HGSET[2]["res"] + HGSET[3]["res"])

            dbg(tagn + "ys5", u_f[:, :, 0:T], u_f_r)
            dbg(tagn + "hc_re", hc_re.t[:], [hc_re.r])
            dbg(tagn + "y2", y2_bf[:, :, 0:T], y2_r)
            dbg(tagn + "qe", qe[:, :, 0:T], qe_r)
            dbg(tagn + "ke", ke[:, :, 0:T], ke_r)
            dbg(tagn + "yhg", yhg[:, :, 0:T], yhg_r)
            dbg(tagn + "Sst", Sst.t[:], Sst_r)
            if nt is not None:
                nt["front_pre"]()
            if not tile.get("glu_done"):
                glu_now()

            def ev_bs(m, pm, pmr):
                tt("dve", ms[:, m, 0:T], pm[:, 0:T], gs5[:, m, 0:T], ALU.mult, [pmr, gs5_r[m]], [ms_r[m]])
            lin_fm("w_bs", 0, 1024, lambda k: y2_bf[:, k, 0:T], y2_r, T, ev_bs)

            def ev_bh(m, pm, pmr):
                gt = gtmp[m % 2]
                tt("dve", gt.t[:, 0:T], pm[:, 0:T], ghg[:, m, 0:T], ALU.mult, [pmr, ghg_r[m]], [gt.r])
                tt("dve", mg_bf[:, m, 0:T], ms[:, m, 0:T], gt.t[:, 0:T], ALU.add, [ms_r[m], gt.r], [mg_r[m]])
            lin_fm("w_bh", 0, 1024, lambda k: yhg[:, k, 0:T], yhg_r, T, ev_bh)

            def tm_mm_gen(wn, nkg, lhs_fn, lhs_res):
                for u in range(1024 // UC):
                    bks = [bank(hold=True) for _ in range(nblk)]
                    for kg in range(nkg):
                        sl, slr = load_unit(wn, kg, u)
                        for b in range(nblk):
                            pm, pmr = bks[b]
                            for k in range(8):
                                kk = kg * 8 + k
                                mm(pm[0:nrows, 0:UC], lhsT=lhs_fn(kk, b), rhs=sl[:, k, :], start=(kk == 0), stop=(kk == nkg * 8 - 1),
                                   r=[slr] + lhs_res, w=[pmr])
                            yield
                    for b in range(nblk):
                        pm, pmr = bks[b]
                        act(mo[0:nrows, b, u * UC:(u + 1) * UC], pm[0:nrows, 0:UC], AF.Copy, [pmr], mo_rl(b))
                        release(bks[b])
                    yield

            def tm_epilogue(gbc, res_fn, out_fn):
                for b in range(nblk):
                    rms_rows(mo[0:nrows, b, :], mo_rl(b), nrows)
                    stt("dve", mo[0:nrows, b, :], mo[0:nrows, b, :], rstd1.t[0:nrows, :], gbc.t[0:nrows, :], ALU.mult, ALU.mult,
                        mo_rl(b) + [rstd1.r, gbc.r], mo_rl(b))
                    res_ap, res_res = res_fn(b)
                    out_ap, out_res = out_fn(b)
                    tt("dve", out_ap, mo[0:nrows, b, :], res_ap, ALU.add, mo_rl(b) + res_res, out_res)

            def lin_tm_norm_res(wn, nkg, lhs_fn, lhs_res, gbc, res_fn, out_fn):
                for _ in tm_mm_gen(wn, nkg, lhs_fn, lhs_res):
                    pass
                tm_epilogue(gbc, res_fn, out_fn)

            xres = {}

            def res_x(b):
                xi = xin[ring["xin"] % NXIN]; ring["xin"] += 1
                S.dma("sp", xi.t[0:nrows, :], xd[tok0 + b * 128: tok0 + b * 128 + nrows, :], writes=[xi.r])
                return xi.t[0:nrows, :], [xi.r]

            lin_tm_norm_res("w_out", 1, lambda kk, b: mg_bf[:, kk, b * 128:b * 128 + nrows], mg_r, gpost_bc, res_x,
                            lambda b: (x1[0:nrows, b, :], x1_rl(b)))

            dbg(tagn + "mg", mg_bf[:, :, 0:T], mg_r)
            dbg(tagn + "x1", x1[:], q_r)
            for b in range(nblk):
                norm_transpose(x1[0:nrows, b, :], x1_rl(b), nrows, g2pre, b * 128, hT2, hT2_r)
            if nt is not None:
                if pre_s5_hook is not None:
                    pre_s5_hook()
                nt["front_rest"]()

            def ev_up(m, pm, pmr):
                gt = gtmp[m % 2]
                act(gt.t[:, 0:T], pm[:, 0:T], AF.Relu, [pmr], [gt.r])
                act(hid[:, m, 0:T], gt.t[:, 0:T], AF.Square, [gt.r], [hid_r[m]])

            obuf = {}

            def out_y(b):
                xi = xin[ring["xin"] % NXIN]; ring["xin"] += 1
                obuf[b] = xi
                return xi.t[0:nrows, :], [xi.r]

            def mlp_gen():
                yield from lin_fm_gen("w_up", 0, 4096, hrhs2, hT2_r, T, ev_up)
                yield from tm_mm_gen("w_down", 4, lambda kk, b: hid[:, kk, b * 128:b * 128 + nrows], hid_r)
            if nt is not None:
                interleave([(mlp_gen(), 2), (nt["s5"](), 3)])
            else:
                for _ in mlp_gen():
                    pass
            tm_epilogue(g2post_bc, lambda b: (x1[0:nrows, b, :], x1_rl(b)), out_y)
            for b in range(nblk):
                xi = obuf[b]
                S.dma("pool", yd[tok0 + b * 128: tok0 + b * 128 + nrows, :], xi.t[0:nrows, :], reads=[xi.r])

        tile["rest"] = rest
        return tile

    def s5_prompt_out(hc, od):
        pm, pmr = bank()
        tr(out=pm[0:16, 0:128], in_=hc.t[:, :, 0], identity=identf.t[:], r=[hc.r, identf.r], w=[pmr])
        cp("dve", scrA.t[0:16, 0:128], pm[0:16, 0:128], [pmr], scrA_q)
        S.dma("sp", od.rearrange("(ct p) -> ct p", p=128), scrA.t[0:16, 0:128], reads=scrA_q)

    def s5_sample_in(hc, sd, sX, sXr):
        S.dma("sp", sX.t[0:NSS, :], sd, writes=sXr)
        pm, pmr = bank()
        for ct in range(16):
            tr(out=pm[:, ct * NSS:(ct + 1) * NSS], in_=sX.t[0:NSS, ct * 128:(ct + 1) * 128], identity=identf.t[0:NSS, 0:NSS],
               r=sXr + [identf.r], w=[pmr])
        cp("dve", hc.t[:].rearrange("p a b -> p (a b)"), pm[:, 0:16 * NSS], [pmr], [hc.r])

    def s5_sample_out(hc, od, sX, sXr):
        for g4 in range(4):
            pm, pmr = bank()
            for j in range(4):
                ct = 4 * g4 + j
                tr(out=pm[0:NSS, j * 128:(j + 1) * 128], in_=hc.t[:, ct, :], identity=identf.t[:], r=[hc.r, identf.r], w=[pmr])
            cp("dve", sX.t[0:NSS, g4 * 512:(g4 + 1) * 512], pm[0:NSS, 0:512], [pmr], sXr)
        S.dma("sp", od, sX.t[0:NSS, :], reads=sXr)

    def swap_to_prompt():
        s5_sample_out(hc_re, o_sre, scrA, scrA_q)
        s5_sample_out(hc_im, o_sim, scrB, scrB_q)
        mset("dve", hc_re.t[:], 0.0, [hc_re.r])
        mset("dve", hc_im.t[:], 0.0, [hc_im.r])

    NT = SEQ // TP
    tiles = [make_tile("s", 0, TS)] + [make_tile("p", t * TP, TP) for t in range(NT)]
    s5_sample_in(hc_re, st_re, scrA, scrA_q)
    s5_sample_in(hc_im, st_im, scrB, scrB_q)
    tiles[0]["front_pre"]()
    tiles[0]["front_rest"]()
    tiles[0]["rest"](tiles[1], swap_to_prompt)
    for t in range(1, NT + 1):
        tiles[t]["rest"](tiles[t + 1] if t < NT else None, None)
    s5_prompt_out(hc_re, o_pre)
    s5_prompt_out(hc_im, o_pim)
    S.dma("sp", o_phg.rearrange("h k v -> k h v"), Sst.t[:], reads=Sst_r)

    S.finalize("sp")
    with contextlib.ExitStack() as es2:
        sems_eng = {e: es2.enter_context(nc.semaphore("s_" + e)) for e in ENGS}
        sems_dma = [es2.enter_context(nc.semaphore("d%d" % i)) for i in range(Sched.NDMA)]
        block = es2.enter_context(nc.Block())

        def mk(ename):
            def f(eng):
                S.emit_engine(ename, eng, sems_eng, sems_dma)
            return f
        block.tensor(mk("pe"))
        block.scalar(mk("act"))
        block.vector(mk("dve"))
        block.gpsimd(mk("pool"))
        block.sync(mk("sp"))
    es.close()
    return nc


_NC_CACHE = {}


def kernel(x_prompt, x_sample, state_s5_re, state_s5_im, state_hg,
           norm_mix_pre, norm_mix_post, norm_mlp_pre, norm_mlp_post, w_in, b_in,
           s5_a_re, s5_a_im, s5_log_dt, s5_b_re, s5_b_im, s5_c_re, s5_c_im, s5_d, s5_w_glu, s5_b_glu,
           hg_lb_logits, hg_norm, w_br_s5, w_br_hg, w_out, w_up, w_down):
    f = lambda a: np.ascontiguousarray(np.asarray(a, dtype=np.float32))
    if "nc" not in _NC_CACHE:
        _NC_CACHE["nc"] = build_nc()
    nc = _NC_CACHE["nc"]
    shared = {
        "g_pre": f(norm_mix_pre).reshape(1024), "g_post": f(norm_mix_post).reshape(1024),
        "g2_pre": f(norm_mlp_pre).reshape(1024), "g2_post": f(norm_mlp_post).reshape(1024),
        "w_in": f(w_in).reshape(1024, 6656), "b_in": f(b_in).reshape(6656),
        "a_re": f(s5_a_re).reshape(2048), "a_im": f(s5_a_im).reshape(2048), "ldt": f(s5_log_dt).reshape(32),
        "b_re": f(s5_b_re).reshape(2048, 16), "b_im": f(s5_b_im).reshape(2048, 16),
        "c_re": f(s5_c_re).reshape(32, 16, 64), "c_im": f(s5_c_im).reshape(32, 16, 64),
        "s5d": f(s5_d).reshape(512), "w_glu": f(s5_w_glu).reshape(512, 512), "b_glu": f(s5_b_glu).reshape(512),
        "lbl": f(hg_lb_logits).reshape(2, 1024), "hgn": f(hg_norm).reshape(128),
        "w_bs": f(w_br_s5).reshape(512, 1024), "w_bh": f(w_br_hg).reshape(1024, 1024),
        "w_out": f(w_out).reshape(1024, 1024), "w_up": f(w_up).reshape(1024, 4096), "w_down": f(w_down).reshape(4096, 1024),
    }
    xpn = f(x_prompt); xsn = f(x_sample)
    sre = f(state_s5_re).reshape(128, 2048); sim = f(state_s5_im).reshape(128, 2048)
    shg = f(state_hg).reshape(128, 8, 128, 128)
    in_maps = []
    for c in range(NCORES):
        d = dict(shared)
        d["xp"] = xpn[c]
        d["xs"] = np.ascontiguousarray(xsn[c * NSS:(c + 1) * NSS].reshape(TS, 1024))
        d["st_re"] = np.ascontiguousarray(sre[c * NSS:(c + 1) * NSS])
        d["st_im"] = np.ascontiguousarray(sim[c * NSS:(c + 1) * NSS])
        d["st_hg"] = np.ascontiguousarray(shg[c * NSS:(c + 1) * NSS])
        in_maps.append(d)
    res = run_bass_kernel_spmd(nc, in_maps, core_ids=list(range(NCORES)))
    R = res.results
    y_prompt = np.stack([R[c]["yp"] for c in range(NCORES)], axis=0).astype(np.float32)
    y_sample = np.concatenate([R[c]["ys"].reshape(NSS, 4, 1024) for c in range(NCORES)], axis=0).astype(np.float32)
    p_re = np.stack([R[c]["o_pre"].reshape(32, 64) for c in range(NCORES)], axis=0)[None].astype(np.float32)
    p_im = np.stack([R[c]["o_pim"].reshape(32, 64) for c in range(NCORES)], axis=0)[None].astype(np.float32)
    p_hg = np.stack([R[c]["o_phg"] for c in range(NCORES)], axis=0)[None].astype(np.float32)
    s_re = np.concatenate([R[c]["o_sre"].reshape(NSS, 32, 64) for c in range(NCORES)], axis=0)[None].astype(np.float32)
    s_im = np.concatenate([R[c]["o_sim"].reshape(NSS, 32, 64) for c in range(NCORES)], axis=0)[None].astype(np.float32)
    s_hg = np.concatenate([R[c]["o_shg"] for c in range(NCORES)], axis=0)[None].astype(np.float32)
    return (y_prompt, y_sample, p_re, p_im, p_hg, s_re, s_im, s_hg)
```

```python
import contextlib
import math
import numpy as np
import concourse.bass as bass
import concourse.mybir as mybir
from concourse.bass_utils import run_bass_kernel_spmd

F32 = mybir.dt.float32
BF16 = mybir.dt.bfloat16
I32 = mybir.dt.int32
AF = mybir.ActivationFunctionType
ALU = mybir.AluOpType

ENGS = ("pe", "act", "dve", "pool", "sp")
TP = 256
NCORES = 8
SEQ = 2048
NSS = 16
TS = 64
TWO_PI = 2.0 * math.pi


class Res:
    __slots__ = ("name", "writers", "readers")

    def __init__(self, name=""):
        self.name = name
        self.writers = {}
        self.readers = {}


class Op:
    __slots__ = ("id", "eng", "emit", "deps", "is_dma", "sig", "signals", "waits")

    def __init__(self, id, eng, emit, deps, is_dma):
        self.id = id
        self.eng = eng
        self.emit = emit
        self.deps = deps
        self.is_dma = is_dma
        self.sig = None
        self.signals = is_dma
        self.waits = []


class Sched:
    NDMA = 56

    def __init__(self):
        self.ops = []

    def op(self, eng, emit, reads=(), writes=(), is_dma=False):
        oid = len(self.ops)
        deps = set()
        for r in reads:
            deps.update(r.writers.values())
        for w in writes:
            deps.update(w.writers.values())
            deps.update(w.readers.values())
        o = Op(oid, eng, emit, deps, is_dma)
        self.ops.append(o)
        k = ("dma", oid) if is_dma else eng
        for r in reads:
            r.readers[k] = oid
        for w in writes:
            w.writers = {k: oid}
            w.readers = {}
        return oid

    def dma(self, q, out, in_, reads=(), writes=(), **kw):
        return self.op(q, lambda e: e.dma_start(out=out, in_=in_, **kw), reads, writes, is_dma=True)

    def finalize(self, final_eng="sp"):
        ops = self.ops
        for o in ops:
            nd = set()
            for d in o.deps:
                do = ops[d]
                if (not do.is_dma) and (not o.is_dma) and do.eng == o.eng and o.eng == "pe":
                    continue
                nd.add(d)
            o.deps = nd
        slot_last = [None] * self.NDMA
        slot_cnt = [0] * self.NDMA
        pools = {"pool": list(range(0, 40)), "sp": list(range(40, self.NDMA)), "act": list(range(40, self.NDMA))}
        rrq = {"pool": 0, "sp": 0, "act": 0}
        for o in ops:
            if o.is_dma:
                pl = pools[o.eng]
                s = pl[rrq[o.eng] % len(pl)]
                rrq[o.eng] += 1
                if slot_last[s] is not None:
                    o.deps.add(slot_last[s])
                slot_last[s] = o.id
                slot_cnt[s] += 16
                o.sig = ("dma", s, slot_cnt[s])
        for o in ops:
            for d in o.deps:
                ops[d].signals = True
        cnt = {e: 0 for e in ENGS}
        for o in ops:
            if (not o.is_dma) and o.signals:
                cnt[o.eng] += 1
                o.sig = ("eng", o.eng, cnt[o.eng])
        seen = {e: {} for e in ENGS}
        for o in ops:
            need = {}
            for d in o.deps:
                kind, key, val = ops[d].sig
                k = (kind, key)
                if val > need.get(k, 0):
                    need[k] = val
            sn = seen[o.eng]
            for k, val in need.items():
                if sn.get(k, 0) >= val:
                    continue
                sn[k] = val
                o.waits.append((k, val))
        self.final_waits = []
        sn = seen[final_eng]
        last = {}
        for o in ops:
            if o.sig is not None:
                kind, key, val = o.sig
                last[(kind, key)] = max(last.get((kind, key), 0), val)
        for k, val in last.items():
            if sn.get(k, 0) < val:
                self.final_waits.append((k, val))
        self.final_eng = final_eng

    def emit_engine(self, e, eng, sems_eng, sems_dma):
        def semof(k):
            kind, key = k
            return sems_dma[key] if kind == "dma" else sems_eng[key]
        for o in self.ops:
            if o.eng != e:
                continue
            for (k, val) in o.waits:
                eng.wait_ge(semof(k), val)
            ins = o.emit(eng)
            if o.signals:
                kind, key, val = o.sig
                if kind == "dma":
                    ins.then_inc(sems_dma[key], 16)
                else:
                    ins.then_inc(sems_eng[key], 1)
        if e == self.final_eng:
            for (k, val) in self.final_waits:
                eng.wait_ge(semof(k), val)


DBG = None


def build_nc():
    nc = bass.Bass("TRN2", target_bir_lowering=False)
    S = Sched()
    es = contextlib.ExitStack()

    def din(name, shape):
        return nc.dram_tensor(name, list(shape), F32, kind="ExternalInput").ap()

    def dout(name, shape):
        return nc.dram_tensor(name, list(shape), F32, kind="ExternalOutput").ap()

    xp = din("xp", [SEQ, 1024])
    xs = din("xs", [TS, 1024])
    st_re = din("st_re", [NSS, 2048])
    st_im = din("st_im", [NSS, 2048])
    st_hg = din("st_hg", [NSS, 8, 128, 128])
    g_pre_d = din("g_pre", [1024])
    g_post_d = din("g_post", [1024])
    g2_pre_d = din("g2_pre", [1024])
    g2_post_d = din("g2_post", [1024])
    w_in = din("w_in", [1024, 6656])
    b_in = din("b_in", [6656])
    a_re_d = din("a_re", [2048])
    a_im_d = din("a_im", [2048])
    ldt_d = din("ldt", [32])
    b_re_d = din("b_re", [2048, 16])
    b_im_d = din("b_im", [2048, 16])
    c_re_d = din("c_re", [32, 16, 64])
    c_im_d = din("c_im", [32, 16, 64])
    s5d_d = din("s5d", [512])
    w_glu = din("w_glu", [512, 512])
    b_glu_d = din("b_glu", [512])
    lbl_d = din("lbl", [2, 1024])
    hgn_d = din("hgn", [128])
    w_bs = din("w_bs", [512, 1024])
    w_bh = din("w_bh", [1024, 1024])
    w_out = din("w_out", [1024, 1024])
    w_up = din("w_up", [1024, 4096])
    w_down = din("w_down", [4096, 1024])

    yp = dout("yp", [SEQ, 1024])
    ys = dout("ys", [TS, 1024])
    o_pre = dout("o_pre", [2048])
    o_pim = dout("o_pim", [2048])
    o_phg = dout("o_phg", [8, 128, 128])
    o_sre = dout("o_sre", [NSS, 2048])
    o_sim = dout("o_sim", [NSS, 2048])
    o_shg = dout("o_shg", [NSS, 8, 128, 128])

    def dbg(name, ap, res):
        if DBG is None or name not in DBG:
            return
        d = nc.dram_tensor("dbg_" + name, list(ap.shape), ap.dtype, kind="ExternalOutput").ap()
        S.dma("sp", d, ap, reads=res)

    def sb(name, shape, dt=F32):
        return es.enter_context(nc.sbuf_tensor("sb_" + name, list(shape), dt))

    class TL:
        def __init__(self, name, shape, dt=F32):
            self.t = sb(name, shape, dt)
            self.r = Res(name)

    def pe(f, r=(), w=()):
        S.op("pe", f, r, w)

    def mm(out, lhsT, rhs, start, stop, r=(), w=()):
        S.op("pe", lambda e: e.matmul(out, lhsT=lhsT, rhs=rhs, start=start, stop=stop), r, w)

    def tr(out, in_, identity, r=(), w=()):
        S.op("pe", lambda e: e.transpose(out=out, in_=in_, identity=identity), r, w)

    def scan(out, d0, d1, r=(), w=()):
        S.op("dve", lambda e: e.tensor_tensor_scan(out=out, data0=d0, data1=d1, initial=0.0, op0=ALU.mult, op1=ALU.add), r, w)

    def scan2(out, d0, d1, init, r=(), w=()):
        S.op("dve", lambda e: e.tensor_tensor_scan(out=out, data0=d0, data1=d1, initial=init, op0=ALU.mult, op1=ALU.add), r, w)

    def act(out, in_, func, r=(), w=(), bias=None, scale=1.0, accum=None):
        def f(e):
            kw = dict(out=out, in_=in_, func=func, scale=scale)
            if bias is not None:
                kw["bias"] = bias
            if accum is not None:
                kw["accum_out"] = accum
            return e.activation(**kw)
        S.op("act", f, r, w)

    def tt(eng, out, in0, in1, op, r=(), w=()):
        S.op(eng, lambda e: e.tensor_tensor(out=out, in0=in0, in1=in1, op=op), r, w)

    def ts(eng, out, in0, s1, s2, op0, op1=None, r=(), w=()):
        if op1 is None:
            S.op(eng, lambda e: e.tensor_scalar(out=out, in0=in0, scalar1=s1, scalar2=None, op0=op0), r, w)
        else:
            S.op(eng, lambda e: e.tensor_scalar(out=out, in0=in0, scalar1=s1, scalar2=s2, op0=op0, op1=op1), r, w)

    def stt(eng, out, in0, scalar, in1, op0, op1, r=(), w=()):
        S.op(eng, lambda e: e.scalar_tensor_tensor(out=out, in0=in0, scalar=scalar, in1=in1, op0=op0, op1=op1), r, w)

    def cp(eng, out, in_, r=(), w=()):
        S.op(eng, lambda e: e.tensor_copy(out=out, in_=in_), r, w)

    def mset(eng, ap, val, w=()):
        S.op(eng, lambda e: e.memset(ap, val), (), w)

    psf = [es.enter_context(nc.psum_tensor("psf%d" % i, [128, 512], F32)) for i in range(6)]
    psf_r = [Res("psf%d" % i) for i in range(6)]
    psb = [es.enter_context(nc.psum_tensor("psb%d" % i, [128, 1024], BF16)) for i in range(2)]
    psb_r = [Res("psb%d" % i) for i in range(2)]
    ring = {"f": 0, "b": 0, "slab": 0, "xin": 0}

    held = set()

    def bank(hold=False):
        while True:
            i = ring["f"] % 5
            ring["f"] += 1
            if i not in held:
                break
        if hold:
            held.add(i)
        return psf[i], psf_r[i]

    def release(bk):
        for i in range(5):
            if psf[i] is bk[0]:
                held.discard(i)

    def bankb():
        i = ring["b"] % 2
        ring["b"] += 1
        return psb[i], psb_r[i]

    NSLAB = 4
    UC = 256
    slabs = [sb("slab%d" % i, [128, 8, UC], BF16) for i in range(NSLAB)]
    slabs_r = [Res("slab%d" % i) for i in range(NSLAB)]

    WSPEC = {"w_in": (w_in, 1024, 6656, 8), "w_glu": (w_glu, 512, 512, 4), "w_bs": (w_bs, 512, 1024, 4),
             "w_bh": (w_bh, 1024, 1024, 8), "w_out": (w_out, 1024, 1024, 8), "w_up": (w_up, 1024, 4096, 8),
             "w_down": (w_down, 4096, 1024, 8)}
    wscr = {}
    wscr_r = {}

    def convert_weight(wn, units=None):
        w, K, N, nk = WSPEC[wn]
        KG = K // (nk * 128)
        NU2 = N // (2 * UC)
        if wn not in wscr:
            wscr[wn] = nc.dram_tensor("scr_" + wn, [KG, NU2, 128, nk, 2 * UC], BF16, kind="Internal").ap()
        ul = list(range(NU2)) if units is None else list(units)
        for kg in range(KG):
            for u2 in ul:
                r = Res("scr_%s_%d_%d" % (wn, kg, u2))
                wscr_r[(wn, kg, u2)] = r
                src = w[kg * nk * 128:(kg + 1) * nk * 128, u2 * 2 * UC:(u2 + 1) * 2 * UC].rearrange("(k p) n -> p k n", p=128)
                S.dma("pool", wscr[wn][kg, u2], src, writes=[r])

    def load_unit(wn, kg, u):
        nk = WSPEC[wn][3]
        i = ring["slab"] % NSLAB
        ring["slab"] += 1
        S.dma("sp", slabs[i][:, 0:nk, :], wscr[wn][kg, u // 2][:, :, (u % 2) * UC:(u % 2 + 1) * UC],
              reads=[wscr_r[(wn, kg, u // 2)]], writes=[slabs_r[i]])
        return slabs[i], slabs_r[i]

    class VW:
        def __init__(self, ap2d, r):
            self.t = ap2d
            self.r = r

    class _T:
        def __init__(self, ap):
            self.ap = ap

        def __getitem__(self, key):
            return self.ap[key]

    TM = TP
    NB = TM // 128
    ge = sb("ge", [128, 2, 4 * TM]); ge_r = [Res("ge0"), Res("ge1")]
    NXIN = 3
    xin = [TL("xin%d" % i, [128, 1024]) for i in range(NXIN)]
    xsbA = [TL("xsbA%d" % i, [128, 1024], BF16) for i in range(2)]
    hT = sb("hT", [128, 8, TM], BF16); hT_r = [Res("hT%d" % k) for k in range(8)]
    hT2 = sb("hT2", [128, 8, TM], BF16); hT2_r = [Res("hT2_%d" % k) for k in range(8)]

    c_eps = TL("c_eps", [128, 1]); mset("pool", c_eps.t[:], 1e-6, [c_eps.r])
    c_negpi = TL("c_negpi", [128, 1]); mset("pool", c_negpi.t[:], -math.pi, [c_negpi.r])
    onesf = TL("onesf", [128, 128]); mset("pool", onesf.t[:], 1.0, [onesf.r])
    ones_bf = TL("ones_bf", [128, 128], BF16); mset("pool", ones_bf.t[:], 1.0, [ones_bf.r])
    identf = TL("identf", [128, 128])
    S.op("pool", lambda e: e.affine_select(out=identf.t[:], in_=onesf.t[:], pattern=[[1, 128]], compare_op=ALU.is_equal,
                                           fill=0.0, base=0, channel_multiplier=-1), [onesf.r], [identf.r])
    ident = TL("ident", [128, 128], BF16)
    cp("dve", ident.t[:], identf.t[:], [identf.r], [ident.r])
    maskP = TL("maskP", [128, 128])
    S.op("pool", lambda e: e.affine_select(out=maskP.t[:], in_=onesf.t[:], pattern=[[1, 128]], compare_op=ALU.is_ge,
                                           fill=0.0, base=0, channel_multiplier=-1), [onesf.r], [maskP.r])
    maskS = TL("maskS", [64, 64])
    cp("pool", maskS.t[:], maskP.t[0:64, 0:64], [maskP.r], [maskS.r])
    for sq in range(NSS):
        S.op("pool", lambda e, sq=sq: e.affine_select(out=maskS.t[:, 4 * sq:4 * sq + 4], in_=maskS.t[:, 4 * sq:4 * sq + 4],
                                                      pattern=[[0, 4]], compare_op=ALU.is_ge, fill=0.0, base=-4 * sq,
                                                      channel_multiplier=1), [maskS.r], [maskS.r])
    seqm = TL("seqm", [64, NSS])
    S.op("pool", lambda e: e.affine_select(out=seqm.t[:], in_=onesf.t[0:64, 0:NSS], pattern=[[-4, NSS]], compare_op=ALU.is_ge,
                                           fill=0.0, base=0, channel_multiplier=1), [onesf.r], [seqm.r])
    S.op("pool", lambda e: e.affine_select(out=seqm.t[:], in_=seqm.t[:], pattern=[[4, NSS]], compare_op=ALU.is_ge,
                                           fill=0.0, base=3, channel_multiplier=-1), [seqm.r], [seqm.r])

    cmask = TL("cmask", [128, NSS, TS], BF16)
    mset("pool", cmask.t[:], 0.0, [cmask.r])
    for sq in range(NSS):
        mset("pool", cmask.t[:, sq, 4 * sq:4 * sq + 4], 1.0, [cmask.r])

    mask2 = TL("mask2", [128, 8])
    S.op("pool", lambda e: e.affine_select(out=mask2.t[:], in_=onesf.t[:, 0:8], pattern=[[-16, 8]], compare_op=ALU.is_ge,
                                           fill=0.0, base=0, channel_multiplier=1), [onesf.r], [mask2.r])
    S.op("pool", lambda e: e.affine_select(out=mask2.t[:], in_=mask2.t[:], pattern=[[16, 8]], compare_op=ALU.is_ge,
                                           fill=0.0, base=15, channel_multiplier=-1), [mask2.r], [mask2.r])
    iot_i = TL("iot_i", [128, 128], I32)
    S.op("pool", lambda e: e.iota(iot_i.t[:], pattern=[[1, 128]], base=1, channel_multiplier=0), [], [iot_i.r])
    convert_weight("w_in", [0, 5, 6, 1, 2, 3, 4, 7, 8, 9, 10, 11, 12])
    for wn in ("w_glu", "w_bs", "w_bh", "w_out", "w_up", "w_down"):
        convert_weight(wn)

    stg = [TL("stg%d" % i, [64, 128]) for i in range(2)]
    stg_i = [0]

    def load_cols(name, vec, n):
        t = TL(name, [128, n])
        st = stg[stg_i[0] % 2]; stg_i[0] += 1
        S.dma("sp", st.t[0:n, :], vec.rearrange("(c p) -> c p", p=128), writes=[st.r])
        pm, pmr = bank()
        tr(out=pm[:, 0:n], in_=st.t[0:n, :], identity=identf.t[0:n, 0:n], r=[st.r, identf.r], w=[pmr])
        cp("dve", t.t[:], pm[:, 0:n], [pmr], [t.r])
        return t

    def load_bc(name, vec, n):
        t = TL(name, [128, n])
        src = bass.AP(vec.tensor, vec.offset, [[0, 128], [1, n]])
        S.dma("sp", t.t[:], src, writes=[t.r])
        return t

    bcol = load_cols("bcol", b_in, 52)
    gpre = load_cols("gpre", g_pre_d, 8)
    g2pre = load_cols("g2pre", g2_pre_d, 8)
    gpost_bc = load_bc("gpost_bc", g_post_d, 1024)
    g2post_bc = load_bc("g2post_bc", g2_post_d, 1024)
    bi_bc = load_bc("bi_bc", b_in[2560:3584], 1024)
    s5d = load_cols("s5d", s5d_d, 4)
    bglu = load_cols("bglu", b_glu_d, 4)
    hgn = load_cols("hgn", hgn_d, 1)
    l0 = load_cols("l0", lbl_d[0, :], 8)
    l1 = load_cols("l1", lbl_d[1, :], 8)
    lbd = TL("lbd", [128, 8])
    tt("dve", lbd.t[:], l0.t[:], l1.t[:], ALU.subtract, [l0.r, l1.r], [lbd.r])
    lb = TL("lb", [128, 8]); act(lb.t[:], lbd.t[:], AF.Sigmoid, [lbd.r], [lb.r])
    oml = TL("oml", [128, 8]); act(oml.t[:], lbd.t[:], AF.Sigmoid, [lbd.r], [oml.r], scale=-1.0)
    noml = TL("noml", [128, 8]); ts("dve", noml.t[:], oml.t[:], -1.0, None, ALU.mult, None, [oml.r], [noml.r])

    are = load_cols("are", a_re_d, 16)
    aim = load_cols("aim", a_im_d, 16)
    ldt = TL("ldt", [128, 16])
    ldt_bc = load_bc("ldt_bc", ldt_d, 32)
    for two in range(2):
        cp("dve", ldt.t[64 * two:64 * two + 64, :], ldt_bc.t[64 * two:64 * two + 64, two::2], [ldt_bc.r], [ldt.r])
    dtt = TL("dtt", [128, 16]); act(dtt.t[:], ldt.t[:], AF.Exp, [ldt.r], [dtt.r])
    dre = TL("dre", [128, 16]); tt("dve", dre.t[:], dtt.t[:], are.t[:], ALU.mult, [dtt.r, are.r], [dre.r])
    th = TL("th", [128, 16]); tt("dve", th.t[:], dtt.t[:], aim.t[:], ALU.mult, [dtt.r, aim.r], [th.r])
    rmag = TL("rmag", [128, 16]); act(rmag.t[:], dre.t[:], AF.Exp, [dre.r], [rmag.r])

    FQ = 512
    scrA = TL("scrA", [128, 2048]); scrB = TL("scrB", [128, 2048])
    rs_a = VW(_T(ge[:, 0, 0:512]), ge_r[0]); rs_b = VW(_T(ge[:, 0, 512:1024]), ge_r[0])
    rs_i = VW(_T(ge[:, 1, 0:512].bitcast(I32)), ge_r[1])

    def range_sin(out_ap, ang_ap, shift, n, rr, ww):
        ta, tb, ti = rs_a, rs_b, rs_i
        ts("dve", ta.t[:, 0:n], ang_ap, shift + math.pi, None, ALU.add, None, rr, [ta.r])
        ts("dve", tb.t[:, 0:n], ta.t[:, 0:n], 1.0 / TWO_PI, None, ALU.mult, None, [ta.r], [tb.r])
        cp("dve", ti.t[:, 0:n], tb.t[:, 0:n], [tb.r], [ti.r])
        cp("dve", tb.t[:, 0:n], ti.t[:, 0:n], [ti.r], [tb.r])
        stt("dve", ta.t[:, 0:n], tb.t[:, 0:n], -TWO_PI, ta.t[:, 0:n], ALU.mult, ALU.add, [tb.r, ta.r], [ta.r])
        ts("dve", tb.t[:, 0:n], ta.t[:, 0:n], 0.0, TWO_PI, ALU.is_lt, ALU.mult, [ta.r], [tb.r])
        tt("dve", ta.t[:, 0:n], ta.t[:, 0:n], tb.t[:, 0:n], ALU.add, [ta.r, tb.r], [ta.r])
        ts("dve", tb.t[:, 0:n], ta.t[:, 0:n], TWO_PI, -TWO_PI, ALU.is_ge, ALU.mult, [ta.r], [tb.r])
        tt("dve", ta.t[:, 0:n], ta.t[:, 0:n], tb.t[:, 0:n], ALU.add, [ta.r, tb.r], [ta.r])
        act(out_ap, ta.t[:, 0:n], AF.Sin, [ta.r, c_negpi.r], ww, bias=c_negpi.t[:], scale=1.0)

    cth = TL("cth", [128, 16]); sth = TL("sth", [128, 16])
    range_sin(cth.t[:], th.t[:], math.pi / 2, 16, [th.r], [cth.r])
    range_sin(sth.t[:], th.t[:], 0.0, 16, [th.r], [sth.r])
    abre = TL("abre", [128, 16]); tt("dve", abre.t[:], rmag.t[:], cth.t[:], ALU.mult, [rmag.r, cth.r], [abre.r])
    abim = TL("abim", [128, 16]); tt("dve", abim.t[:], rmag.t[:], sth.t[:], ALU.mult, [rmag.r, sth.r], [abim.r])
    nr = TL("nr", [128, 16]); ts("dve", nr.t[:], abre.t[:], -1.0, None, ALU.add, None, [abre.r], [nr.r])
    den = TL("den", [128, 16]); t16 = TL("t16", [128, 16]); t16b = TL("t16b", [128, 16])
    tt("dve", den.t[:], are.t[:], are.t[:], ALU.mult, [are.r], [den.r])
    tt("dve", t16.t[:], aim.t[:], aim.t[:], ALU.mult, [aim.r], [t16.r])
    tt("dve", den.t[:], den.t[:], t16.t[:], ALU.add, [den.r, t16.r], [den.r])
    rden = TL("rden", [128, 16])
    S.op("dve", lambda e: e.reciprocal(out=rden.t[:], in_=den.t[:]), [den.r], [rden.r])
    cre = TL("cre", [128, 16]); cim = TL("cim", [128, 16])
    tt("dve", t16.t[:], nr.t[:], are.t[:], ALU.mult, [nr.r, are.r], [t16.r])
    tt("dve", t16b.t[:], abim.t[:], aim.t[:], ALU.mult, [abim.r, aim.r], [t16b.r])
    tt("dve", t16.t[:], t16.t[:], t16b.t[:], ALU.add, [t16.r, t16b.r], [t16.r])
    tt("dve", cre.t[:], t16.t[:], rden.t[:], ALU.mult, [t16.r, rden.r], [cre.r])
    tt("dve", t16.t[:], abim.t[:], are.t[:], ALU.mult, [abim.r, are.r], [t16.r])
    tt("dve", t16b.t[:], nr.t[:], aim.t[:], ALU.mult, [nr.r, aim.r], [t16b.r])
    tt("dve", t16.t[:], t16.t[:], t16b.t[:], ALU.subtract, [t16.r, t16b.r], [t16.r])
    tt("dve", cim.t[:], t16.t[:], rden.t[:], ALU.mult, [t16.r, rden.r], [cim.r])

    def v3(ap):
        return _T(ap.rearrange("p (a b) -> p a b", a=16))
    Bre = VW(v3(xin[0].t[:, 0:256]), xin[0].r); Bim = VW(v3(xin[0].t[:, 256:512]), xin[0].r)
    S.dma("sp", Bre.t[:], b_re_d.rearrange("(ct p) c -> p ct c", p=128), writes=[Bre.r])
    S.dma("sp", Bim.t[:], b_im_d.rearrange("(ct p) c -> p ct c", p=128), writes=[Bim.r])
    Bbre = VW(v3(xin[0].t[:, 512:768]), xin[0].r); Bbim = VW(v3(xin[0].t[:, 768:1024]), xin[0].r)
    tB = VW(v3(xin[1].t[:, 0:256]), xin[1].r)
    creb = cre.t[:].unsqueeze(2).to_broadcast([128, 16, 16])
    cimb = cim.t[:].unsqueeze(2).to_broadcast([128, 16, 16])
    tt("dve", Bbre.t[:], Bre.t[:], creb, ALU.mult, [Bre.r, cre.r], [Bbre.r])
    tt("dve", tB.t[:], Bim.t[:], cimb, ALU.mult, [Bim.r, cim.r], [tB.r])
    tt("dve", Bbre.t[:], Bbre.t[:], tB.t[:], ALU.subtract, [Bbre.r, tB.r], [Bbre.r])
    tt("dve", Bbim.t[:], Bim.t[:], creb, ALU.mult, [Bim.r, cre.r], [Bbim.r])
    tt("dve", tB.t[:], Bre.t[:], cimb, ALU.mult, [Bre.r, cim.r], [tB.r])
    tt("dve", Bbim.t[:], Bbim.t[:], tB.t[:], ALU.add, [Bbim.r, tB.r], [Bbim.r])

    lhsT_B = TL("lhsT_B", [128, 16, 2, 128], BF16)
    lhsT_C = TL("lhsT_C", [128, 16, 2, 128], BF16)
    padf = scrA
    padb = VW(_T(hT[:].rearrange("p a b -> p (a b)").rearrange("p (a b) -> p a b", a=16)), Res("padb"))
    padb_extra = hT_r
    padf3 = padf.t[:].rearrange("p (a b) -> p a b", a=16)
    for ri, Bb in enumerate((Bbre, Bbim)):
        mset("dve", padf.t[:], 0.0, [padf.r])
        for two in range(2):
            for m in range(4):
                col = (2 * m + two) * 16
                cp("dve", padf3[64 * two:64 * two + 64, m::4, col:col + 16], Bb.t[64 * two:64 * two + 64, m::4, :],
                   [Bb.r, padf.r], [padf.r])
        cp("dve", padb.t[:], padf3, [padf.r], [padb.r])
        for half in range(2):
            pb, pbr = bankb()
            for j in range(8):
                ct = half * 8 + j
                tr(out=pb[:, j * 128:(j + 1) * 128], in_=padb.t[:, ct, :], identity=ident.t[:], r=[padb.r, ident.r], w=[pbr])
            cp("dve", lhsT_B.t[:, half * 8:half * 8 + 8, ri, :], pb[:].rearrange("p (a b) -> p a b", a=8), [pbr], [lhsT_B.r])
    for ri, cd in enumerate((c_re_d, c_im_d)):
        Cn = xin[1].t[:, 256 + 256 * ri:512 + 256 * ri].rearrange("p (u n) -> p u n", u=4)
        S.dma("sp", Cn, cd.rearrange("g c n -> (g c) n").rearrange("(u q) n -> q u n", q=128), writes=[xin[1].r])
        for uc in range(4):
            tt("dve", padf3[:, 4 * uc:4 * uc + 4, :].rearrange("p m (t n) -> p m t n", t=2),
               Cn[:, uc, :].unsqueeze(1).unsqueeze(1).to_broadcast([128, 4, 2, 64]),
               mask2.t[:].rearrange("p (m t) -> p m t", t=2).unsqueeze(3).to_broadcast([128, 4, 2, 64]), ALU.mult,
               [xin[1].r, mask2.r], [padf.r])
        if ri == 0:
            cp("dve", padb.t[:], padf3, [padf.r], [padb.r])
        else:
            ts("dve", padb.t[:], padf3, -1.0, None, ALU.mult, None, [padf.r], [padb.r])
        for half in range(2):
            pb, pbr = bankb()
            for j in range(8):
                ct = half * 8 + j
                tr(out=pb[:, j * 128:(j + 1) * 128], in_=padb.t[:, ct, :], identity=ident.t[:], r=[padb.r, ident.r], w=[pbr])
            cp("dve", lhsT_C.t[:, half * 8:half * 8 + 8, ri, :], pb[:].rearrange("p (a b) -> p a b", a=8), [pbr], [lhsT_C.r])

    iot_f = TL("iot_f", [128, 128])
    cp("dve", iot_f.t[:], iot_i.t[:], [iot_i.r], [iot_f.r])
    cosT = TL("cosT", [128, 16, 128]); sinT = TL("sinT", [128, 16, 128])
    ang = scrB
    tt("dve", ang.t[:].rearrange("p (a b) -> p a b", a=16), th.t[:].unsqueeze(2).to_broadcast([128, 16, 128]),
       iot_f.t[:].unsqueeze(1).to_broadcast([128, 16, 128]), ALU.mult, [th.r, iot_f.r], [ang.r])
    cosTf = cosT.t[:].rearrange("p a b -> p (a b)"); sinTf = sinT.t[:].rearrange("p a b -> p (a b)")
    for pc in range(4):
        range_sin(cosTf[:, pc * 512:(pc + 1) * 512], ang.t[:, pc * 512:(pc + 1) * 512], math.pi / 2, 512, [ang.r], [cosT.r])
        range_sin(sinTf[:, pc * 512:(pc + 1) * 512], ang.t[:, pc * 512:(pc + 1) * 512], 0.0, 512, [ang.r], [sinT.r])
    d0s = TL("d0s", [128, 16, 64]); d0p = d0s
    cp("dve", d0s.t[:], rmag.t[:].unsqueeze(2).to_broadcast([128, 16, 64]), [rmag.r], [d0s.r])
    mset("dve", d0s.t[:].rearrange("p a (s t) -> p a s t", t=4)[:, :, :, 0:1], 0.0, [d0s.r])
    d0hp = TL("d0hp", [128, TP]); mset("dve", d0hp.t[:], 1.0, [d0hp.r])
    mset("dve", d0hp.t[:].rearrange("p (c t) -> p c t", t=128)[:, :, 0:1], 0.0, [d0hp.r])
    d0hs = TL("d0hs", [128, TS]); mset("dve", d0hs.t[:], 1.0, [d0hs.r])
    mset("dve", d0hs.t[:].rearrange("p (c t) -> p c t", t=4)[:, :, 0:1], 0.0, [d0hs.r])

    dbg("rmag", rmag.t[:], [rmag.r]); dbg("cth", cth.t[:], [cth.r]); dbg("sth", sth.t[:], [sth.r])
    dbg("cre", cre.t[:], [cre.r]); dbg("cim", cim.t[:], [cim.r]); dbg("Bbre", Bbre.t[:], [Bbre.r]); dbg("Bbim", Bbim.t[:], [Bbim.r])
    dbg("lhsT_B", lhsT_B.t[:], [lhsT_B.r]); dbg("lhsT_C", lhsT_C.t[:], [lhsT_C.r])
    dbg("cosT", cosT.t[:], [cosT.r]); dbg("sinT", sinT.t[:], [sinT.r])
    dbg("maskS", maskS.t[:], [maskS.r]); dbg("seqm", seqm.t[:], [seqm.r]); dbg("lb", lb.t[:], [lb.r])
    hc_re = TL("hc_re", [128, 16, NSS]); hc_im = TL("hc_im", [128, 16, NSS])
    mset("dve", hc_re.t[:], 0.0, [hc_re.r]); mset("dve", hc_im.t[:], 0.0, [hc_im.r])
    Sst = TL("Sst", [128, 8, 128])
    Sst_r = [Res("Sst%d" % h) for h in range(8)]
    mset("dve", Sst.t[:], 0.0, Sst_r)

    xsb = TL("xsb", [128, 1024], BF16)
    junk = xsb
    ssq = TL("ssq", [128, 1]); lnv1 = TL("lnv1", [128, 1]); rstd1 = TL("rstd1", [128, 1])
    u_f = sb("u_f", [128, 4, TM]); u_f_r = [Res("u_f%d" % k) for k in range(4)]
    u_bf = sb("u_bf", [128, 4, TM], BF16); u_bf_r = [Res("u_bf%d" % k) for k in range(4)]
    yg_bf = sb("yg_bf", [128, 4, TM], BF16); yg_r = [Res("yg%d" % k) for k in range(4)]
    y2_bf = sb("y2_bf", [128, 4, TM], BF16); y2_r = [Res("y2%d" % k) for k in range(4)]
    q_f = sb("q_f", [128, 8, TM]); q_r = [Res("q%d" % k) for k in range(8)]
    sig_f = sb("sig_f", [128, 8, TM]); sig_r = [Res("sig%d" % k) for k in range(8)]
    v_tok = sb("v_tok", [128, NB, 1024], BF16); v_r = [Res("v%d" % b) for b in range(NB)]
    gs5 = sb("gs5", [128, 8, TM], BF16); gs5_r = [Res("gs5%d" % k) for k in range(8)]
    ghg = sb("ghg", [128, 8, TM], BF16); ghg_r = [Res("ghg%d" % k) for k in range(8)]
    big = sb("big", [128, 32, TM], BF16); big_r = [Res("big%d" % k) for k in range(32)]
    qe = big[:, 0:8, :]; qe_r = big_r[0:8]
    ke = big[:, 8:16, :]; ke_r = big_r[8:16]
    sg_bf = big[:, 16:24, :]; sg_r = big_r[16:24]
    yhg = big[:, 24:32, :]; yhg_r = big_r[24:32]
    hid = big; hid_r = big_r
    ms = sig_f; ms_r = sig_r
    mg_bf = qe; mg_r = qe_r
    x1 = q_f[:].rearrange("p a b -> p (a b)").rearrange("p (n d) -> p n d", n=NB)
    x1_rl = lambda b: q_r[(8 // NB) * b:(8 // NB) * (b + 1)]
    def mk_hgset(idx):
        if idx == 0:
            f = [TL("hg%d_%d" % (idx, i), [128, TM]) for i in range(7)]
            osq_ = TL("hg%d_osq" % idx, [128, TM], BF16)
        elif idx == 1:
            gflat = ge[:].rearrange("p a b -> p (a b)")
            f = [VW(_T(gflat[:, i * TM:(i + 1) * TM]), Res("hg1_%d" % i)) for i in range(7)]
            osq_ = VW(_T(gflat[:, 7 * TM:7 * TM + TM // 2].bitcast(BF16)), Res("hg1_osq"))
        else:
            sc = scrA if idx == 2 else scrB
            f = [VW(_T(sc.t[:, i * TM:(i + 1) * TM]), Res("hg%d_%d" % (idx, i))) for i in range(7)]
            osq_ = VW(_T(sc.t[:, 7 * TM:7 * TM + TM // 2].bitcast(BF16)), Res("hg%d_osq" % idx))
        hG_ = TL("hG%d" % idx, [128, NSS]); hel_ = TL("hel%d" % idx, [128, NSS])
        Ssc_ = TL("Ssc%d" % idx, [128, 128], BF16)
        if idx >= 2 and 7 * TM + TM // 2 + 128 <= 2048:
            dst_ = VW(_T(sc.t[:, 7 * TM + TM // 2:7 * TM + TM // 2 + 128]), Res("hg%d_dst" % idx))
        else:
            dst_ = TL("dst%d" % idx, [128, 128])
        kt_ = TL("ketok%d" % idx, [128, 128], BF16); sm_ = TL("scm%d" % idx, [128, 128], BF16)
        bufs = f + [osq_, hG_, hel_, Ssc_, dst_, kt_, sm_]
        return {"bufs": bufs, "res": [x.r for x in f] + [osq_.r, dst_.r], "po": None}
    HGSET = [mk_hgset(0), mk_hgset(1), mk_hgset(2), mk_hgset(3)]
    HGSET[0]["po"] = (psf[5], psf_r[5])
    ketokM = TL("ketokM", [64, NSS, 128], BF16)
    qeM = TL("qeM", [128, NSS, TS], BF16)
    gtmp = [TL("gtmp%d" % i, [128, TM]) for i in range(2)]
    scrA_q = [Res("scrAq%d" % i) for i in range(8)]
    scrB_q = [Res("scrBq%d" % i) for i in range(8)]
    S5SET = []
    for si, (sc, scq) in enumerate(((scrA, scrA_q), (scrB, scrB_q))):
        S5SET.append([VW(_T(sc.t[:, i * 256:(i + 1) * 256]), scq[i]) for i in range(8)])
    S.op("dve", lambda e: e.memset(scrA.t[0:1, 0:1], 0.0), [], [scrA.r, scrB.r, padb.r] + scrA_q + scrB_q + hT_r)
    hre_bf = sb("hre_bf", [128, FQ], BF16); him_bf = sb("him_bf", [128, FQ], BF16)
    hbf_r = [[Res("hre0"), Res("him0")], [Res("hre1"), Res("him1")]]
    ctmp = sb("ctmp", [128, 4, NSS]); ctmp_r = [Res("ctmp0"), Res("ctmp1")]
    ge1 = VW(_T(ge[:, 0, :]), ge_r[0]); ge2 = VW(_T(ge[:, 1, :]), ge_r[1])
    assert 8 * TM == NB * 1024
    mo = ge[:].rearrange("p a b -> p (a b)").rearrange("p (n d) -> p n d", n=NB)
    mo_rl = lambda b: ge_r if NB == 1 else [ge_r[b]]

    def lin_fm_gen(wn, col0, ncols, rhs_fn, rhs_res, T, evac):
        nk = WSPEC[wn][3]
        for u0 in range(0, ncols, UC):
            sl, slr = load_unit(wn, 0, (col0 + u0) // UC)
            for mi in range(UC // 128):
                pm, pmr = bank()
                for k in range(nk):
                    mm(pm[:, 0:T], lhsT=sl[:, k, mi * 128:(mi + 1) * 128], rhs=rhs_fn(k), start=(k == 0), stop=(k == nk - 1),
                       r=[slr] + rhs_res, w=[pmr])
                evac((u0 // 128) + mi, pm, pmr)
                yield

    def lin_fm(*a, **k):
        for _ in lin_fm_gen(*a, **k):
            pass

    def interleave(gens):
        act_l = list(gens)
        while act_l:
            for item in list(act_l):
                g, k = item
                for _ in range(k):
                    try:
                        next(g)
                    except StopIteration:
                        act_l.remove(item)
                        break

    def rms_rows(src_ap, src_res, nrows):
        mset("dve", ssq.t[:], 0.0, [ssq.r])
        act(junk.t[0:nrows, :], src_ap, AF.Square, src_res, [junk.r, ssq.r], accum=ssq.t[0:nrows, :])
        act(lnv1.t[0:nrows, :], ssq.t[0:nrows, :], AF.Ln, [ssq.r, c_eps.r], [lnv1.r], bias=c_eps.t[0:nrows, :], scale=1.0 / 1024)
        act(rstd1.t[0:nrows, :], lnv1.t[0:nrows, :], AF.Exp, [lnv1.r], [rstd1.r], scale=-0.5)

    def norm_transpose(src_ap, src_res, nrows, gcol, col0, dstT=None, dstT_r=None):
        if dstT is None:
            dstT, dstT_r = hT, hT_r
        rms_rows(src_ap, src_res, nrows)
        act(xsb.t[0:nrows, :], src_ap, AF.Copy, src_res + [rstd1.r], [xsb.r], scale=rstd1.t[0:nrows, :])
        pb, pbr = bankb()
        for k in range(8):
            tr(out=pb[:, k * 128:k * 128 + nrows], in_=xsb.t[0:nrows, k * 128:(k + 1) * 128],
                                                 identity=ident.t[0:nrows, 0:nrows], r=[xsb.r, ident.r], w=[pbr])
        tt("dve", dstT[:, :, col0:col0 + nrows], pb[:].rearrange("p (a b) -> p a b", a=8)[:, :, 0:nrows],
           gcol.t[:].unsqueeze(2).to_broadcast([128, 8, nrows]), ALU.mult, [pbr, gcol.r], dstT_r)

    a_state = {"done": None}

    def a_pre(kind, tok0, T):
        nrows = 128 if kind == "p" else 64
        nblk = T // 128 if kind == "p" else 1
        xd = xp if kind == "p" else xs
        for b in range(nblk):
            xi = xin[ring["xin"] % NXIN]; ring["xin"] += 1
            S.dma("sp", xi.t[0:nrows, :], xd[tok0 + b * 128: tok0 + b * 128 + nrows, :], writes=[xi.r])
            rms_rows(xi.t[0:nrows, :], [xi.r], nrows)
            ts("dve", xsbA[b].t[0:nrows, :], xi.t[0:nrows, :], rstd1.t[0:nrows, :], None, ALU.mult, None, [xi.r, rstd1.r], [xsbA[b].r])

    def a_tr(kind, tok0, T):
        nrows = 128 if kind == "p" else 64
        nblk = T // 128 if kind == "p" else 1
        for b in range(nblk):
            pb, pbr = bankb()
            for k in range(8):
                tr(out=pb[:, k * 128:k * 128 + nrows], in_=xsbA[b].t[0:nrows, k * 128:(k + 1) * 128], identity=ident.t[0:nrows, 0:nrows],
                   r=[xsbA[b].r, ident.r], w=[pbr])
            tt("dve", hT[:, :, b * 128:b * 128 + nrows], pb[:].rearrange("p (a b) -> p a b", a=8)[:, :, 0:nrows],
               gpre.t[:].unsqueeze(2).to_broadcast([128, 8, nrows]), ALU.mult, [pbr, gpre.r], hT_r)
        a_state["done"] = (kind, tok0)

    def make_tile(kind, tok0, T):
        nrows = 128 if kind == "p" else 64
        nblk = T // 128 if kind == "p" else 1
        xd = xp if kind == "p" else xs
        yd = yp if kind == "p" else ys
        hrhs = lambda k: hT[:, k, 0:T]
        hrhs2 = lambda k: hT2[:, k, 0:T]
        tagn = "%s%d_" % (kind, tok0)
        tile = {}

        def front_pre():
            a_pre(kind, tok0, T)

        def ev_u(m, pm, pmr):
            act(u_f[:, m, 0:T], pm[:, 0:T], AF.Identity, [pmr, bcol.r], [u_f_r[m]], bias=bcol.t[:, m:m + 1])
            act(u_bf[:, m, 0:T], u_f[:, m, 0:T], AF.Copy, [u_f_r[m]], [u_bf_r[m]])

        def front_rest():
            a_tr(kind, tok0, T)
            lin_fm("w_in", 0, 512, hrhs, hT_r, T, ev_u)
            for s_ in range(4):
                sl, slr = load_unit("w_in", 0, 10 + s_)
                for b in range(nblk):
                    pm, pmr = bank()
                    for k in range(8):
                        mm(pm[0:nrows, 0:UC], lhsT=hT[:, k, b * 128:b * 128 + nrows], rhs=sl[:, k, :], start=(k == 0), stop=(k == 7),
                           r=[slr] + hT_r, w=[pmr])
                    tt("dve", v_tok[0:nrows, b, s_ * UC:(s_ + 1) * UC], pm[0:nrows, 0:UC], bi_bc.t[0:nrows, s_ * UC:(s_ + 1) * UC], ALU.add,
                       [pmr, bi_bc.r], [v_r[b]])
        tile["front_pre"] = front_pre
        tile["front_rest"] = front_rest

        def ev_q(m, pm, pmr):
            act(q_f[:, m, 0:T], pm[:, 0:T], AF.Silu, [pmr, bcol.r], [q_r[m]], bias=bcol.t[:, 4 + m:5 + m])

        def ev_g(m, pm, pmr):
            act(sg_bf[:, m, 0:T], pm[:, 0:T], AF.Silu, [pmr, bcol.r], [sg_r[m]], bias=bcol.t[:, 28 + m:29 + m])

        def ev_f(m, pm, pmr):
            act(sig_f[:, m, 0:T], pm[:, 0:T], AF.Sigmoid, [pmr, bcol.r], [sig_r[m]], bias=bcol.t[:, 12 + m:13 + m])

        def ev_gs(m, pm, pmr):
            act(gs5[:, m, 0:T], pm[:, 0:T], AF.Sigmoid, [pmr, bcol.r], [gs5_r[m]], bias=bcol.t[:, 36 + m:37 + m])

        def ev_gh(m, pm, pmr):
            act(ghg[:, m, 0:T], pm[:, 0:T], AF.Sigmoid, [pmr, bcol.r], [ghg_r[m]], bias=bcol.t[:, 44 + m:45 + m])

        def proj_a_gen():
            segs = ((512, ev_q), (1536, ev_f)) if kind == "p" else ((512, ev_q), (3584, ev_g), (1536, ev_f))
            for (c0_, ev) in segs:
                yield from lin_fm_gen("w_in", c0_, 1024, hrhs, hT_r, T, ev)

        def proj_b_gen():
            segs = ((3584, ev_g), (4608, ev_gs), (5632, ev_gh)) if kind == "p" else ((4608, ev_gs), (5632, ev_gh))
            for (c0_, ev) in segs:
                yield from lin_fm_gen("w_in", c0_, 1024, hrhs, hT_r, T, ev)

        if kind == "p":
            groups = [(c * 128, 128, 1) for c in range(T // 128)]
        else:
            groups = [(0, 4, NSS)]

        def s5_gen():
            pending_c = []
            for (c0, L, nch) in groups:
                F = L * nch
                NF = 2 * F
                d0 = d0p if kind == "p" else d0s
                assert F == (128 if kind == "p" else 64)
                py = None
                for pr in range(8):
                    si = pr % 2
                    a0, a1, a2, a3, wre, wim, zre, zim = S5SET[si]
                    hre = hre_bf[:, si * 256:si * 256 + NF]; him = him_bf[:, si * 256:si * 256 + NF]
                    hre_r, him_r = hbf_r[si]
                    ctm = ctmp[:, 2 * si:2 * si + 2, 0:nch]; ctm_r = ctmp_r[si]
                    uc = pr // 2
                    pp, ppr = bank()
                    for j in range(2):
                        ct = 2 * pr + j
                        mm(pp[:, j * F:(j + 1) * F], lhsT=lhsT_B.t[:, ct, 0, :], rhs=u_bf[:, uc, c0:c0 + F], start=True, stop=True,
                           r=[lhsT_B.r, u_bf_r[uc]], w=[ppr])
                        mm(pp[:, 256 + j * F:256 + (j + 1) * F], lhsT=lhsT_B.t[:, ct, 1, :], rhs=u_bf[:, uc, c0:c0 + F], start=True, stop=True,
                           r=[lhsT_B.r, u_bf_r[uc]], w=[ppr])
                    pre = pp[:, 0:NF]; pim = pp[:, 256:256 + NF]

                    def v4(ap, nch=nch):
                        return ap.rearrange("p (j c t) -> p j c t", j=2, c=nch)
                    cosb = cosT.t[:, 2 * pr:2 * pr + 2, 0:L].unsqueeze(2).to_broadcast([128, 2, nch, L])
                    sinb = sinT.t[:, 2 * pr:2 * pr + 2, 0:L].unsqueeze(2).to_broadcast([128, 2, nch, L])
                    tt("dve", v4(a0.t[:, 0:NF]), v4(pre), cosb, ALU.mult, [ppr, cosT.r], [a0.r])
                    tt("dve", v4(a1.t[:, 0:NF]), v4(pim), sinb, ALU.mult, [ppr, sinT.r], [a1.r])
                    tt("dve", v4(a2.t[:, 0:NF]), v4(pim), cosb, ALU.mult, [ppr, cosT.r], [a2.r])
                    tt("dve", v4(a3.t[:, 0:NF]), v4(pre), sinb, ALU.mult, [ppr, sinT.r], [a3.r])
                    yield
                    tt("pool", wre.t[:, 0:NF], a0.t[:, 0:NF], a1.t[:, 0:NF], ALU.add, [a0.r, a1.r], [wre.r])
                    tt("pool", wim.t[:, 0:NF], a2.t[:, 0:NF], a3.t[:, 0:NF], ALU.subtract, [a2.r, a3.r], [wim.r])
                    yield
                    if kind == "p":
                        for j in range(2):
                            ct = 2 * pr + j
                            rbc = rmag.t[:, ct:ct + 1].to_broadcast([128, F])
                            scan2(zre.t[:, j * F:(j + 1) * F], rbc, wre.t[:, j * F:(j + 1) * F], hc_re.t[:, ct, 0:1],
                                  [rmag.r, wre.r, hc_re.r], [zre.r])
                            scan2(zim.t[:, j * F:(j + 1) * F], rbc, wim.t[:, j * F:(j + 1) * F], hc_im.t[:, ct, 0:1],
                                  [rmag.r, wim.r, hc_im.r], [zim.r])
                            yield
                    else:
                        rb = rmag.t[:, 2 * pr:2 * pr + 2].unsqueeze(2).to_broadcast([128, 2, nch])
                        for (wt, hc) in ((wre, hc_re), (wim, hc_im)):
                            tt("pool", ctm, hc.t[:, 2 * pr:2 * pr + 2, 0:nch], rb, ALU.mult, [hc.r, rmag.r], [ctm_r])
                            w0 = v4(wt.t[:, 0:NF])[:, :, :, 0]
                            tt("pool", w0, w0, ctm, ALU.add, [wt.r, ctm_r], [wt.r])
                        yield
                        d0q = d0.t[:, 2 * pr:2 * pr + 2, :].rearrange("p a b -> p (a b)")
                        scan(zre.t[:, 0:NF], d0q, wre.t[:, 0:NF], [d0.r, wre.r], [zre.r])
                        scan(zim.t[:, 0:NF], d0q, wim.t[:, 0:NF], [d0.r, wim.r], [zim.r])
                        yield
                    tt("pool", v4(a0.t[:, 0:NF]), v4(zre.t[:, 0:NF]), cosb, ALU.mult, [zre.r, cosT.r], [a0.r])
                    tt("dve", v4(a1.t[:, 0:NF]), v4(zim.t[:, 0:NF]), sinb, ALU.mult, [zim.r, sinT.r], [a1.r])
                    yield
                    tt("pool", v4(a2.t[:, 0:NF]), v4(zim.t[:, 0:NF]), cosb, ALU.mult, [zim.r, cosT.r], [a2.r])
                    tt("dve", v4(a3.t[:, 0:NF]), v4(zre.t[:, 0:NF]), sinb, ALU.mult, [zre.r, sinT.r], [a3.r])
                    yield
                    if pending_c and pr % 2 == 0:
                        pending_c.pop(0)()
                    tt("dve", hre, a0.t[:, 0:NF], a1.t[:, 0:NF], ALU.subtract, [a0.r, a1.r], [hre_r])
                    tt("pool", him, a2.t[:, 0:NF], a3.t[:, 0:NF], ALU.add, [a2.r, a3.r], [him_r])
                    tt("pool", hc_re.t[:, 2 * pr:2 * pr + 2, 0:nch], v4(a0.t[:, 0:NF])[:, :, :, L - 1], v4(a1.t[:, 0:NF])[:, :, :, L - 1],
                       ALU.subtract, [a0.r, a1.r], [hc_re.r])
                    tt("pool", hc_im.t[:, 2 * pr:2 * pr + 2, 0:nch], v4(a2.t[:, 0:NF])[:, :, :, L - 1], v4(a3.t[:, 0:NF])[:, :, :, L - 1],
                       ALU.add, [a2.r, a3.r], [hc_im.r])
                    yield
                    if pr % 2 == 1:
                        def do_c(pr=pr, uc=uc, c0=c0, F=F, NF=NF):
                            pyt, pyr = bank()
                            idx = 0
                            for pq in (pr - 1, pr):
                                sj = pq % 2
                                hre_q = hre_bf[:, sj * 256:sj * 256 + NF]; him_q = him_bf[:, sj * 256:sj * 256 + NF]
                                for j in range(2):
                                    ct = 2 * pq + j
                                    mm(pyt[:, 0:F], lhsT=lhsT_C.t[:, ct, 0, :], rhs=hre_q[:, j * F:(j + 1) * F],
                                       start=(idx == 0), stop=False, r=[lhsT_C.r, hbf_r[sj][0]], w=[pyr])
                                    mm(pyt[:, 0:F], lhsT=lhsT_C.t[:, ct, 1, :], rhs=him_q[:, j * F:(j + 1) * F],
                                       start=False, stop=(idx == 3), r=[lhsT_C.r, hbf_r[sj][1]], w=[pyr])
                                    idx += 1
                            stt("dve", u_f[:, uc, c0:c0 + F], u_f[:, uc, c0:c0 + F], s5d.t[:, uc:uc + 1], pyt[:, 0:F], ALU.mult, ALU.add,
                                [u_f_r[uc], s5d.r, pyr], [u_f_r[uc]])
                        pending_c.append(do_c)
                    yield
            while pending_c:
                pending_c.pop(0)()
            yield

        def gelu_glu_gen():
            for m in range(4):
                yv = u_f[:, m, 0:T]
                g1 = scrA.t[:, m * 256:m * 256 + T]; g1r = scrA_q[m]
                g2 = scrB.t[:, m * 256:m * 256 + T]; g2r = scrB_q[m]
                act(g1, yv, AF.Square, [u_f_r[m]], [g1r])
                ts("dve", g1, g1, 0.044715, 1.0, ALU.mult, ALU.add, [g1r], [g1r])
                tt("dve", g1, g1, yv, ALU.mult, [g1r, u_f_r[m]], [g1r])
                yield
                act(g2, g1, AF.Sigmoid, [g1r], [g2r], scale=1.5957691216057308)
                tt("dve", yv, yv, g2, ALU.mult, [u_f_r[m], g2r], [u_f_r[m]])
                act(yg_bf[:, m, 0:T], yv, AF.Copy, [u_f_r[m]], [yg_r[m]])
                yield

        def glu_now():
            def ev_glu(m, pm, pmr):
                gt = scrA.t[:, 1024 + (m % 2) * 256:1024 + (m % 2) * 256 + T]; gtr = scrA_q[4 + m % 2]
                act(gt, pm[:, 0:T], AF.Sigmoid, [pmr, bglu.r], [gtr], bias=bglu.t[:, m:m + 1])
                tt("dve", y2_bf[:, m, 0:T], u_f[:, m, 0:T], gt, ALU.mult, [u_f_r[m], gtr], [y2_r[m]])
            lin_fm("w_glu", 0, 512, lambda k: yg_bf[:, k, 0:T], yg_r, T, ev_glu)

        def s5_plus_gen():
            yield from s5_gen()
            yield from gelu_glu_gen()

        s5_holder = {}

        def s5():
            if "g" not in s5_holder:
                s5_holder["g"] = s5_plus_gen()
            return s5_holder["g"]
        tile["s5"] = s5

        def rest(nt=None, pre_s5_hook=None):
            g_s5 = tile["s5"]()
            g_pa = proj_a_gen()
            done = {"s5": False}

            def step(g, n):
                for _ in range(n):
                    try:
                        next(g)
                    except StopIteration:
                        return False
                return True
            while True:
                if not done["s5"] and not step(g_s5, 2):
                    done["s5"] = True
                if not step(g_pa, 1):
                    break
            if kind == "s" and not done["s5"]:
                for _ in g_s5:
                    pass
                done["s5"] = True
            if done["s5"]:
                glu_now()
                tile["glu_done"] = True

            dbg(tagn + "u_f", u_f[:, :, 0:T], u_f_r)
            dbg(tagn + "q_f", q_f[:, :, 0:T], q_r)
            dbg(tagn + "sig_f", sig_f[:, :, 0:T], sig_r)
            dbg(tagn + "v_tok", v_tok[:], v_r)
            dbg(tagn + "sg", sg_bf[:, :, 0:T], sg_r)
            dbg(tagn + "gs5", gs5[:, :, 0:T], gs5_r)

            if kind == "p":
                chunks = [(c * 128, 128) for c in range(T // 128)]
            else:
                chunks = [(0, 64)]
            ncs = len(chunks)

            def hg_head_gen(h, st):
                hf, hb, hbm, heb, henb, o_sb, orstd, osq, hG, hel, Ssc, dst, kt, sm = st["bufs"]
                hlf = hf; hk = hf; olv = orstd
                d0h = d0hp if kind == "p" else d0hs
                act(hlf.t[:, 0:T], sig_f[:, h, 0:T], AF.Ln, [sig_r[h], oml.r, lb.r], [hlf.r], bias=lb.t[:, h:h + 1], scale=oml.t[:, h:h + 1])
                yield
                scan(hb.t[:, 0:T], d0h.t[:, 0:T], hlf.t[:, 0:T], [d0h.r, hlf.r], [hb.r])
                yield
                if kind == "p":
                    hb3 = hb.t[:, 0:T].rearrange("p (c t) -> p c t", t=128)
                    act(hG.t[:, 0:ncs], hb3[:, :, 63], AF.Exp, [hb.r], [hG.r])
                    act(hel.t[:, 0:ncs], hb3[:, :, 127], AF.Exp, [hb.r], [hel.r])
                    tt("dve", hbm.t[:, 0:T].rearrange("p (c t) -> p c t", t=128), hb3, hb3[:, :, 63:64].to_broadcast([128, ncs, 128]),
                       ALU.subtract, [hb.r], [hbm.r])
                    yield
                    act(heb.t[:, 0:T], hbm.t[:, 0:T], AF.Exp, [hbm.r], [heb.r])
                    act(henb.t[:, 0:T], hbm.t[:, 0:T], AF.Exp, [hbm.r], [henb.r], scale=-1.0)
                else:
                    act(hel.t[:, 0:NSS], hb.t[:, 0:T].rearrange("p (c t) -> p c t", t=4)[:, :, 3], AF.Exp, [hb.r], [hel.r])
                    yield
                    act(heb.t[:, 0:T], hb.t[:, 0:T], AF.Exp, [hb.r], [heb.r])
                    act(henb.t[:, 0:T], hb.t[:, 0:T], AF.Exp, [hb.r], [henb.r], scale=-1.0)
                ts("dve", hk.t[:, 0:T], sig_f[:, h, 0:T], noml.t[:, h:h + 1], oml.t[:, h:h + 1], ALU.mult, ALU.add,
                   [sig_r[h], noml.r, oml.r], [hk.r])
                yield
                tt("pool", qe[:, h, 0:T], q_f[:, h, 0:T], heb.t[:, 0:T], ALU.mult, [q_r[h], heb.r], [qe_r[h]])
                tt("pool", ke[:, h, 0:T], hk.t[:, 0:T], henb.t[:, 0:T], ALU.mult, [hk.r, henb.r], [ke_r[h]])
                yield

                if st["po"] is None:
                    pob = bank(hold=True)
                else:
                    pob = st["po"]
                po, por = pob
                for ci, (c0, Sz) in enumerate(chunks):
                    pb, pbr = bankb()
                    tr(out=pb[0:Sz, 0:128], in_=ke[:, h, c0:c0 + Sz], identity=ident.t[:], r=[ke_r[h], ident.r], w=[pbr])
                    psc, pscr = bank()
                    mm(psc[0:Sz, 0:Sz], lhsT=ke[:, h, c0:c0 + Sz], rhs=qe[:, h, c0:c0 + Sz],
                       start=True, stop=True, r=[ke_r[h], qe_r[h]], w=[pscr])
                    act(kt.t[0:Sz, :], pb[0:Sz, 0:128], AF.Copy, [pbr], [kt.r])
                    mk = maskP if kind == "p" else maskS
                    tt("dve", sm.t[0:Sz, 0:Sz], psc[0:Sz, 0:Sz], mk.t[0:Sz, 0:Sz], ALU.mult, [pscr, mk.r], [sm.r])
                    vb = ci if kind == "p" else 0
                    if kind == "p":
                        act(Ssc.t[:], Sst.t[:, h, :], AF.Copy, [Sst_r[h], hG.r], [Ssc.r], scale=hG.t[:, ci:ci + 1])
                        yield
                        mm(po[:, c0:c0 + Sz], lhsT=v_tok[0:Sz, vb, h * 128:(h + 1) * 128], rhs=sm.t[0:Sz, 0:Sz], start=True, stop=False,
                           r=[v_r[vb], sm.r], w=[por])
                        mm(po[:, c0:c0 + Sz], lhsT=Ssc.t[:], rhs=qe[:, h, c0:c0 + Sz], start=False, stop=True, r=[Ssc.r, qe_r[h]], w=[por])
                        pds, pdsr = bank()
                        mm(pds[:, 0:128], lhsT=kt.t[0:Sz, :], rhs=v_tok[0:Sz, vb, h * 128:(h + 1) * 128], start=True, stop=True,
                           r=[kt.r, v_r[vb]], w=[pdsr])
                        ts("dve", dst.t[:], pds[:, 0:128], heb.t[:, c0 + 127:c0 + 128], None, ALU.mult, None, [pdsr, heb.r], [dst.r])
                        stt("dve", Sst.t[:, h, :], Sst.t[:, h, :], hel.t[:, ci:ci + 1], dst.t[:], ALU.mult, ALU.add,
                            [Sst_r[h], hel.r, dst.r], [Sst_r[h]])
                        yield
                    else:
                        mm(po[:, 0:64], lhsT=v_tok[0:64, 0, h * 128:(h + 1) * 128], rhs=sm.t[0:64, 0:64],
                           start=True, stop=False, r=[v_r[0], sm.r], w=[por])
                        tt("dve", ketokM.t[:], kt.t[0:64, :].unsqueeze(1).to_broadcast([64, NSS, 128]),
                           seqm.t[:].unsqueeze(2).to_broadcast([64, NSS, 128]), ALU.mult, [kt.r, seqm.r], [ketokM.r])
                        tt("dve", qeM.t[:], qe[:, h, 0:TS].unsqueeze(1).to_broadcast([128, NSS, TS]), cmask.t[:], ALU.mult,
                           [qe_r[h], cmask.r], [qeM.r])
                        sX, sXr = (scrA, scrA_q) if h % 2 == 0 else (scrB, scrB_q)
                        s03 = sX.t[:].rearrange("p (s v) -> p s v", s=NSS)
                        xsb3 = xsb.t[:].rearrange("p (s v) -> p s v", s=8)
                        for half in range(2):
                            if half == 0:
                                act(xsb3, s03[:, 0:8, :], AF.Copy, sXr, [xsb.r])
                            else:
                                act(xsb3, s03[:, 8:16, :], AF.Copy, sXr, [xsb.r])
                            for j in range(8):
                                sq = half * 8 + j
                                mm(po[:, 0:64], lhsT=xsb3[:, j, :], rhs=qeM.t[:, sq, :], start=False, stop=(sq == NSS - 1),
                                   r=[xsb.r, qeM.r], w=[por])
                        sn3 = ge[:].rearrange("p a b -> p (a b)").rearrange("p (s v) -> p s v", s=NSS)
                        for g4 in range(4):
                            pds, pdsr = bank()
                            for j in range(4):
                                sq = 4 * g4 + j
                                mm(pds[:, j * 128:(j + 1) * 128], lhsT=ketokM.t[:, sq, :], rhs=v_tok[0:64, 0, h * 128:(h + 1) * 128],
                                   start=True, stop=True, r=[ketokM.r, v_r[0]], w=[pdsr])
                            tt("dve", sn3[:, 4 * g4:4 * g4 + 4, :], pds[:, 0:512].rearrange("p (s v) -> p s v", s=4), s03[:, 4 * g4:4 * g4 + 4, :],
                               ALU.add, [pdsr] + sXr, ge_r)
                            tt("dve", sn3[:, 4 * g4:4 * g4 + 4, :], sn3[:, 4 * g4:4 * g4 + 4, :],
                               hel.t[:, 4 * g4:4 * g4 + 4].unsqueeze(2).to_broadcast([128, 4, 128]), ALU.mult, ge_r + [hel.r], ge_r)
                        S.dma("sp", o_shg[:, h].rearrange("s k v -> k s v"), sn3, reads=ge_r)
                        yield
                act(o_sb.t[:, 0:T], po[:, 0:T], AF.Copy, [por], [o_sb.r])
                act(osq.t[:, 0:T], po[:, 0:T], AF.Square, [por], [osq.r])
                if st["po"] is None:
                    release(pob)
                yield
                pss, pssr = bank()
                mm(pss[:, 0:T], lhsT=ones_bf.t[:], rhs=osq.t[:, 0:T], start=True, stop=True, r=[ones_bf.r, osq.r], w=[pssr])
                act(olv.t[:, 0:T], pss[:, 0:T], AF.Ln, [pssr, c_eps.r], [olv.r], bias=c_eps.t[:], scale=1.0 / 128)
                yield
                act(orstd.t[:, 0:T], olv.t[:, 0:T], AF.Exp, [olv.r], [orstd.r], scale=-0.5)
                yield
                tt("dve", o_sb.t[:, 0:T], o_sb.t[:, 0:T], orstd.t[:, 0:T], ALU.mult, [o_sb.r, orstd.r], [o_sb.r])
                yield
                stt("dve", yhg[:, h, 0:T], o_sb.t[:, 0:T], hgn.t[:, 0:1], sg_bf[:, h, 0:T], ALU.mult, ALU.mult,
                    [o_sb.r, hgn.r, sg_r[h]], [yhg_r[h]])
                yield

            def hg_all_gen():
                if kind == "p":
                    nway = 4 if done["s5"] else 2
                    for h in range(0, 8, nway):
                        alive = [hg_head_gen(h + j, HGSET[j]) for j in range(nway)]
                        while alive:
                            for g in list(alive):
                                try:
                                    next(g)
                                    yield
                                except StopIteration:
                                    alive.remove(g)
                else:
                    def s0_load(hh):
                        sX, sXr = (scrA, scrA_q) if hh % 2 == 0 else (scrB, scrB_q)
                        S.dma("sp", sX.t[:].rearrange("p (s v) -> p s v", s=NSS), st_hg[:, hh].rearrange("s k v -> k s v"), writes=sXr)
                    s0_load(0)
                    for h in range(8):
                        if h + 1 < 8:
                            s0_load(h + 1)
                        yield from hg_head_gen(h, HGSET[0])

            if kind == "p":
                S.op("pool", lambda e: e.memset(HGSET[1]["bufs"][9].t[0:1, 0:1], 0.0), [], ge_r + HGSET[1]["res"])
                if done["s5"]:
                    S.op("pool", lambda e: e.memset(HGSET[2]["bufs"][9].t[0:1, 0:1], 0.0), [],
                         scrA_q + scrB_q + HGSET[2]["res"] + HGSET[3]["res"])
            gl = [(g_s5, 8), (hg_all_gen(), 16), (proj_b_gen(), 8)] if not done["s5"] else [(hg_all_gen(), 16), (proj_b_gen(), 8)]
            interleave(gl)
            if kind == "p":
                S.op("pool", lambda e: e.memset(HGSET[1]["bufs"][9].t[0:1, 0:1], 0.0), [], ge_r + HGSET[1]["res"])
                if done["s5"]:
                    S.op("pool", lambda e: e.memset(HGSET[2]["bufs"][9].t[0:1, 0:1], 0.0), [],
                         scrA_q + scrB_q + HGSET[2]["res"] + HGSET[3]["res"])

            dbg(tagn + "ys5", u_f[:, :, 0:T], u_f_r)
            dbg(tagn + "hc_re", hc_re.t[:], [hc_re.r])
            dbg(tagn + "y2", y2_bf[:, :, 0:T], y2_r)
            dbg(tagn + "qe", qe[:, :, 0:T], qe_r)
            dbg(tagn + "ke", ke[:, :, 0:T], ke_r)
            dbg(tagn + "yhg", yhg[:, :, 0:T], yhg_r)
            dbg(tagn + "Sst", Sst.t[:], Sst_r)
            if nt is not None:
                nt["front_pre"]()
            if not tile.get("glu_done"):
                glu_now()

            def ev_bs(m, pm, pmr):
                tt("dve", ms[:, m, 0:T], pm[:, 0:T], gs5[:, m, 0:T], ALU.mult, [pmr, gs5_r[m]], [ms_r[m]])
            lin_fm("w_bs", 0, 1024, lambda k: y2_bf[:, k, 0:T], y2_r, T, ev_bs)

            def ev_bh(m, pm, pmr):
                gt = gtmp[m % 2]
                tt("dve", gt.t[:, 0:T], pm[:, 0:T], ghg[:, m, 0:T], ALU.mult, [pmr, ghg_r[m]], [gt.r])
                tt("dve", mg_bf[:, m, 0:T], ms[:, m, 0:T], gt.t[:, 0:T], ALU.add, [ms_r[m], gt.r], [mg_r[m]])
            lin_fm("w_bh", 0, 1024, lambda k: yhg[:, k, 0:T], yhg_r, T, ev_bh)

            def tm_mm_gen(wn, nkg, lhs_fn, lhs_res):
                for u in range(1024 // UC):
                    bks = [bank(hold=True) for _ in range(nblk)]
                    for kg in range(nkg):
                        sl, slr = load_unit(wn, kg, u)
                        for b in range(nblk):
                            pm, pmr = bks[b]
                            for k in range(8):
                                kk = kg * 8 + k
                                mm(pm[0:nrows, 0:UC], lhsT=lhs_fn(kk, b), rhs=sl[:, k, :], start=(kk == 0), stop=(kk == nkg * 8 - 1),
                                   r=[slr] + lhs_res, w=[pmr])
                            yield
                    for b in range(nblk):
                        pm, pmr = bks[b]
                        act(mo[0:nrows, b, u * UC:(u + 1) * UC], pm[0:nrows, 0:UC], AF.Copy, [pmr], mo_rl(b))
                        release(bks[b])
                    yield

            def tm_epilogue(gbc, res_fn, out_fn):
                for b in range(nblk):
                    rms_rows(mo[0:nrows, b, :], mo_rl(b), nrows)
                    stt("dve", mo[0:nrows, b, :], mo[0:nrows, b, :], rstd1.t[0:nrows, :], gbc.t[0:nrows, :], ALU.mult, ALU.mult,
                        mo_rl(b) + [rstd1.r, gbc.r], mo_rl(b))
                    res_ap, res_res = res_fn(b)
                    out_ap, out_res = out_fn(b)
                    tt("dve", out_ap, mo[0:nrows, b, :], res_ap, ALU.add, mo_rl(b) + res_res, out_res)

            def lin_tm_norm_res(wn, nkg, lhs_fn, lhs_res, gbc, res_fn, out_fn):
                for _ in tm_mm_gen(wn, nkg, lhs_fn, lhs_res):
                    pass
                tm_epilogue(gbc, res_fn, out_fn)

            xres = {}

            def res_x(b):
                xi = xin[ring["xin"] % NXIN]; ring["xin"] += 1
                S.dma("sp", xi.t[0:nrows, :], xd[tok0 + b * 128: tok0 + b * 128 + nrows, :], writes=[xi.r])
                return xi.t[0:nrows, :], [xi.r]

            lin_tm_norm_res("w_out", 1, lambda kk, b: mg_bf[:, kk, b * 128:b * 128 + nrows], mg_r, gpost_bc, res_x,
                            lambda b: (x1[0:nrows, b, :], x1_rl(b)))

            dbg(tagn + "mg", mg_bf[:, :, 0:T], mg_r)
            dbg(tagn + "x1", x1[:], q_r)
            for b in range(nblk):
                norm_transpose(x1[0:nrows, b, :], x1_rl(b), nrows, g2pre, b * 128, hT2, hT2_r)
            if nt is not None:
                if pre_s5_hook is not None:
                    pre_s5_hook()
                nt["front_rest"]()

            def ev_up(m, pm, pmr):
                gt = gtmp[m % 2]
                act(gt.t[:, 0:T], pm[:, 0:T], AF.Relu, [pmr], [gt.r])
                act(hid[:, m, 0:T], gt.t[:, 0:T], AF.Square, [gt.r], [hid_r[m]])

            obuf = {}

            def out_y(b):
                xi = xin[ring["xin"] % NXIN]; ring["xin"] += 1
                obuf[b] = xi
                return xi.t[0:nrows, :], [xi.r]

            def mlp_gen():
                yield from lin_fm_gen("w_up", 0, 4096, hrhs2, hT2_r, T, ev_up)
                yield from tm_mm_gen("w_down", 4, lambda kk, b: hid[:, kk, b * 128:b * 128 + nrows], hid_r)
            if nt is not None:
                interleave([(mlp_gen(), 2), (nt["s5"](), 3)])
            else:
                for _ in mlp_gen():
                    pass
            tm_epilogue(g2post_bc, lambda b: (x1[0:nrows, b, :], x1_rl(b)), out_y)
            for b in range(nblk):
                xi = obuf[b]
                S.dma("pool", yd[tok0 + b * 128: tok0 + b * 128 + nrows, :], xi.t[0:nrows, :], reads=[xi.r])

        tile["rest"] = rest
        return tile

    def s5_prompt_out(hc, od):
        pm, pmr = bank()
        tr(out=pm[0:16, 0:128], in_=hc.t[:, :, 0], identity=identf.t[:], r=[hc.r, identf.r], w=[pmr])
        cp("dve", scrA.t[0:16, 0:128], pm[0:16, 0:128], [pmr], scrA_q)
        S.dma("sp", od.rearrange("(ct p) -> ct p", p=128), scrA.t[0:16, 0:128], reads=scrA_q)

    def s5_sample_in(hc, sd, sX, sXr):
        S.dma("sp", sX.t[0:NSS, :], sd, writes=sXr)
        pm, pmr = bank()
        for ct in range(16):
            tr(out=pm[:, ct * NSS:(ct + 1) * NSS], in_=sX.t[0:NSS, ct * 128:(ct + 1) * 128], identity=identf.t[0:NSS, 0:NSS],
               r=sXr + [identf.r], w=[pmr])
        cp("dve", hc.t[:].rearrange("p a b -> p (a b)"), pm[:, 0:16 * NSS], [pmr], [hc.r])

    def s5_sample_out(hc, od, sX, sXr):
        for g4 in range(4):
            pm, pmr = bank()
            for j in range(4):
                ct = 4 * g4 + j
                tr(out=pm[0:NSS, j * 128:(j + 1) * 128], in_=hc.t[:, ct, :], identity=identf.t[:], r=[hc.r, identf.r], w=[pmr])
            cp("dve", sX.t[0:NSS, g4 * 512:(g4 + 1) * 512], pm[0:NSS, 0:512], [pmr], sXr)
        S.dma("sp", od, sX.t[0:NSS, :], reads=sXr)

    def swap_to_prompt():
        s5_sample_out(hc_re, o_sre, scrA, scrA_q)
        s5_sample_out(hc_im, o_sim, scrB, scrB_q)
        mset("dve", hc_re.t[:], 0.0, [hc_re.r])
        mset("dve", hc_im.t[:], 0.0, [hc_im.r])

    NT = SEQ // TP
    tiles = [make_tile("s", 0, TS)] + [make_tile("p", t * TP, TP) for t in range(NT)]
    s5_sample_in(hc_re, st_re, scrA, scrA_q)
    s5_sample_in(hc_im, st_im, scrB, scrB_q)
    tiles[0]["front_pre"]()
    tiles[0]["front_rest"]()
    tiles[0]["rest"](tiles[1], swap_to_prompt)
    for t in range(1, NT + 1):
        tiles[t]["rest"](tiles[t + 1] if t < NT else None, None)
    s5_prompt_out(hc_re, o_pre)
    s5_prompt_out(hc_im, o_pim)
    S.dma("sp", o_phg.rearrange("h k v -> k h v"), Sst.t[:], reads=Sst_r)

    S.finalize("sp")
    with contextlib.ExitStack() as es2:
        sems_eng = {e: es2.enter_context(nc.semaphore("s_" + e)) for e in ENGS}
        sems_dma = [es2.enter_context(nc.semaphore("d%d" % i)) for i in range(Sched.NDMA)]
        block = es2.enter_context(nc.Block())

        def mk(ename):
            def f(eng):
                S.emit_engine(ename, eng, sems_eng, sems_dma)
            return f
        block.tensor(mk("pe"))
        block.scalar(mk("act"))
        block.vector(mk("dve"))
        block.gpsimd(mk("pool"))
        block.sync(mk("sp"))
    es.close()
    return nc


_NC_CACHE = {}


def kernel(x_prompt, x_sample, state_s5_re, state_s5_im, state_hg,
           norm_mix_pre, norm_mix_post, norm_mlp_pre, norm_mlp_post, w_in, b_in,
           s5_a_re, s5_a_im, s5_log_dt, s5_b_re, s5_b_im, s5_c_re, s5_c_im, s5_d, s5_w_glu, s5_b_glu,
           hg_lb_logits, hg_norm, w_br_s5, w_br_hg, w_out, w_up, w_down):
    f = lambda a: np.ascontiguousarray(np.asarray(a, dtype=np.float32))
    if "nc" not in _NC_CACHE:
        _NC_CACHE["nc"] = build_nc()
    nc = _NC_CACHE["nc"]
    shared = {
        "g_pre": f(norm_mix_pre).reshape(1024), "g_post": f(norm_mix_post).reshape(1024),
        "g2_pre": f(norm_mlp_pre).reshape(1024), "g2_post": f(norm_mlp_post).reshape(1024),
        "w_in": f(w_in).reshape(1024, 6656), "b_in": f(b_in).reshape(6656),
        "a_re": f(s5_a_re).reshape(2048), "a_im": f(s5_a_im).reshape(2048), "ldt": f(s5_log_dt).reshape(32),
        "b_re": f(s5_b_re).reshape(2048, 16), "b_im": f(s5_b_im).reshape(2048, 16),
        "c_re": f(s5_c_re).reshape(32, 16, 64), "c_im": f(s5_c_im).reshape(32, 16, 64),
        "s5d": f(s5_d).reshape(512), "w_glu": f(s5_w_glu).reshape(512, 512), "b_glu": f(s5_b_glu).reshape(512),
        "lbl": f(hg_lb_logits).reshape(2, 1024), "hgn": f(hg_norm).reshape(128),
        "w_bs": f(w_br_s5).reshape(512, 1024), "w_bh": f(w_br_hg).reshape(1024, 1024),
        "w_out": f(w_out).reshape(1024, 1024), "w_up": f(w_up).reshape(1024, 4096), "w_down": f(w_down).reshape(4096, 1024),
    }
    xpn = f(x_prompt); xsn = f(x_sample)
    sre = f(state_s5_re).reshape(128, 2048); sim = f(state_s5_im).reshape(128, 2048)
    shg = f(state_hg).reshape(128, 8, 128, 128)
    in_maps = []
    for c in range(NCORES):
        d = dict(shared)
        d["xp"] = xpn[c]
        d["xs"] = np.ascontiguousarray(xsn[c * NSS:(c + 1) * NSS].reshape(TS, 1024))
        d["st_re"] = np.ascontiguousarray(sre[c * NSS:(c + 1) * NSS])
        d["st_im"] = np.ascontiguousarray(sim[c * NSS:(c + 1) * NSS])
        d["st_hg"] = np.ascontiguousarray(shg[c * NSS:(c + 1) * NSS])
        in_maps.append(d)
    res = run_bass_kernel_spmd(nc, in_maps, core_ids=list(range(NCORES)))
    R = res.results
    y_prompt = np.stack([R[c]["yp"] for c in range(NCORES)], axis=0).astype(np.float32)
    y_sample = np.concatenate([R[c]["ys"].reshape(NSS, 4, 1024) for c in range(NCORES)], axis=0).astype(np.float32)
    p_re = np.stack([R[c]["o_pre"].reshape(32, 64) for c in range(NCORES)], axis=0)[None].astype(np.float32)
    p_im = np.stack([R[c]["o_pim"].reshape(32, 64) for c in range(NCORES)], axis=0)[None].astype(np.float32)
    p_hg = np.stack([R[c]["o_phg"] for c in range(NCORES)], axis=0)[None].astype(np.float32)
    s_re = np.concatenate([R[c]["o_sre"].reshape(NSS, 32, 64) for c in range(NCORES)], axis=0)[None].astype(np.float32)
    s_im = np.concatenate([R[c]["o_sim"].reshape(NSS, 32, 64) for c in range(NCORES)], axis=0)[None].astype(np.float32)
    s_hg = np.concatenate([R[c]["o_shg"] for c in range(NCORES)], axis=0)[None].astype(np.float32)
    return (y_prompt, y_sample, p_re, p_im, p_hg, s_re, s_im, s_hg)
```
